# Optimizing a Trainium2 kernel written in Bass

```python
import math
import jax, jax.numpy as jnp
from jax import lax
import numpy as np

D_MODEL = 1024
BATCH = 8
SEQ = 4096
DEPTH = 2

SSD_HEADS = 16
SSD_HEAD_DIM = 64
SSD_INNER = SSD_HEADS * SSD_HEAD_DIM
SSD_GROUPS = 4
SSD_STATE = 128
SSD_CONV = 4
SSD_CHUNK = 128
SSD_XBC = SSD_INNER + 2 * SSD_GROUPS * SSD_STATE
MOBA_HEADS = 8
MOBA_HEAD_DIM = 64
MOBA_INNER = MOBA_HEADS * MOBA_HEAD_DIM
MOBA_BLOCK = 256
MOBA_TOPK = 3
MOBA_Q_CHUNK = 32
FOX_HEADS = 16
FOX_HEAD_DIM = 64
FOX_INNER = FOX_HEADS * FOX_HEAD_DIM
FOX_Q_BLOCK = 128
D_FF = 4 * D_MODEL
ROPE_THETA = 10000.0
NORM_EPS = 1e-5
N_EVEN = (DEPTH + 1) // 2
N_ODD = DEPTH // 2
EVEN_IN = SSD_INNER + SSD_XBC + SSD_HEADS + 3 * MOBA_INNER
EVEN_OUT = SSD_INNER + MOBA_INNER
ODD_IN = 3 * FOX_INNER + FOX_HEADS

kernel_name = 'hybrid_ssd_moba_fox_trunk'


def rms_norm(x, g):
    xf = x.astype(jnp.float32)
    y = xf * lax.rsqrt(jnp.mean(xf * xf, axis=-1, keepdims=True) + NORM_EPS)
    return (y * g.astype(jnp.float32)).astype(x.dtype)


def rope(x):
    s, d = x.shape[1], x.shape[3]
    half = d // 2
    inv = jnp.power(ROPE_THETA, -jnp.arange(half, dtype=jnp.float32) / half)
    ang = jnp.arange(s, dtype=jnp.float32)[:, None] * inv[None, :]
    cos = jnp.cos(ang)[None, :, None, :]
    sin = jnp.sin(ang)[None, :, None, :]
    xf = x.astype(jnp.float32)
    x1, x2 = xf[..., :half], xf[..., half:]
    return jnp.concatenate([x1 * cos - x2 * sin, x2 * cos + x1 * sin], axis=-1)


def causal_dwconv(u, w, b):
    c = u.shape[-1]
    y = lax.conv_general_dilated(u, w[:, None, :].astype(u.dtype), window_strides=(1,),
                                 padding=[(SSD_CONV - 1, 0)],
                                 dimension_numbers=('NWC', 'WIO', 'NWC'),
                                 feature_group_count=c)
    return y + b.astype(u.dtype)


def ssd_scan(x, dt, a, b_in, c_in):
    bsz, s = x.shape[0], x.shape[1]
    nc = s // SSD_CHUNK
    L = SSD_CHUNK
    r = SSD_HEADS // SSD_GROUPS
    xc = (x * dt[..., None]).reshape(bsz, nc, L, SSD_GROUPS, r, SSD_HEAD_DIM)
    ad = (dt * a).reshape(bsz, nc, L, SSD_GROUPS, r)
    bc = b_in.reshape(bsz, nc, L, SSD_GROUPS, SSD_STATE)
    cc = c_in.reshape(bsz, nc, L, SSD_GROUPS, SSD_STATE)
    acum = jnp.cumsum(ad, axis=2)
    acum_t = jnp.moveaxis(acum, 2, -1)
    causal = jnp.tril(jnp.ones((L, L), dtype=bool))
    decay = jnp.exp(jnp.where(causal, acum_t[..., :, None] - acum_t[..., None, :], -jnp.inf))
    cb = jnp.einsum('bclgn,bcsgn->bcgls', cc, bc)
    y_diag = jnp.einsum('bcgrls,bcsgrp->bclgrp', cb[:, :, :, None] * decay, xc)
    decay_to_end = jnp.exp(acum[:, :, -1:] - acum)
    states = jnp.einsum('bclgn,bclgr,bclgrp->bcgrpn', bc, decay_to_end, xc)
    chunk_decay = jnp.exp(acum[:, :, -1])

    def step(h, inp):
        st, dec = inp
        return h * dec[..., None, None] + st, h

    h0 = jnp.zeros((bsz, SSD_GROUPS, r, SSD_HEAD_DIM, SSD_STATE), jnp.float32)
    _, prev = lax.scan(step, h0, (jnp.moveaxis(states, 1, 0), jnp.moveaxis(chunk_decay, 1, 0)))
    prev = jnp.moveaxis(prev, 0, 1)
    y_off = jnp.einsum('bclgn,bcgrpn,bclgr->bclgrp', cc, prev, jnp.exp(acum))
    return (y_diag + y_off).reshape(bsz, s, SSD_HEADS, SSD_HEAD_DIM)


def ssd_mixer(z, xbc, dt_raw, conv_w, conv_b, dt_bias, a_log, d_skip, gate_norm):
    bsz, s = z.shape[0], z.shape[1]
    xbc = jax.nn.silu(causal_dwconv(xbc, conv_w, conv_b)).astype(jnp.float32)
    gn = SSD_GROUPS * SSD_STATE
    xs = xbc[..., :SSD_INNER].reshape(bsz, s, SSD_HEADS, SSD_HEAD_DIM)
    bm = xbc[..., SSD_INNER:SSD_INNER + gn].reshape(bsz, s, SSD_GROUPS, SSD_STATE)
    cm = xbc[..., SSD_INNER + gn:].reshape(bsz, s, SSD_GROUPS, SSD_STATE)
    dt = jax.nn.softplus(dt_raw.astype(jnp.float32) + dt_bias.astype(jnp.float32))
    a = -jnp.exp(a_log.astype(jnp.float32))
    y = ssd_scan(xs, dt, a, bm, cm) + xs * d_skip.astype(jnp.float32)[:, None]
    y = y.reshape(bsz, s, SSD_INNER) * jax.nn.silu(z.astype(jnp.float32))
    yg = y.reshape(bsz, s, SSD_GROUPS, SSD_INNER // SSD_GROUPS)
    yg = yg * lax.rsqrt(jnp.mean(yg * yg, axis=-1, keepdims=True) + NORM_EPS)
    return (yg.reshape(bsz, s, SSD_INNER) * gate_norm.astype(jnp.float32)).astype(z.dtype)


def moba_attention(q, k, v):
    bsz, s = q.shape[0], q.shape[1]
    nb = -(-s // MOBA_BLOCK)
    pad = nb * MOBA_BLOCK - s
    topk = min(MOBA_TOPK, nb)
    scale = MOBA_HEAD_DIM ** -0.5

    def to_blocks(t):
        t = jnp.pad(t, ((0, 0), (0, pad), (0, 0), (0, 0))).transpose(0, 2, 1, 3)
        return t.reshape(bsz, MOBA_HEADS, nb, MOBA_BLOCK, MOBA_HEAD_DIM)

    kb, vb = to_blocks(k), to_blocks(v)
    kmean = jnp.mean(kb, axis=3)
    n_chunks = s // MOBA_Q_CHUNK
    qc = q.transpose(0, 2, 1, 3).reshape(bsz, MOBA_HEADS, n_chunks, MOBA_Q_CHUNK, MOBA_HEAD_DIM)
    qc = qc.transpose(2, 0, 1, 3, 4)
    gather = jax.vmap(jax.vmap(lambda blocks, i: blocks[i]))
    blk_ids = jnp.arange(nb)
    own_offsets = jnp.arange(MOBA_BLOCK)

    def chunk_fn(args):
        q_blk, ci = args
        start = ci * MOBA_Q_CHUNK
        pos = start + jnp.arange(MOBA_Q_CHUNK)
        own = start // MOBA_BLOCK
        gate = jnp.einsum('bhqd,bhnd->bhqn', q_blk, kmean)
        gate = jnp.where(blk_ids[None, :] < own, gate, -jnp.inf)
        _, idx = lax.top_k(gate, topk)
        valid = jnp.arange(topk) < own
        k_sel = gather(kb, idx)
        s_sel = jnp.einsum('bhqd,bhqkjd->bhqkj', q_blk, k_sel) * scale
        s_sel = jnp.where(valid[:, None], s_sel, -jnp.inf).reshape(bsz, MOBA_HEADS, MOBA_Q_CHUNK, topk * MOBA_BLOCK)
        k_own = lax.dynamic_slice_in_dim(kb, own, 1, axis=2)[:, :, 0]
        v_own = lax.dynamic_slice_in_dim(vb, own, 1, axis=2)[:, :, 0]
        s_own = jnp.einsum('bhqd,bhjd->bhqj', q_blk, k_own) * scale
        own_pos = own * MOBA_BLOCK + own_offsets
        s_own = jnp.where(own_pos[None, :] <= pos[:, None], s_own, -jnp.inf)
        p = jax.nn.softmax(jnp.concatenate([s_sel, s_own], axis=-1), axis=-1)
        p_sel = p[..., :topk * MOBA_BLOCK].reshape(bsz, MOBA_HEADS, MOBA_Q_CHUNK, topk, MOBA_BLOCK)
        p_own = p[..., topk * MOBA_BLOCK:]
        v_sel = gather(vb, idx)
        return (jnp.einsum('bhqkj,bhqkjd->bhqd', p_sel, v_sel)
                + jnp.einsum('bhqj,bhjd->bhqd', p_own, v_own))

    out = lax.map(chunk_fn, (qc, jnp.arange(n_chunks)))
    return out.transpose(1, 0, 3, 2, 4).reshape(bsz, s, MOBA_HEADS, MOBA_HEAD_DIM)


def forgetting_attention(q, k, v, log_f):
    bsz, s = q.shape[0], q.shape[1]
    nqb = s // FOX_Q_BLOCK
    scale = FOX_HEAD_DIM ** -0.5
    ch = jnp.cumsum(log_f, axis=1).transpose(0, 2, 1)
    kh = k.transpose(0, 2, 1, 3)
    vh = v.transpose(0, 2, 1, 3)
    qb = q.reshape(bsz, nqb, FOX_Q_BLOCK, FOX_HEADS, FOX_HEAD_DIM).transpose(1, 0, 3, 2, 4)
    cq = ch.reshape(bsz, FOX_HEADS, nqb, FOX_Q_BLOCK).transpose(2, 0, 1, 3)
    key_pos = jnp.arange(s)

    def blk(args):
        q_blk, c_blk, bi = args
        qpos = bi * FOX_Q_BLOCK + jnp.arange(FOX_Q_BLOCK)
        logits = (jnp.einsum('bhqd,bhkd->bhqk', q_blk, kh) * scale
                  + (c_blk[..., :, None] - ch[..., None, :]))
        logits = jnp.where(key_pos[None, :] <= qpos[:, None], logits, -jnp.inf)
        return jnp.einsum('bhqk,bhkd->bhqd', jax.nn.softmax(logits, axis=-1), vh)

    out = lax.map(blk, (qb, cq, jnp.arange(nqb)))
    return out.transpose(1, 0, 3, 2, 4).reshape(bsz, s, FOX_INNER)


def ssd_moba_layer(h, w_in, conv_w, conv_b, dt_bias, a_log, d_skip, gate_norm, w_out):
    bsz, s = h.shape[0], h.shape[1]
    proj = h @ w_in
    o1 = SSD_INNER
    o2 = o1 + SSD_XBC
    o3 = o2 + SSD_HEADS
    o4 = o3 + MOBA_INNER
    o5 = o4 + MOBA_INNER
    z, xbc, dt_raw = proj[..., :o1], proj[..., o1:o2], proj[..., o2:o3]
    q = proj[..., o3:o4].reshape(bsz, s, MOBA_HEADS, MOBA_HEAD_DIM)
    k = proj[..., o4:o5].reshape(bsz, s, MOBA_HEADS, MOBA_HEAD_DIM)
    v = proj[..., o5:].reshape(bsz, s, MOBA_HEADS, MOBA_HEAD_DIM).astype(jnp.float32)
    y_ssd = ssd_mixer(z, xbc, dt_raw, conv_w, conv_b, dt_bias, a_log, d_skip, gate_norm)
    y_att = moba_attention(rope(q), rope(k), v).reshape(bsz, s, MOBA_INNER).astype(h.dtype)
    return jnp.concatenate([y_ssd, y_att], axis=-1) @ w_out


def fox_layer(h, w_in, fgate_bias, w_out):
    bsz, s = h.shape[0], h.shape[1]
    proj = (h @ w_in).astype(jnp.float32)
    q = proj[..., :FOX_INNER].reshape(bsz, s, FOX_HEADS, FOX_HEAD_DIM)
    k = proj[..., FOX_INNER:2 * FOX_INNER].reshape(bsz, s, FOX_HEADS, FOX_HEAD_DIM)
    v = proj[..., 2 * FOX_INNER:3 * FOX_INNER].reshape(bsz, s, FOX_HEADS, FOX_HEAD_DIM)
    log_f = jax.nn.log_sigmoid(proj[..., 3 * FOX_INNER:] + fgate_bias.astype(jnp.float32))
    return forgetting_attention(q, k, v, log_f).astype(h.dtype) @ w_out


def sq_relu_mlp(h, w_up, w_down):
    return jnp.square(jax.nn.relu(h @ w_up)) @ w_down


def setup_inputs(seed: int = 0) -> dict:
    key = jax.random.key(seed)
    ks = jax.random.split(key, 20)
    f32 = jnp.float32

    def nrm(k, shape, fan_in):
        return jax.random.normal(k, shape, f32) * fan_in ** -0.5

    def gain(k, shape):
        return 1.0 + 0.02 * jax.random.normal(k, shape, f32)

    x = jax.random.normal(ks[0], (BATCH, SEQ, D_MODEL), f32)
    dt = jnp.exp(jax.random.uniform(ks[5], (N_EVEN, SSD_HEADS), f32, math.log(1e-3), math.log(1e-1)))
    return {
        'x': x,
        'norm_mix_even': gain(ks[1], (N_EVEN, D_MODEL)),
        'w_in_even': nrm(ks[2], (N_EVEN, D_MODEL, EVEN_IN), D_MODEL),
        'conv_w': nrm(ks[3], (N_EVEN, SSD_CONV, SSD_XBC), SSD_CONV),
        'conv_b': 0.01 * jax.random.normal(ks[4], (N_EVEN, SSD_XBC), f32),
        'dt_bias': dt + jnp.log(-jnp.expm1(-dt)),
        'a_log': jnp.log(jax.random.uniform(ks[6], (N_EVEN, SSD_HEADS), f32, 1.0, 16.0)),
        'd_skip': 1.0 + 0.1 * jax.random.normal(ks[7], (N_EVEN, SSD_HEADS), f32),
        'ssd_gate_norm': gain(ks[8], (N_EVEN, SSD_INNER)),
        'w_out_even': nrm(ks[9], (N_EVEN, EVEN_OUT, D_MODEL), EVEN_OUT),
        'norm_mix_odd': gain(ks[10], (N_ODD, D_MODEL)),
        'w_in_odd': nrm(ks[11], (N_ODD, D_MODEL, ODD_IN), D_MODEL),
        'fgate_bias': jax.random.uniform(ks[12], (N_ODD, FOX_HEADS), f32, 1.0, 5.0),
        'w_out_odd': nrm(ks[13], (N_ODD, FOX_INNER, D_MODEL), FOX_INNER),
        'norm_mlp': gain(ks[14], (DEPTH, D_MODEL)),
        'w_up': nrm(ks[15], (DEPTH, D_MODEL, D_FF), D_MODEL),
        'w_down': nrm(ks[16], (DEPTH, D_FF, D_MODEL), D_FF),
        'final_norm': gain(ks[17], (D_MODEL,)),
    }


def reference(x, norm_mix_even, w_in_even, conv_w, conv_b, dt_bias, a_log, d_skip, ssd_gate_norm,
              w_out_even, norm_mix_odd, w_in_odd, fgate_bias, w_out_odd, norm_mlp, w_up, w_down,
              final_norm):
    for layer in range(DEPTH):
        i = layer // 2
        if layer % 2 == 0:
            mix = ssd_moba_layer(rms_norm(x, norm_mix_even[i]), w_in_even[i], conv_w[i], conv_b[i],
                                 dt_bias[i], a_log[i], d_skip[i], ssd_gate_norm[i], w_out_even[i])
        else:
            mix = fox_layer(rms_norm(x, norm_mix_odd[i]), w_in_odd[i], fgate_bias[i], w_out_odd[i])
        x = x + mix.astype(x.dtype)
        x = x + sq_relu_mlp(rms_norm(x, norm_mlp[layer]), w_up[layer], w_down[layer]).astype(x.dtype)
    return rms_norm(x, final_norm)
```

```python
import math
import os
import numpy as np
import ml_dtypes
import concourse.bass as bass
import concourse.mybir as mybir
from concourse.bass_utils import run_bass_kernel_spmd

F32 = mybir.dt.float32
BF16 = mybir.dt.bfloat16
ALU = mybir.AluOpType
AF = mybir.ActivationFunctionType
AX = mybir.AxisListType

D = 1024
S_FULL = 4096
DFF = 4096
EPS = 1e-5
NEG = -30000.0


class Tk:
    __slots__ = ("w", "r", "name", "excl")

    def __init__(self, name="", excl=False):
        self.w = {}
        self.r = {}
        self.name = name
        self.excl = excl


class Sched:
    CH = 24000

    def __init__(self, nc):
        self.nc = nc
        self.h = {"pe": nc.tensor, "act": nc.scalar, "dve": nc.vector, "pool": nc.gpsimd, "sp": nc.sync}
        self.sems = {k: [] for k in self.h}
        self.cnt = {k: 0 for k in self.h}
        self.seen = {k: {} for k in self.h}
        self.pool = [nc.alloc_semaphore(f"s{i}") for i in range(96)]
        self.dma_free = []
        self.dma_sems = {}
        self.nwait = 0
        self.nins = 0

    def _next_tok(self, en):
        i = self.cnt[en]
        if i % self.CH == 0:
            self.sems[en].append(self.pool.pop())
        self.cnt[en] = i + 1
        return (self.sems[en][-1], i % self.CH + 1, en)

    def _dma_ent(self, key):
        if key not in self.dma_sems:
            if self.dma_free:
                self.dma_sems[key] = self.dma_free.pop()
            else:
                self.dma_sems[key] = [self.pool.pop(), 0]
        return self.dma_sems[key]

    def end_phase(self):
        for ent in self.dma_sems.values():
            if ent[1] < 20000:
                self.dma_free.append(ent)
        self.dma_sems = {}

    def _wait(self, en, tok):
        sem, val, src = tok
        if src == en and en == "pe":
            return
        sid = id(sem)
        if self.seen[en].get(sid, 0) >= val:
            return
        self.h[en].wait_ge(sem, val)
        self.seen[en][sid] = val
        self.nwait += 1

    def _deps(self, en, reads, writes, join):
        for t in reads:
            for tok in t.w.values():
                self._wait(en, tok)
            if t.excl:
                for tok in t.r.values():
                    if tok[2] != en:
                        self._wait(en, tok)
        for t in writes:
            if not join:
                for tok in t.w.values():
                    self._wait(en, tok)
            for tok in t.r.values():
                self._wait(en, tok)

    def _record(self, tok, reads, writes, join):
        sid = id(tok[0])
        for t in reads:
            t.r[sid] = tok
        for t in writes:
            if join:
                t.w[sid] = tok
            else:
                t.w = {sid: tok}
                t.r = {}

    def op(self, en, fn, reads=(), writes=(), join=False):
        self._deps(en, reads, writes, join)
        ins = fn(self.h[en])
        tok = self._next_tok(en)
        ins.then_inc(tok[0], 1)
        self._record(tok, reads, writes, join)
        self.nins += 1
        return tok

    def dma(self, en, key, out, in_, reads=(), writes=(), join=False, **kw):
        self._deps(en, reads, writes, join)
        ent = self._dma_ent(key)
        ins = self.h[en].dma_start(out=out, in_=in_, **kw)
        ent[1] += 16
        ins.then_inc(ent[0], 16)
        tok = (ent[0], ent[1], "dma")
        self._record(tok, reads, writes, join)
        self.nins += 1
        return tok

    def wait_all(self, en, tks):
        for t in tks:
            for tok in list(t.w.values()) + list(t.r.values()):
                self._wait(en, tok)


class KB:
    def __init__(self, nc, S):
        self.nc = nc
        self.S = S
        self.NT = S // 128
        self.sc = Sched(nc)
        self.uid = 0
        self.ps = [nc.alloc_psum_tensor(f"ps{i}", [128, 512], F32) for i in range(8)]
        self.pst = [Tk(f"ps{i}", excl=True) for i in range(8)]
        self.ps_rr = 0
        self.SB_BYTES = 207 * 1024
        self.big = nc.alloc_sbuf_tensor("big", [128, self.SB_BYTES // 4], F32)
        self.sb_off = 0
        self.sb_base = 0

    def sb(self, shape, dt, name=None):
        esz = 4 if dt == F32 else 2
        n = 1
        for d_ in shape[1:]:
            n *= d_
        nbytes = (n * esz + 31) // 32 * 32
        off = self.sb_off
        self.sb_off += nbytes
        assert self.sb_off <= self.SB_BYTES, f"SBUF overflow {self.sb_off}"
        ap = self.big[0:shape[0], off // 4:(off + nbytes) // 4]
        if dt != F32:
            ap = ap.bitcast(dt)
        ap = ap[:, 0:n]
        if len(shape) == 3:
            ap = ap.rearrange("p (a b) -> p a b", a=shape[1])
        elif len(shape) == 4:
            ap = ap.rearrange("p (a b c) -> p a b c", a=shape[1], b=shape[2])
        return ap

    def sb_reset(self):
        self.sb_off = self.sb_base

    def bank(self):
        i = self.ps_rr
        self.ps_rr = (i + 1) % 8
        return i

    def setup_consts(self, ident_d, consts=None):
        nc, sc = self.nc, self.sc
        self.ident = self.sb([128, 128], BF16, "ident")
        self.ident_t = Tk("ident")
        sc.dma("sp", "const_ident", self.ident[:], ident_d[:, :], writes=[self.ident_t])
        c = {"tk": Tk()}
        for nm, (ap_d, shape, dt) in (consts or {}).items():
            c[nm] = self.sb(shape, dt)
            sc.dma("sp", "const", c[nm][:], ap_d, writes=[c["tk"]], join=True)
        self.c = c
        self.sb_base = self.sb_off

    def load_weight(self, w_sb, w_tk, w_dram, K, N, stage, stage_tk, col_chunk=2048, dram_col0=0, sb_col0=0):
        sc = self.sc
        kt = K // 128
        i = 0
        for k in range(kt):
            for c0 in range(0, N, col_chunk):
                cw = min(col_chunk, N - c0)
                st, stk = stage[i % len(stage)], stage_tk[i % len(stage)]
                sc.dma("sp", stk, st[:, 0:cw],
                       w_dram[k * 128:(k + 1) * 128, dram_col0 + c0:dram_col0 + c0 + cw], writes=[stk])
                en = "pool" if i % 2 == 0 else "act"
                if en == "pool":
                    sc.op("pool", lambda e, st=st, k=k, c0=c0, cw=cw: e.tensor_copy(
                        out=w_sb[:, k, sb_col0 + c0:sb_col0 + c0 + cw], in_=st[:, 0:cw]),
                        reads=[stk], writes=[w_tk], join=True)
                else:
                    sc.op("act", lambda e, st=st, k=k, c0=c0, cw=cw: e.activation(
                        out=w_sb[:, k, sb_col0 + c0:sb_col0 + c0 + cw], in_=st[:, 0:cw], func=AF.Copy),
                        reads=[stk], writes=[w_tk], join=True)
                i += 1

    def rstd(self, st, st_tk):
        sc = self.sc
        sc.op("dve", lambda e: e.tensor_scalar_add(out=st[:, 1:2], in0=st[:, 0:1], scalar1=EPS), reads=[st_tk], writes=[st_tk])
        sc.op("act", lambda e: e.activation(out=st[:, 3:4], in_=st[:, 1:2], func=AF.Sqrt), reads=[st_tk], writes=[st_tk])
        sc.op("dve", lambda e: e.reciprocal(out=st[:, 2:3], in_=st[:, 3:4]), reads=[st_tk], writes=[st_tk])

    def norm_tile(self, x_dram_rows, g_sb, g_tk, bufs, hT, hT_tk, col0, idx, preloaded=False):
        sc = self.sc
        xin, xin_tk = bufs["xin"][idx % len(bufs["xin"])]
        hb, hb_tk = bufs["hb"][idx % len(bufs["hb"])]
        st, st_tk = bufs["st"][idx % len(bufs["st"])]
        junk, junk_tk = bufs["junk"]
        if not preloaded:
            sc.dma("sp", xin_tk, xin[:], x_dram_rows, writes=[xin_tk])
        sc.op("act", lambda e: e.activation(out=junk[:], in_=xin[:], func=AF.Square, scale=float(1.0 / math.sqrt(D)), accum_out=st[:, 0:1]),
              reads=[xin_tk], writes=[junk_tk, st_tk])
        self.rstd(st, st_tk)
        sc.op("dve", lambda e: e.scalar_tensor_tensor(out=hb[:], in0=xin[:], scalar=st[:, 2:3], in1=g_sb[:],
                                                      op0=ALU.mult, op1=ALU.mult),
              reads=[xin_tk, st_tk, g_tk], writes=[hb_tk])
        b = self.bank()
        pt = self.ps[b][:].bitcast(BF16)
        for k in range(8):
            sc.op("pe", lambda e, k=k: e.transpose(out=pt[:, k * 128:(k + 1) * 128], in_=hb[:, k * 128:(k + 1) * 128],
                                                   identity=self.ident[:]),
                  reads=[hb_tk, self.ident_t], writes=[self.pst[b]], join=(k > 0))
        sc.op("act", lambda e: e.activation(out=hT[:, 0:8, col0:col0 + 128],
                                            in_=pt.rearrange("p (k t) -> p k t", k=8), func=AF.Copy),
              reads=[self.pst[b]], writes=[hT_tk], join=True)

    def phase_final_norm(self, x_dram, g_dram, out_dram):
        nc, sc = self.nc, self.sc
        if True:
            self.sb_reset()
            g_sb = self.sb([128, D], F32, "g")
            g_tk = Tk()
            sc.dma("sp", "gload", g_sb[:], g_dram.partition_broadcast(128), writes=[g_tk])
            NB = 3
            xin = [(self.sb([128, D], F32, "xin"), Tk()) for _ in range(NB)]
            xo = [(self.sb([128, D], F32, "xo"), Tk()) for _ in range(NB)]
            st = [(self.sb([128, 4], F32, "st"), Tk()) for _ in range(NB)]
            junk, junk_tk = self.sb([128, D], F32, "junk"), Tk()
            for t in range(self.NT):
                xi, xi_tk = xin[t % NB]
                xot, xo_tk = xo[t % NB]
                s_, s_tk = st[t % NB]
                if t == 0:
                    for tt in range(min(2, self.NT)):
                        sc.dma("sp", xin[tt % NB][1], xin[tt % NB][0][:], x_dram[tt * 128:(tt + 1) * 128, :], writes=[xin[tt % NB][1]])
                if t + 2 < self.NT:
                    sc.dma("sp", xin[(t + 2) % NB][1], xin[(t + 2) % NB][0][:], x_dram[(t + 2) * 128:(t + 3) * 128, :], writes=[xin[(t + 2) % NB][1]])
                sc.op("act", lambda e: e.activation(out=junk[:], in_=xi[:], func=AF.Square, scale=float(1.0 / math.sqrt(D)), accum_out=s_[:, 0:1]),
                      reads=[xi_tk], writes=[junk_tk, s_tk])
                self.rstd(s_, s_tk)
                sc.op("dve", lambda e: e.scalar_tensor_tensor(out=xot[:], in0=xi[:], scalar=s_[:, 2:3], in1=g_sb[:],
                                                              op0=ALU.mult, op1=ALU.mult),
                      reads=[xi_tk, s_tk, g_tk], writes=[xo_tk])
                sc.dma("sp", xo_tk, out_dram[t * 128:(t + 1) * 128, :], xot[:], reads=[xo_tk])
            self.phase_barrier([tk for _, tk in xo])

    def phase_mlp(self, x_dram, g_dram, wup_dram, wdn_dram, xout_dram=None, wbf=None):
        nc, sc = self.nc, self.sc
        xout_dram = x_dram if xout_dram is None else xout_dram
        if True:
            self.sb_reset()
            g_sb, g_tk = self.sb([128, D], F32, "g"), Tk()
            sc.dma("sp", "gload", g_sb[:], g_dram.partition_broadcast(128), writes=[g_tk])
            wu, wu_tk = self.sb([128, 8, DFF], BF16, "wu"), Tk()
            wd, wd_tk = self.sb([128, 32, D], BF16, "wd"), Tk()
            hmid, hmid_tk = self.sb([128, 32, 512], BF16, "hmid"), [Tk() for _ in range(32)]
            hm32 = hmid.rearrange("p a b -> p (a b)").bitcast(F32)
            stage = [hm32[:, i * 2048:(i + 1) * 2048] for i in range(4)]
            stage_tk = [Tk() for _ in range(4)]
            if wbf is None:
                self.load_weight(wu, wu_tk, wup_dram, D, DFF, stage, stage_tk)
                self.load_weight(wd, wd_tk, wdn_dram, DFF, D, stage, stage_tk, col_chunk=1024)
                for tk in hmid_tk:
                    for s in stage_tk:
                        tk.w.update(s.w)
                        tk.r.update(s.r)
            else:
                for k in range(8):
                    sc.dma("sp", wu_tk, wu[:, k, :], wbf[0][k * 128:(k + 1) * 128, :], writes=[wu_tk], join=(k > 0))
                for k in range(32):
                    sc.dma("sp", wd_tk, wd[:, k, :], wbf[1][k * 128:(k + 1) * 128, :], writes=[wd_tk], join=(k > 0))
            hT = [(self.sb([128, 8, 512], BF16, "hT"), Tk()) for _ in range(1)]
            import os
            STOP = int(os.environ.get("MLP_STOP", "9"))
            bufs = {
                "xin": [(self.sb([128, D], F32, "xin"), Tk()) for _ in range(2)],
                "hb": [(self.sb([128, D], BF16, "hb"), Tk()) for _ in range(2)],
                "st": [(self.sb([128, 4], F32, "st"), Tk()) for _ in range(2)],
                "junk": (self.sb([128, D], F32, "junk"), Tk()),
            }
            xres = [(self.sb([128, 512], F32, "xres"), Tk()) for _ in range(4)]
            NG = self.S // 512 if STOP >= 1 else 0
            for g in range(NG):
                hTg, hTg_tk = hT[0]
                for t in range(4):
                    r0 = g * 512 + t * 128
                    self.norm_tile(x_dram[r0:r0 + 128, :], g_sb, g_tk, bufs, hTg, hTg_tk, t * 128, g * 4 + t,
                                   preloaded=(g > 0 and t < 2))
                if STOP < 2:
                    continue
                for f in range(32):
                    b = self.bank()
                    for k in range(8):
                        sc.op("pe", lambda e, k=k, f=f, b=b: e.matmul(self.ps[b][:], lhsT=wu[:, k, f * 128:(f + 1) * 128],
                                                                       rhs=hTg[:, k, :], start=(k == 0), stop=(k == 7)),
                              reads=[wu_tk, hTg_tk], writes=[self.pst[b]], join=(k > 0))
                    sc.op("act", lambda e, f=f, b=b: e.activation(out=hmid[:, f, :], in_=self.ps[b][:], func=AF.Relu),
                          reads=[self.pst[b]], writes=[hmid_tk[f]])
                    sc.op("dve" if f % 2 == 0 else "pool", lambda e, f=f: e.tensor_tensor(
                        out=hmid[:, f, :], in0=hmid[:, f, :], in1=hmid[:, f, :], op=ALU.mult),
                        reads=[hmid_tk[f]], writes=[hmid_tk[f]])
                if STOP < 3:
                    continue
                if g + 1 < NG:
                    for t in range(2):
                        r1 = (g + 1) * 512 + t * 128
                        xi_, xi_tk_ = bufs["xin"][((g + 1) * 4 + t) % len(bufs["xin"])]
                        sc.dma("sp", xi_tk_, xi_[:], x_dram[r1:r1 + 128, :], writes=[xi_tk_])

                def xr_load(p):
                    t_, c_ = p // 2, p % 2
                    xr_, xr_tk_ = xres[p % 4]
                    sc.dma("sp", xr_tk_, xr_[:], x_dram[g * 512 + t_ * 128:g * 512 + (t_ + 1) * 128, c_ * 512:(c_ + 1) * 512], writes=[xr_tk_])

                xr_load(0)
                for t in range(4):
                    r0 = g * 512 + t * 128
                    for c in range(2):
                        xr, xr_tk = xres[(t * 2 + c) % 4]
                        if t * 2 + c + 1 < 8:
                            xr_load(t * 2 + c + 1)
                        b = self.bank()
                        for f in range(32):
                            sc.op("pe", lambda e, f=f, t=t, c=c, b=b: e.matmul(
                                self.ps[b][:], lhsT=hmid[:, f, t * 128:(t + 1) * 128], rhs=wd[:, f, c * 512:(c + 1) * 512],
                                start=(f == 0), stop=(f == 31)),
                                reads=[wd_tk, hmid_tk[f]], writes=[self.pst[b]], join=(f > 0))
                        sc.op("dve", lambda e, b=b, xr=xr: e.tensor_tensor(out=xr[:], in0=self.ps[b][:], in1=xr[:], op=ALU.add),
                              reads=[self.pst[b], xr_tk], writes=[xr_tk])
                        sc.dma("sp", xr_tk, xout_dram[r0:r0 + 128, c * 512:(c + 1) * 512], xr[:], reads=[xr_tk])
            self.phase_barrier([tk for _, tk in xres])

    def norm_all(self, x_dram, g_dram):
        sc = self.sc
        g_sb, g_tk = self.sb([128, D], F32, "g"), Tk()
        sc.dma("sp", g_tk, g_sb[:], g_dram.partition_broadcast(128), writes=[g_tk])
        hT, hT_tk = self.sb([128, 8, self.S], BF16, "hTall"), Tk()
        save = self.sb_off
        bufs = {
            "xin": [(self.sb([128, D], F32), Tk()) for _ in range(2)],
            "hb": [(self.sb([128, D], BF16), Tk()) for _ in range(2)],
            "st": [(self.sb([128, 4], F32), Tk()) for _ in range(2)],
            "junk": (self.sb([128, D], F32), Tk()),
        }
        for t in range(self.NT):
            self.norm_tile(x_dram[t * 128:(t + 1) * 128, :], g_sb, g_tk, bufs, hT, hT_tk, t * 128, t)
        self.norm_bufs_tks = [tk for _, tk in bufs["xin"]] + [tk for _, tk in bufs["hb"]] + [bufs["junk"][1]]
        return hT, hT_tk, save

    def load_wcols(self, w_dram, c0, ncols, wt, wt_tk, stg, stg_tk, perm_heads=False):
        sc = self.sc
        if not perm_heads:
            sc.dma("sp", stg_tk, stg[:, :, 0:ncols], w_dram[:, c0:c0 + ncols].rearrange("(k p) c -> p k c", p=128),
                   writes=[stg_tk])
        else:
            first = True
            for hh in range(ncols // 64):
                for half in range(2):
                    src = w_dram[:, c0 + hh * 64 + (1 - half) * 32:c0 + hh * 64 + (1 - half) * 32 + 32]
                    sc.dma("sp", stg_tk, stg[:, :, hh * 64 + half * 32:hh * 64 + half * 32 + 32],
                           src.rearrange("(k p) c -> p k c", p=128), writes=[stg_tk], join=not first)
                    first = False
        sc.op("pool", lambda e: e.tensor_copy(out=wt[:, :, 0:ncols], in_=stg[:, :, 0:ncols]), reads=[stg_tk], writes=[wt_tk])

    def phase_fox_proj(self, x_dram, g_dram, w_in, fb_dram, qkT_d, vp_d, RL_d, c):
        sc, S, NT = self.sc, self.S, self.NT
        self.sb_reset()
        nl, nl_tk = self.sb([128, NT, 16], F32), Tk()
        hT, hT_tk, _ = self.norm_all(x_dram, g_dram)
        NG = S // 512
        wts = [(self.sb([128, 8, 128], BF16), Tk()) for _ in range(2)]
        stgs = [(self.sb([128, 8, 128], F32), Tk()) for _ in range(2)]
        rows = [(self.sb([128, S], BF16), Tk()) for _ in range(2)]
        out_tks = []
        self.load_wcols(w_in, 0, 128, wts[0][0], wts[0][1], stgs[0][0], stgs[0][1])
        for f in range(16):
            wt, wt_tk = wts[f % 2]
            row, row_tk = rows[f % 2]
            if f + 1 < 16:
                self.load_wcols(w_in, (f + 1) * 128, 128, wts[(f + 1) % 2][0], wts[(f + 1) % 2][1], stgs[(f + 1) % 2][0], stgs[(f + 1) % 2][1])
            for tg in range(NG):
                b = self.bank()
                for k in range(8):
                    sc.op("pe", lambda e: e.matmul(self.ps[b][:], lhsT=wt[:, k, :], rhs=hT[:, k, tg * 512:(tg + 1) * 512],
                                                   start=(k == 0), stop=(k == 7)),
                          reads=[wt_tk, hT_tk], writes=[self.pst[b]], join=(k > 0))
                sc.op("act", lambda e: e.activation(out=row[:, tg * 512:(tg + 1) * 512], in_=self.ps[b][:], func=AF.Copy,
                                                    scale=(0.125 if f < 8 else 1.0)),
                      reads=[self.pst[b]], writes=[row_tk], join=(tg > 0))
            sc.dma("sp", row_tk, qkT_d[f * 128:(f + 1) * 128, :], row[:], reads=[row_tk])
            out_tks.append(row_tk)
        wv, wv_tk = self.sb([128, 8, 1024], BF16), Tk()
        stv = [(self.sb([128, 8, 512], F32), Tk()) for _ in range(1)]
        for cc in range(2):
            sc.dma("sp", stv[0][1], stv[0][0][:], w_in[:, 2048 + cc * 512:2048 + (cc + 1) * 512].rearrange("(k p) c -> p k c", p=128),
                   writes=[stv[0][1]])
            sc.op("pool", lambda e: e.tensor_copy(out=wv[:, :, cc * 512:(cc + 1) * 512], in_=stv[0][0][:]),
                  reads=[stv[0][1]], writes=[wv_tk], join=(cc > 0))
        vts = [(self.sb([128, 16, 65], BF16), Tk()) for _ in range(2)]
        for vt, vt_tk in vts:
            sc.op("pool", lambda e: e.memset(vt[:], 1.0), writes=[vt_tk])
        for t in range(NT):
            vt, vt_tk = vts[t % 2]
            for cc in range(2):
                b = self.bank()
                for k in range(8):
                    sc.op("pe", lambda e: e.matmul(self.ps[b][:], lhsT=hT[:, k, t * 128:(t + 1) * 128], rhs=wv[:, k, cc * 512:(cc + 1) * 512],
                                                   start=(k == 0), stop=(k == 7)),
                          reads=[wv_tk, hT_tk], writes=[self.pst[b]], join=(k > 0))
                sc.op("act", lambda e: e.activation(out=vt[:, cc * 8:(cc + 1) * 8, 0:64],
                                                    in_=self.ps[b][:].rearrange("p (h d) -> p h d", h=8), func=AF.Copy),
                      reads=[self.pst[b]], writes=[vt_tk], join=(cc > 0))
            sc.dma("sp", vt_tk, vp_d[t * 128:(t + 1) * 128, :], vt.rearrange("p h d -> p (h d)"), reads=[vt_tk])
            out_tks.append(vt_tk)
        wf, wf_tk = self.sb([128, 8, 16], BF16), Tk()
        stf, stf_tk = self.sb([128, 8, 16], F32), Tk()
        self.load_wcols(w_in, 3072, 16, wf, wf_tk, stf, stf_tk)
        fb, fb_tk = self.sb([128, 16], F32), Tk()
        sc.dma("sp", fb_tk, fb[:], fb_dram.partition_broadcast(128), writes=[fb_tk])
        tmp, tmp_tk = self.sb([128, NT, 16], F32), Tk()
        for t in range(NT):
            b = self.bank()
            for k in range(8):
                sc.op("pe", lambda e: e.matmul(self.ps[b][:, 0:16], lhsT=hT[:, k, t * 128:(t + 1) * 128], rhs=wf[:, k, :],
                                               start=(k == 0), stop=(k == 7)),
                      reads=[wf_tk, hT_tk], writes=[self.pst[b]], join=(k > 0))
            sc.op("dve", lambda e: e.tensor_tensor(out=tmp[:, t, :], in0=self.ps[b][:, 0:16], in1=fb[:], op=ALU.add),
                  reads=[self.pst[b], fb_tk], writes=[tmp_tk], join=True)
        sc.op("act", lambda e: e.activation(out=tmp[:], in_=tmp[:], func=AF.Exp, scale=-1.0), reads=[tmp_tk], writes=[tmp_tk])
        sc.op("dve", lambda e: e.tensor_scalar_add(out=tmp[:], in0=tmp[:], scalar1=1.0), reads=[tmp_tk], writes=[tmp_tk])
        sc.op("act", lambda e: e.activation(out=nl[:], in_=tmp[:], func=AF.Ln), reads=[tmp_tk], writes=[nl_tk])
        self.phase_barrier(out_tks + [nl_tk])
        self.sb_reset()
        nl, nl_tk = self.sb([128, NT, 16], F32), Tk()
        out_tks = []
        self.cumsum_rows(nl, nl_tk, RL_d, c, out_tks)
        self.phase_barrier(out_tks)

    def cumsum_rows(self, nl, nl_tk, RL_d, c, out_tks, aw_d=None):
        sc, S, NT = self.sc, self.S, self.NT
        runs, runs_tk = self.sb([128, NT, 16], F32), Tk()
        sc.op("dve", lambda e: e.memset(runs[:, 0, :], 0.0), writes=[runs_tk])
        for t in range(1, NT):
            sc.op("dve", lambda e: e.tensor_tensor(out=runs[:, t, :], in0=runs[:, t - 1, :], in1=nl[:, t - 1, :], op=ALU.add),
                  reads=[nl_tk, runs_tk], writes=[runs_tk])
        cT, cT_tk = self.sb([16, S], F32), Tk()
        cn, cn_tk = self.sb([128, 32], F32), [Tk() for _ in range(2)]
        cn2 = self.sb([128, 32], F32)
        cns = [cn, cn2]
        for t in range(NT):
            b = self.bank()
            if aw_d is not None:
                sc.op("pe", lambda e: e.matmul(self.ps[b][:, 16:32], lhsT=c["triu"][:], rhs=nl[:, t, :], start=True, stop=True),
                      reads=[nl_tk, c["tk"]], writes=[self.pst[b]])
            sc.op("pe", lambda e: e.matmul(self.ps[b][:, 0:16], lhsT=c["triu"][:], rhs=nl[:, t, :], start=True, stop=False),
                  reads=[nl_tk, c["tk"]], writes=[self.pst[b]], join=(aw_d is not None))
            sc.op("pe", lambda e: e.matmul(self.ps[b][:, 0:16], lhsT=c["ones32"][:], rhs=runs[:, t, :], start=False, stop=True),
                  reads=[runs_tk, c["tk"]], writes=[self.pst[b]], join=True)
            cur, cur_tk = cns[t % 2], cn_tk[t % 2]
            ncp = 32 if aw_d is not None else 16
            sc.op("dve", lambda e: e.tensor_copy(out=cur[:, 0:ncp], in_=self.ps[b][:, 0:ncp]), reads=[self.pst[b]], writes=[cur_tk])
            if aw_d is not None:
                sc.dma("sp", cur_tk, aw_d[t * 128:(t + 1) * 128, :], cur[:, 16:32], reads=[cur_tk])
            b2 = self.bank()
            sc.op("pe", lambda e: e.transpose(out=self.ps[b2][0:16, 0:128], in_=cur[:, 0:16], identity=c["ident32"][:]),
                  reads=[cur_tk, c["tk"]], writes=[self.pst[b2]])
            sc.op("act", lambda e: e.activation(out=cT[:, t * 128:(t + 1) * 128], in_=self.ps[b2][0:16, 0:128], func=AF.Copy),
                  reads=[self.pst[b2]], writes=[cT_tk], join=True)
        pb = [(self.sb([16, S], BF16), Tk()) for _ in range(3)]
        nb = [(self.sb([16, S], BF16), Tk()) for _ in range(3)]
        f32a, f32a_tk = self.sb([16, S], F32), Tk()
        rem, rem_tk = cT, cT_tk
        for i in range(3):
            p_, p_tk = pb[i]
            n_, n_tk = nb[i]
            sc.op("dve", lambda e: e.tensor_copy(out=p_[:], in_=rem[:]), reads=[rem_tk], writes=[p_tk])
            sc.op("act", lambda e: e.activation(out=n_[:], in_=p_[:], func=AF.Copy, scale=-1.0), reads=[p_tk], writes=[n_tk])
            if i < 2:
                sc.op("pool", lambda e: e.tensor_copy(out=f32a[:], in_=p_[:]), reads=[p_tk], writes=[f32a_tk])
                sc.op("dve", lambda e: e.tensor_tensor(out=rem[:], in0=rem[:], in1=f32a[:], op=ALU.subtract),
                      reads=[rem_tk, f32a_tk], writes=[rem_tk])
        onesb, onesb_tk = self.sb([16, S], BF16), Tk()
        sc.op("pool", lambda e: e.memset(onesb[:], 1.0), writes=[onesb_tk])
        for i in range(3):
            sc.dma("sp", nb[i][1], RL_d[:, i, :], nb[i][0][:], reads=[nb[i][1]])
            sc.dma("sp", pb[i][1], RL_d[:, 9 + i, :], pb[i][0][:], reads=[pb[i][1]])
            sc.dma("sp", onesb_tk, RL_d[:, 3 + i, :], onesb[:], reads=[onesb_tk])
            sc.dma("sp", onesb_tk, RL_d[:, 6 + i, :], onesb[:], reads=[onesb_tk])
        out_tks += [onesb_tk] + [tk for _, tk in pb] + [tk for _, tk in nb] + cn_tk

    def attn_core(self, H, GQ, load_head, score_mm, y_d, c, bias=True, selmask=False, linear=False, VW=65, ycol0=0):
        sc, S, NT = self.sc, self.S, self.NT
        NGR = NT // GQ
        W = GQ * 128
        NPT = 4
        pts = [(self.sb([128, W], BF16), Tk()) for _ in range(NPT)]
        ets = [(self.sb([128, W], F32), Tk()) for _ in range(3)] if linear else None
        yos = [(self.sb([128, GQ, 64], BF16), Tk()) for _ in range(2)]
        recs = [(self.sb([128, 4], F32), Tk()) for _ in range(2)]
        obanks2 = [6, 7] if linear else [4, 5]
        sbanks = [0, 1, 2, 3, 4, 5] if linear else [0, 1, 2, 3]
        NSB = len(sbanks)
        LAG = 2
        st = {"sb": 0, "et": 0}
        out_tks = [tk for _, tk in yos]
        heads = {}

        def get_head(h):
            if h not in heads:
                heads[h] = load_head(h, h % 2)
            return heads[h]

        blocks = [(h, G, kt) for h in range(H) for G in range(NGR) for kt in range((G + 1) * GQ)]

        def stage_a(i):
            h, G, kt = blocks[i]
            hd = get_head(h)
            htks = hd["tks"]
            j = kt - G * GQ
            c0 = max(j, 0) * 128
            q0 = G * W + c0
            q1 = (G + 1) * W
            b = sbanks[st["sb"] % NSB]
            st["sb"] += 1
            lhsT, rhs = score_mm(hd, kt, q0, q1, j)
            last_is_score = (not (bias or j >= 0 or selmask)) or linear
            sc.op("pe", lambda e: e.matmul(self.ps[b][:, c0:W], lhsT=lhsT, rhs=rhs, start=True, stop=last_is_score),
                  reads=htks, writes=[self.pst[b]])
            if linear:
                bd = sbanks[st["sb"] % NSB]
                st["sb"] += 1
                first = True
            else:
                bd = b
                first = False
            if bias:
                lastb = not (j >= 0 or (selmask and j < 0))
                sc.op("pe", lambda e: e.matmul(self.ps[bd][:, c0:W], lhsT=hd["L"][:, kt * 128:(kt + 1) * 128], rhs=hd["R"][:, q0:q1],
                                               start=first, stop=lastb),
                      reads=htks, writes=[self.pst[bd]], join=not first)
                first = False
            if selmask and j < 0:
                blk = kt // 2
                sc.op("pe", lambda e: e.matmul(self.ps[bd][:, c0:W], lhsT=c["onehot"][:, blk, :], rhs=hd["NS"][:, q0:q1],
                                               start=False, stop=True),
                      reads=htks + [c["tk"]], writes=[self.pst[bd]], join=True)
            if j >= 0:
                sc.op("pe", lambda e: e.matmul(self.ps[bd][:, c0:c0 + 128], lhsT=self.ident[:], rhs=c["tri"][:],
                                               start=first, stop=True),
                      reads=[self.ident_t, c["tk"]], writes=[self.pst[bd]], join=True)
            return (b, bd, c0, j)

        def stage_b(i, info):
            b, bd, c0, j = info
            pt, pt_tk = pts[i % NPT]
            if not linear:
                sc.op("act", lambda e: e.activation(out=pt[:, c0:W], in_=self.ps[b][:, c0:W], func=AF.Exp),
                      reads=[self.pst[b]], writes=[pt_tk])
            else:
                et, et_tk = ets[st["et"] % 3]
                st["et"] += 1
                sc.op("act", lambda e: e.activation(out=et[:, c0:W], in_=self.ps[bd][:, c0:W], func=AF.Exp),
                      reads=[self.pst[bd]], writes=[et_tk])
                sc.op("dve", lambda e: e.tensor_tensor(out=pt[:, c0:W], in0=self.ps[b][:, c0:W], in1=et[:, c0:W], op=ALU.mult),
                      reads=[self.pst[b], et_tk], writes=[pt_tk])

        def stage_c(i, info):
            h, G, kt = blocks[i]
            hd = get_head(h)
            htks = hd["tks"]
            b, bd, c0, j = info
            pt, pt_tk = pts[i % NPT]
            gi = h * NGR + G
            ob = obanks2[gi % 2]
            for qi in range(max(j, 0), GQ):
                sc.op("pe", lambda e: e.matmul(self.ps[ob][:, qi * 128:qi * 128 + VW], lhsT=pt[:, qi * 128:(qi + 1) * 128],
                                               rhs=hd["V"][:, kt, 0:VW], start=(kt == 0 and qi == 0), stop=(kt == G * GQ + qi),
                                               skip_group_check=True),
                      reads=[pt_tk] + htks, writes=[self.pst[ob]], join=not (kt == 0 and qi == 0))
            if kt == (G + 1) * GQ - 1:
                yo, yo_tk = yos[gi % 2]
                rec, rec_tk = recs[gi % 2]
                ov = self.ps[ob][:, 0:GQ * 128].rearrange("p (q c) -> p q c", c=128)
                if not linear:
                    sc.op("dve", lambda e: e.reciprocal(out=rec[:, 0:GQ], in_=ov[:, :, 64]), reads=[self.pst[ob]], writes=[rec_tk])
                    for qi in range(GQ):
                        sc.op("dve", lambda e: e.tensor_scalar_mul(out=yo[:, qi, :], in0=ov[:, qi, 0:64], scalar1=rec[:, qi:qi + 1]),
                              reads=[self.pst[ob], rec_tk], writes=[yo_tk], join=(qi > 0))
                else:
                    sc.op("dve", lambda e: e.tensor_copy(out=yo[:], in_=ov[:, :, 0:64]), reads=[self.pst[ob]], writes=[yo_tk])
                sc.dma("sp", yo_tk, y_d[G * W:(G + 1) * W, ycol0 + h * 64:ycol0 + (h + 1) * 64].rearrange("(q p) d -> p q d", p=128),
                       yo[:], reads=[yo_tk])

        infos = {}
        n = len(blocks)
        first_blk = {}
        for i, (h, G, kt) in enumerate(blocks):
            if G == 0 and kt == 0:
                first_blk[i + LAG - 1] = h
        nblk_head = n // H
        bgq = {"q": [], "rate": 0}

        def flush_bg(k=None):
            m = len(bgq["q"]) if k is None else min(k, len(bgq["q"]))
            for _ in range(m):
                bgq["q"].pop(0)()

        for i in range(n + LAG):
            if i < n:
                h, G, kt = blocks[i]
                if G == 0 and kt == 0:
                    if h not in heads:
                        hd0 = get_head(h)
                        bgq["q"] = list(hd0.get("bg", []))
                    flush_bg()
                infos[i] = stage_a(i)
                stage_b(i, infos[i])
            if i >= LAG:
                stage_c(i - LAG, infos.pop(i - LAG))
            if i in first_blk and first_blk[i] + 1 < H:
                hdn = get_head(first_blk[i] + 1)
                bgq["q"] = list(hdn.get("bg", []))
                bgq["rate"] = -(-len(bgq["q"]) // max(nblk_head - LAG - 2, 1))
            elif bgq["q"]:
                flush_bg(bgq["rate"])
        return out_tks

    def weight_preconvert(self, jobs):
        sc = self.sc
        stg = [(self.sb([128, 2048], F32), Tk()) for _ in range(2)]
        obs = [(self.sb([128, 2048], BF16), Tk()) for _ in range(2)]
        i = 0
        for w32, wbf, cc in jobs:
            K_, N_ = w32.shape[0], w32.shape[1]
            for k in range(K_ // 128):
                for c0 in range(0, N_, cc):
                    st_, st_tk = stg[i % 2]
                    ob, ob_tk = obs[i % 2]
                    sc.dma("pool", st_tk, st_[:, 0:cc], w32[k * 128:(k + 1) * 128, c0:c0 + cc], writes=[st_tk])
                    sc.op("pool", lambda e: e.tensor_copy(out=ob[:, 0:cc], in_=st_[:, 0:cc]), reads=[st_tk], writes=[ob_tk])
                    sc.dma("pool", ob_tk, wbf[k * 128:(k + 1) * 128, c0:c0 + cc], ob[:, 0:cc], reads=[ob_tk])
                    i += 1
        return [tk for _, tk in stg] + [tk for _, tk in obs]

    def phase_fox_attn(self, qkT_d, vp_d, RL_d, y_d, c, wjobs=None):
        sc, S, NT = self.sc, self.S, self.NT
        self.sb_reset()
        slots = []
        for sl in range(2):
            d = {"q": self.sb([128, S], BF16), "k": self.sb([128, S], BF16), "V": self.sb([128, NT, 65], BF16), "tk": Tk()}
            sc.op("pool", lambda e: e.memset(d["q"][64:128, :], 0.0), writes=[d["tk"]])
            sc.op("pool", lambda e: e.memset(d["k"][64:128, :], 0.0), writes=[d["tk"]], join=True)
            slots.append(d)

        def load_head(h, sl):
            d = slots[sl]
            tk = d["tk"]
            sc.dma("sp", tk, d["q"][0:64, :], qkT_d[h * 64:(h + 1) * 64, :], writes=[tk])
            sc.dma("sp", tk, d["k"][0:64, :], qkT_d[1024 + h * 64:1024 + (h + 1) * 64, :], writes=[tk], join=True)
            sc.dma("sp", tk, d["q"][64:70, :], RL_d[h, 0:6, :], writes=[tk], join=True)
            sc.dma("sp", tk, d["k"][64:70, :], RL_d[h, 6:12, :], writes=[tk], join=True)
            sc.dma("sp", tk, d["V"][:], vp_d[:, h * 65:(h + 1) * 65].rearrange("(t p) d -> p t d", p=128), writes=[tk], join=True)
            return {"q": d["q"], "k": d["k"], "V": d["V"], "tks": [tk]}

        def score_mm(hd, kt, q0, q1, j):
            return hd["k"][:, kt * 128:(kt + 1) * 128], hd["q"][:, q0:q1]

        wtks = self.weight_preconvert(wjobs) if wjobs else []
        out_tks = self.attn_core(16, 4 if NT >= 4 else NT, load_head, score_mm, y_d, c, bias=False)
        self.phase_barrier(out_tks + [d["tk"] for d in slots] + wtks)

    def phase_out_proj(self, x_dram, y_d, w_out, KT, xout_dram=None, wbf=None):
        sc, S, NT = self.sc, self.S, self.NT
        xout_dram = x_dram if xout_dram is None else xout_dram
        self.sb_reset()
        wo, wo_tk = self.sb([128, KT, D], BF16), Tk()
        stage = [self.sb([128, 1024], F32) for _ in range(2)]
        stage_tk = [Tk() for _ in range(2)]
        if wbf is None:
            self.load_weight(wo, wo_tk, w_out, KT * 128, D, stage, stage_tk, col_chunk=1024)
        else:
            for k in range(KT):
                sc.dma("sp", wo_tk, wo[:, k, :], wbf[k * 128:(k + 1) * 128, :], writes=[wo_tk], join=(k > 0))
        yts = [(self.sb([128, KT * 128], BF16), Tk()) for _ in range(2)]
        yTs = [(self.sb([128, KT, 128], BF16), Tk()) for _ in range(2)]
        xres = [(self.sb([128, 512], F32), Tk()) for _ in range(4)]
        sc.dma("sp", yts[0][1], yts[0][0][:], y_d[0:128, :], writes=[yts[0][1]])
        for t in range(NT):
            yt, yt_tk = yts[t % 2]
            yT, yT_tk = yTs[t % 2]
            if t + 1 < NT:
                sc.dma("sp", yts[(t + 1) % 2][1], yts[(t + 1) % 2][0][:], y_d[(t + 1) * 128:(t + 2) * 128, :], writes=[yts[(t + 1) % 2][1]])
            for cc in range(2):
                xr, xr_tk = xres[(t * 2 + cc) % 4]
                sc.dma("sp", xr_tk, xr[:], x_dram[t * 128:(t + 1) * 128, cc * 512:(cc + 1) * 512], writes=[xr_tk])
            for k0 in range(0, KT, 8):
                kn = min(8, KT - k0)
                b = self.bank()
                pt = self.ps[b][:].bitcast(BF16)
                for k in range(kn):
                    sc.op("pe", lambda e: e.transpose(out=pt[:, k * 128:(k + 1) * 128], in_=yt[:, (k0 + k) * 128:(k0 + k + 1) * 128],
                                                      identity=self.ident[:]),
                          reads=[yt_tk, self.ident_t], writes=[self.pst[b]], join=(k > 0))
                sc.op("act", lambda e: e.activation(out=yT[:, k0:k0 + kn, :], in_=pt[:, 0:kn * 128].rearrange("p (k t) -> p k t", k=kn),
                                                    func=AF.Copy),
                      reads=[self.pst[b]], writes=[yT_tk], join=(k0 > 0))
            for cc in range(2):
                xr, xr_tk = xres[(t * 2 + cc) % 4]
                b = self.bank()
                for k in range(KT):
                    sc.op("pe", lambda e: e.matmul(self.ps[b][:], lhsT=yT[:, k, :], rhs=wo[:, k, cc * 512:(cc + 1) * 512],
                                                   start=(k == 0), stop=(k == KT - 1)),
                          reads=[wo_tk, yT_tk], writes=[self.pst[b]], join=(k > 0))
                sc.op("dve", lambda e: e.tensor_tensor(out=xr[:], in0=self.ps[b][:], in1=xr[:], op=ALU.add),
                      reads=[self.pst[b], xr_tk], writes=[xr_tk])
                sc.dma("sp", xr_tk, xout_dram[t * 128:(t + 1) * 128, cc * 512:(cc + 1) * 512], xr[:], reads=[xr_tk])
        self.phase_barrier([tk for _, tk in xres])

    def phase_l0_proj(self, x_dram, g_dram, w_in, conv_w, conv_b, dtb_d, alog_d, cos_d, sin_d,
                      xbcT_d, qkT_d, zs_d, vp_d, RL_d, c, aw_d=None):
        sc, S, NT = self.sc, self.S, self.NT
        NG = S // 512
        self.sb_reset()
        nl, nl_tk = self.sb([128, NT, 16], F32), Tk()
        dtk, dtk_tk = self.sb([128, NT, 16], F32), Tk()
        self.persist_end = self.sb_off
        self.dtk_view = dtk
        hT, hT_tk, mark = self.norm_all(x_dram, g_dram)
        self.phase_barrier([hT_tk])
        self.sb_off = mark
        wts = [(self.sb([128, 8, 128], BF16), Tk()) for _ in range(4)]
        stgs = [(self.sb([128, 8, 128], F32), Tk()) for _ in range(2)]
        rows = [(self.sb([128, S], BF16), Tk()) for _ in range(2)]
        out_tks = []
        cw, cw_tk = self.sb([128, 4, 16], F32), Tk()
        cb, cb_tk = self.sb([128, 16], F32), Tk()
        for k in range(4):
            sc.dma("sp", cw_tk, cw[:, k, :], conv_w[k, :].rearrange("(f p) -> p f", p=128), writes=[cw_tk], join=(k > 0),
                   allow_slow_non_contiguous=True)
        sc.dma("sp", cb_tk, cb[:], conv_b[0, :].rearrange("(f p) -> p f", p=128), writes=[cb_tk], allow_slow_non_contiguous=True)
        u, u_tk = self.sb([128, S + 8], F32), Tk()
        acc, acc_tk = self.sb([128, S], F32), Tk()
        sc.op("pool", lambda e: e.memset(u[:, 0:8], 0.0), writes=[u_tk])
        self.load_wcols(w_in, 1024, 128, wts[0][0], wts[0][1], stgs[0][0], stgs[0][1])
        for f in range(16):
            wt, wt_tk = wts[f % 2]
            row, row_tk = rows[f % 2]
            if f + 1 < 16:
                self.load_wcols(w_in, 1024 + (f + 1) * 128, 128, wts[(f + 1) % 2][0], wts[(f + 1) % 2][1], stgs[(f + 1) % 2][0], stgs[(f + 1) % 2][1])
            for tg in range(NG):
                b = self.bank()
                for k in range(8):
                    sc.op("pe", lambda e: e.matmul(self.ps[b][:], lhsT=wt[:, k, :], rhs=hT[:, k, tg * 512:(tg + 1) * 512],
                                                   start=(k == 0), stop=(k == 7)),
                          reads=[wt_tk, hT_tk], writes=[self.pst[b]], join=(k > 0))
                sc.op("act", lambda e: e.activation(out=u[:, 3 + tg * 512:3 + (tg + 1) * 512], in_=self.ps[b][:], func=AF.Copy),
                      reads=[self.pst[b]], writes=[u_tk], join=True)
            sc.op("dve", lambda e: e.tensor_scalar_mul(out=acc[:], in0=u[:, 0:S], scalar1=cw[:, 0, f:f + 1]),
                  reads=[u_tk, cw_tk], writes=[acc_tk])
            for k in range(1, 4):
                sc.op("dve", lambda e: e.scalar_tensor_tensor(out=acc[:], in0=u[:, k:k + S], scalar=cw[:, k, f:f + 1], in1=acc[:],
                                                              op0=ALU.mult, op1=ALU.add),
                      reads=[u_tk, cw_tk, acc_tk], writes=[acc_tk])
            sc.op("act", lambda e: e.activation(out=row[:], in_=acc[:], func=AF.Silu, bias=cb[:, f:f + 1]),
                  reads=[acc_tk, cb_tk], writes=[row_tk])
            sc.dma("sp", row_tk, xbcT_d[f * 128:(f + 1) * 128, :], row[:], reads=[row_tk])
            out_tks.append(row_tk)
        self.phase_barrier(out_tks)
        self.sb_off = mark
        wts = [(self.sb([128, 8, 128], BF16), Tk()) for _ in range(4)]
        stgs = [(self.sb([128, 8, 128], F32), Tk()) for _ in range(2)]
        rows = [(self.sb([128, S], BF16), Tk()) for _ in range(2)]
        out_tks = []
        cosT, sinT, cs_tk = self.sb([128, S], F32), self.sb([128, S], F32), Tk()
        sc.dma("sp", cs_tk, cosT[:], cos_d[:, :], writes=[cs_tk])
        sc.dma("sp", cs_tk, sinT[:], sin_d[:, :], writes=[cs_tk], join=True)
        t1s = [(self.sb([128, 512], F32), Tk()) for _ in range(2)]
        t2s = [(self.sb([128, 512], F32), Tk()) for _ in range(2)]
        i = 0

        def load_qk(ii):
            col_ = 3088 + (ii // 4) * 512 + (ii % 4) * 128
            self.load_wcols(w_in, col_, 128, wts[ii % 2][0], wts[ii % 2][1], stgs[0][0], stgs[0][1])
            self.load_wcols(w_in, col_, 128, wts[2 + ii % 2][0], wts[2 + ii % 2][1], stgs[1][0], stgs[1][1], perm_heads=True)

        load_qk(0)
        for which in range(2):
            for f in range(4):
                wt, wt_tk = wts[i % 2]
                wp, wp_tk = wts[2 + i % 2]
                row, row_tk = rows[i % 2]
                if i + 1 < 8:
                    load_qk(i + 1)
                for tg in range(NG):
                    ba, bb = self.bank(), self.bank()
                    for k in range(8):
                        sc.op("pe", lambda e: e.matmul(self.ps[ba][:], lhsT=wt[:, k, :], rhs=hT[:, k, tg * 512:(tg + 1) * 512],
                                                       start=(k == 0), stop=(k == 7)),
                              reads=[wt_tk, hT_tk], writes=[self.pst[ba]], join=(k > 0))
                    for k in range(8):
                        sc.op("pe", lambda e: e.matmul(self.ps[bb][:], lhsT=wp[:, k, :], rhs=hT[:, k, tg * 512:(tg + 1) * 512],
                                                       start=(k == 0), stop=(k == 7)),
                              reads=[wp_tk, hT_tk], writes=[self.pst[bb]], join=(k > 0))
                    t1, t1_tk = t1s[tg % 2]
                    t2, t2_tk = t2s[tg % 2]
                    sc.op("dve", lambda e: e.tensor_tensor(out=t1[:], in0=self.ps[ba][:], in1=cosT[:, tg * 512:(tg + 1) * 512], op=ALU.mult),
                          reads=[self.pst[ba], cs_tk], writes=[t1_tk])
                    sc.op("dve", lambda e: e.tensor_tensor(out=t2[:], in0=self.ps[bb][:], in1=sinT[:, tg * 512:(tg + 1) * 512], op=ALU.mult),
                          reads=[self.pst[bb], cs_tk], writes=[t2_tk])
                    sc.op("pool", lambda e: e.tensor_tensor(out=t1[:], in0=t1[:], in1=t2[:], op=ALU.add),
                          reads=[t1_tk, t2_tk], writes=[t1_tk])
                    sc.op("act", lambda e: e.activation(out=row[:, tg * 512:(tg + 1) * 512], in_=t1[:], func=AF.Copy,
                                                        scale=(0.125 if which == 0 else 1.0)),
                          reads=[t1_tk], writes=[row_tk], join=(tg > 0))
                sc.dma("sp", row_tk, qkT_d[which * 512 + f * 128:which * 512 + (f + 1) * 128, :], row[:], reads=[row_tk])
                out_tks.append(row_tk)
                i += 1
        self.phase_barrier(out_tks)
        self.sb_off = mark
        out_tks = []
        wz, wz_tk = self.sb([128, 8, 1024], BF16), Tk()
        wv, wv_tk = self.sb([128, 8, 512], BF16), Tk()
        wd, wd_tk = self.sb([128, 8, 16], BF16), Tk()
        stz, stz_tk = self.sb([128, 8, 512], F32), Tk()
        for cc in range(2):
            self.load_wcols(w_in, cc * 512, 512, wz[:, :, cc * 512:(cc + 1) * 512], wz_tk, stz, stz_tk)
        self.load_wcols(w_in, 4112, 512, wv, wv_tk, stz, stz_tk)
        self.load_wcols(w_in, 3072, 16, wd, wd_tk, stz, stz_tk)
        dtb, ea, sm_tk = self.sb([128, 16], F32), self.sb([128, 16], F32), Tk()
        sc.dma("sp", sm_tk, dtb[:], dtb_d.partition_broadcast(128), writes=[sm_tk])
        sc.dma("sp", sm_tk, ea[:], alog_d.partition_broadcast(128), writes=[sm_tk], join=True)
        sc.op("act", lambda e: e.activation(out=ea[:], in_=ea[:], func=AF.Exp), reads=[sm_tk], writes=[sm_tk])
        zts = [(self.sb([128, 1024], BF16), Tk()) for _ in range(2)]
        vts = [(self.sb([128, 8, 65], BF16), Tk()) for _ in range(2)]
        for vt, vt_tk in vts:
            sc.op("pool", lambda e: e.memset(vt[:], 1.0), writes=[vt_tk])
        for t in range(NT):
            zt, zt_tk = zts[t % 2]
            vt, vt_tk = vts[t % 2]
            for cc in range(2):
                b = self.bank()
                for k in range(8):
                    sc.op("pe", lambda e: e.matmul(self.ps[b][:], lhsT=hT[:, k, t * 128:(t + 1) * 128], rhs=wz[:, k, cc * 512:(cc + 1) * 512],
                                                   start=(k == 0), stop=(k == 7)),
                          reads=[wz_tk, hT_tk], writes=[self.pst[b]], join=(k > 0))
                sc.op("act", lambda e: e.activation(out=zt[:, cc * 512:(cc + 1) * 512], in_=self.ps[b][:], func=AF.Silu),
                      reads=[self.pst[b]], writes=[zt_tk], join=(cc > 0))
            sc.dma("sp", zt_tk, zs_d[t * 128:(t + 1) * 128, :], zt[:], reads=[zt_tk])
            b = self.bank()
            for k in range(8):
                sc.op("pe", lambda e: e.matmul(self.ps[b][:], lhsT=hT[:, k, t * 128:(t + 1) * 128], rhs=wv[:, k, :],
                                               start=(k == 0), stop=(k == 7)),
                      reads=[wv_tk, hT_tk], writes=[self.pst[b]], join=(k > 0))
            sc.op("act", lambda e: e.activation(out=vt[:, :, 0:64], in_=self.ps[b][:].rearrange("p (h d) -> p h d", h=8), func=AF.Copy),
                  reads=[self.pst[b]], writes=[vt_tk])
            sc.dma("sp", vt_tk, vp_d[t * 128:(t + 1) * 128, :], vt.rearrange("p h d -> p (h d)"), reads=[vt_tk])
            b = self.bank()
            for k in range(8):
                sc.op("pe", lambda e: e.matmul(self.ps[b][:, 0:16], lhsT=hT[:, k, t * 128:(t + 1) * 128], rhs=wd[:, k, :],
                                               start=(k == 0), stop=(k == 7)),
                      reads=[wd_tk, hT_tk], writes=[self.pst[b]], join=(k > 0))
            sc.op("dve", lambda e: e.tensor_tensor(out=dtk[:, t, :], in0=self.ps[b][:, 0:16], in1=dtb[:], op=ALU.add),
                  reads=[self.pst[b], sm_tk], writes=[dtk_tk], join=True)
            out_tks += [zt_tk, vt_tk]
        sc.op("act", lambda e: e.activation(out=dtk[:], in_=dtk[:], func=AF.Exp), reads=[dtk_tk], writes=[dtk_tk])
        sc.op("dve", lambda e: e.tensor_scalar_add(out=dtk[:], in0=dtk[:], scalar1=1.0), reads=[dtk_tk], writes=[dtk_tk])
        sc.op("act", lambda e: e.activation(out=dtk[:], in_=dtk[:], func=AF.Ln), reads=[dtk_tk], writes=[dtk_tk])
        for t in range(NT):
            sc.op("dve", lambda e: e.tensor_tensor(out=nl[:, t, :], in0=dtk[:, t, :], in1=ea[:], op=ALU.mult),
                  reads=[dtk_tk, sm_tk], writes=[nl_tk], join=True)
        self.phase_barrier(out_tks + [nl_tk, dtk_tk])
        self.sb_off = self.persist_end
        out_tks = []
        self.cumsum_rows(nl, nl_tk, RL_d, c, out_tks, aw_d=aw_d)
        self.phase_barrier(out_tks)

    def phase_ssd_prep(self, xbcT_d, xs_d, xc_d, btok_d=None):
        sc, S, NT = self.sc, self.S, self.NT
        self.sb_off = self.persist_end
        dtk = self.dtk_view
        TG = 4 if NT >= 4 else NT
        ins = [(self.sb([128, 8, TG * 128], BF16), Tk()) for _ in range(2)]
        xss = [(self.sb([128, 16, 64], BF16), Tk()) for _ in range(2)]
        xcs = [(self.sb([128, 16, 64], BF16), Tk()) for _ in range(2)]
        inb = [(self.sb([128, 4, TG * 128], BF16), Tk()) for _ in range(2)]
        bts = [(self.sb([128, 512], BF16), Tk()) for _ in range(2)]

        def prep_load(gg):
            cs = slice(gg * TG * 128, (gg + 1) * TG * 128)
            sc.dma("sp", ins[gg % 2][1], ins[gg % 2][0][:], xbcT_d[0:1024, cs].rearrange("(k p) t -> p k t", p=128), writes=[ins[gg % 2][1]])
            if btok_d is not None:
                sc.dma("sp", inb[gg % 2][1], inb[gg % 2][0][:], xbcT_d[1024:1536, cs].rearrange("(k p) t -> p k t", p=128), writes=[inb[gg % 2][1]])

        for t in range(NT):
            tg, tt = t // TG, t % TG
            it_, it_tk = ins[tg % 2]
            it_ = it_[:, :, tt * 128:(tt + 1) * 128]
            xs, xs_tk = xss[t % 2]
            xc, xc_tk = xcs[t % 2]
            if tt == 0:
                if tg == 0:
                    prep_load(0)
                if (tg + 1) * TG < NT:
                    prep_load(tg + 1)
            b = self.bank()
            pt = self.ps[b][:].bitcast(BF16)
            for k in range(8):
                sc.op("pe", lambda e: e.transpose(out=pt[:, k * 128:(k + 1) * 128], in_=it_[:, k, :], identity=self.ident[:]),
                      reads=[it_tk, self.ident_t], writes=[self.pst[b]], join=(k > 0))
            sc.op("act", lambda e: e.activation(out=xs.rearrange("p h d -> p (h d)"), in_=pt, func=AF.Copy),
                  reads=[self.pst[b]], writes=[xs_tk])
            for h in range(16):
                sc.op("dve" if h % 2 == 0 else "pool", lambda e: e.tensor_scalar_mul(out=xc[:, h, :], in0=xs[:, h, :], scalar1=dtk[:, t, h:h + 1]),
                      reads=[xs_tk], writes=[xc_tk], join=(h > 0))
            sc.dma("sp", xs_tk, xs_d[t * 128:(t + 1) * 128, :], xs.rearrange("p h d -> p (h d)"), reads=[xs_tk])
            sc.dma("sp", xc_tk, xc_d[t * 128:(t + 1) * 128, :], xc.rearrange("p h d -> p (h d)"), reads=[xc_tk])
            if btok_d is not None:
                ib, ib_tk = inb[tg % 2]
                ib = ib[:, :, tt * 128:(tt + 1) * 128]
                bt, bt_tk = bts[t % 2]
                b = self.bank()
                pt = self.ps[b][:].bitcast(BF16)
                for k in range(4):
                    sc.op("pe", lambda e: e.transpose(out=pt[:, k * 128:(k + 1) * 128], in_=ib[:, k, :], identity=self.ident[:]),
                          reads=[ib_tk, self.ident_t], writes=[self.pst[b]], join=(k > 0))
                sc.op("act", lambda e: e.activation(out=bt[:], in_=pt[:, 0:512], func=AF.Copy), reads=[self.pst[b]], writes=[bt_tk])
                sc.dma("sp", bt_tk, btok_d[t * 128:(t + 1) * 128, :], bt[:], reads=[bt_tk])
        self.phase_barrier([tk for _, tk in xss] + [tk for _, tk in xcs] + [tk for _, tk in bts])

    def phase_ssd_chunk(self, xbcT_d, btok_d, xc_d, RL_d, aw_d, y_d, c):
        sc, S, NT = self.sc, self.S, self.NT
        self.sb_reset()
        aw, ea, dte, cd = (self.sb([128, NT, 16], F32) for _ in range(4))
        f_tk = Tk()
        sc.dma("sp", f_tk, aw[:], aw_d.rearrange("(t p) h -> p t h", p=128), writes=[f_tk])
        awf = aw.rearrange("p t h -> p (t h)")
        NF = NT * 16
        bt_ = self.bank()
        sc.op("pe", lambda e: e.matmul(self.ps[bt_][:, 0:NF], lhsT=c["sel127"][:], rhs=awf, start=True, stop=True),
              reads=[f_tk, c["tk"]], writes=[self.pst[bt_]])
        tot = self.sb([128, NF], F32)
        sc.op("dve", lambda e: e.tensor_copy(out=tot[:], in_=self.ps[bt_][:, 0:NF]), reads=[self.pst[bt_]], writes=[f_tk], join=True)
        sc.op("act", lambda e: e.activation(out=ea.rearrange("p t h -> p (t h)"), in_=awf, func=AF.Exp, scale=-1.0),
              reads=[f_tk], writes=[f_tk], join=True)
        sc.op("dve", lambda e: e.tensor_tensor(out=dte.rearrange("p t h -> p (t h)"), in0=awf, in1=tot[:], op=ALU.subtract),
              reads=[f_tk], writes=[f_tk])
        sc.op("act", lambda e: e.activation(out=dte.rearrange("p t h -> p (t h)"), in_=dte.rearrange("p t h -> p (t h)"), func=AF.Exp),
              reads=[f_tk], writes=[f_tk])
        sc.op("act", lambda e: e.activation(out=cd.rearrange("p t h -> p (t h)"), in_=tot[:], func=AF.Exp, scale=-1.0),
              reads=[f_tk], writes=[f_tk])
        H = self.sb([128, 16, 64], F32)
        Hb = self.sb([128, 1024], BF16)
        H_tk = [Tk() for _ in range(4)]
        Hb_tk = [Tk() for _ in range(4)]
        sc.op("pool", lambda e: e.memset(H[:], 0.0), writes=H_tk)
        sc.op("pool", lambda e: e.memset(Hb[:], 0.0), writes=Hb_tk)
        ld = []
        for i in range(2):
            d = {"CT": self.sb([128, 4, 128], BF16), "BT": self.sb([128, 4, 128], BF16), "Bk": self.sb([128, 512], BF16),
                 "xc": self.sb([128, 1024], BF16), "L": self.sb([128, 16, 128], BF16), "R": self.sb([128, 16, 128], BF16), "tk": Tk()}
            sc.op("pool", lambda e: e.memset(d["L"][:], 0.0), writes=[d["tk"]])
            sc.op("pool", lambda e: e.memset(d["R"][:], 0.0), writes=[d["tk"]], join=True)
            ld.append(d)

        def load(ci):
            d = ld[ci % 2]
            tk = d["tk"]
            cs = slice(ci * 128, (ci + 1) * 128)
            sc.dma("sp", tk, d["CT"][:], xbcT_d[1536:2048, cs].rearrange("(g n) t -> n g t", n=128), writes=[tk])
            sc.dma("sp", tk, d["BT"][:], xbcT_d[1024:1536, cs].rearrange("(g n) t -> n g t", n=128), writes=[tk], join=True)
            sc.dma("sp", tk, d["Bk"][:], btok_d[cs, :], writes=[tk], join=True)
            sc.dma("sp", tk, d["xc"][:], xc_d[cs, :], writes=[tk], join=True)
            sc.dma("sp", tk, d["R"][0:6, :, :], RL_d[:, 0:6, cs].rearrange("h j t -> j h t"), writes=[tk], join=True)
            sc.dma("sp", tk, d["L"][0:6, :, :], RL_d[:, 6:12, cs].rearrange("h j t -> j h t"), writes=[tk], join=True)

        ets = [(self.sb([128, 512], F32), Tk()) for _ in range(2)]
        mts = [(self.sb([128, 4, 128], BF16), Tk()) for _ in range(2)]
        yds = [(self.sb([128, 256], F32), Tk()) for _ in range(2)]
        xcds = [(self.sb([128, 256], BF16), Tk()) for _ in range(2)]
        yts = [(self.sb([128, 1024], BF16), Tk()) for _ in range(2)]
        load(0)
        if NT > 1:
            load(1)
        steps = [(ci, g) for ci in range(NT) for g in range(4)]

        def bufs_for(i):
            return {"db": 2 + i % 2, "yb": 4 + i % 2, "sbk": 6 + i % 2, "et": ets[i % 2], "mt": mts[i % 2], "yd": yds[i % 2], "xcd": xcds[i % 2]}

        def front(i):
            ci, g = steps[i]
            d = ld[ci % 2]
            tk = d["tk"]
            cbk = ci % 2
            B = bufs_for(i)
            db = B["db"]
            et, et_tk = B["et"]
            mt, mt_tk = B["mt"]
            sc.op("pe", lambda e: e.matmul(self.ps[cbk][:, g * 128:(g + 1) * 128], lhsT=d["BT"][:, g, :], rhs=d["CT"][:, g, :],
                                           start=True, stop=True, skip_group_check=True),
                  reads=[tk], writes=[self.pst[cbk]], join=(g > 0))
            for r in range(4):
                h = 4 * g + r
                sc.op("pe", lambda e: e.matmul(self.ps[db][:, r * 128:(r + 1) * 128], lhsT=d["L"][:, h, :], rhs=d["R"][:, h, :],
                                               start=True, stop=False, skip_group_check=True),
                      reads=[tk], writes=[self.pst[db]], join=(r > 0))
                sc.op("pe", lambda e: e.matmul(self.ps[db][:, r * 128:(r + 1) * 128], lhsT=self.ident[:], rhs=c["tri"][:],
                                               start=False, stop=True, skip_group_check=True),
                      reads=[self.ident_t, c["tk"]], writes=[self.pst[db]], join=True)
            sc.op("act", lambda e: e.activation(out=et[:], in_=self.ps[db][:], func=AF.Exp), reads=[self.pst[db]], writes=[et_tk])
            for r in range(4):
                sc.op("dve", lambda e: e.tensor_tensor(out=mt[:, r, :], in0=self.ps[cbk][:, g * 128:(g + 1) * 128],
                                                       in1=et[:, r * 128:(r + 1) * 128], op=ALU.mult),
                      reads=[self.pst[cbk], et_tk], writes=[mt_tk], join=(r > 0))

        def back(i):
            ci, g = steps[i]
            d = ld[ci % 2]
            tk = d["tk"]
            yt, yt_tk = yts[ci % 2]
            B = bufs_for(i)
            yb, sbk = B["yb"], B["sbk"]
            mt, mt_tk = B["mt"]
            yd, yd_tk = B["yd"]
            xcd, xcd_tk = B["xcd"]
            for r in range(4):
                h = 4 * g + r
                sc.op("pe", lambda e: e.matmul(self.ps[yb][:, r * 64:(r + 1) * 64], lhsT=mt[:, r, :], rhs=d["xc"][:, h * 64:(h + 1) * 64],
                                               start=True, stop=True, skip_group_check=True),
                      reads=[mt_tk, tk], writes=[self.pst[yb]], join=(r > 0))
            sc.op("pe", lambda e: e.matmul(self.ps[yb][:, 256:512], lhsT=d["CT"][:, g, :], rhs=Hb[:, g * 256:(g + 1) * 256],
                                           start=True, stop=True, skip_group_check=True),
                  reads=[tk, Hb_tk[g]], writes=[self.pst[yb]], join=True)
            sc.op("act", lambda e: e.activation(out=yd[:], in_=self.ps[yb][:, 0:256], func=AF.Copy), reads=[self.pst[yb]], writes=[yd_tk])
            for r in range(4):
                h = 4 * g + r
                sc.op("dve", lambda e: e.scalar_tensor_tensor(out=yt[:, h * 64:(h + 1) * 64], in0=self.ps[yb][:, 256 + r * 64:256 + (r + 1) * 64],
                                                              scalar=ea[:, ci, h:h + 1], in1=yd[:, r * 64:(r + 1) * 64],
                                                              op0=ALU.mult, op1=ALU.add),
                      reads=[self.pst[yb], yd_tk, f_tk], writes=[yt_tk], join=not (g == 0 and r == 0))
            for r in range(4):
                h = 4 * g + r
                sc.op("pool" if r % 2 == 0 else "dve", lambda e: e.tensor_scalar_mul(out=xcd[:, r * 64:(r + 1) * 64], in0=d["xc"][:, h * 64:(h + 1) * 64],
                                                                                      scalar1=dte[:, ci, h:h + 1]),
                      reads=[tk, f_tk], writes=[xcd_tk], join=(r > 0))
            sc.op("pe", lambda e: e.matmul(self.ps[sbk][:, 0:256], lhsT=d["Bk"][:, g * 128:(g + 1) * 128], rhs=xcd[:],
                                           start=True, stop=True),
                  reads=[tk, xcd_tk], writes=[self.pst[sbk]])
            for r in range(4):
                h = 4 * g + r
                sc.op("dve", lambda e: e.scalar_tensor_tensor(out=H[:, h, :], in0=H[:, h, :], scalar=cd[:, ci, h:h + 1],
                                                              in1=self.ps[sbk][:, r * 64:(r + 1) * 64], op0=ALU.mult, op1=ALU.add),
                      reads=[self.pst[sbk], f_tk, H_tk[g]], writes=[H_tk[g]])
            sc.op("act", lambda e: e.activation(out=Hb[:, g * 256:(g + 1) * 256], in_=H[:, 4 * g:4 * g + 4, :].rearrange("p h d -> p (h d)"),
                                                func=AF.Copy),
                  reads=[H_tk[g]], writes=[Hb_tk[g]])
            if g == 3:
                sc.dma("sp", yt_tk, y_d[ci * 128:(ci + 1) * 128, :], yt[:], reads=[yt_tk])
                if ci + 2 < NT:
                    load(ci + 2)

        nst = len(steps)
        for i in range(nst + 1):
            if i < nst:
                front(i)
            if i >= 1:
                back(i - 1)
        self.phase_barrier([tk for _, tk in yts] + [d["tk"] for d in ld])

    def phase_ssd_attn(self, xbcT_d, xc_d, RL_d, y_d, c):
        sc, S, NT = self.sc, self.S, self.NT
        self.sb_reset()
        slots = []
        for sl in range(2):
            d = {"B": self.sb([128, S], BF16), "C": self.sb([128, S], BF16), "R": self.sb([128, S], BF16),
                 "L": self.sb([128, S], BF16), "V": self.sb([128, NT, 64], BF16), "tk": Tk()}
            sc.op("pool", lambda e: e.memset(d["R"][:], 0.0), writes=[d["tk"]])
            sc.op("pool", lambda e: e.memset(d["L"][:], 0.0), writes=[d["tk"]], join=True)
            slots.append(d)

        def load_head(h, sl):
            d = slots[sl]
            tk = d["tk"]
            g = h // 4
            sc.dma("sp", tk, d["B"][:], xbcT_d[1024 + g * 128:1024 + (g + 1) * 128, :], writes=[tk])
            sc.dma("sp", tk, d["C"][:], xbcT_d[1536 + g * 128:1536 + (g + 1) * 128, :], writes=[tk], join=True)
            sc.dma("sp", tk, d["R"][0:6, :], RL_d[h, 0:6, :], writes=[tk], join=True)
            sc.dma("sp", tk, d["L"][0:6, :], RL_d[h, 6:12, :], writes=[tk], join=True)
            sc.dma("sp", tk, d["V"][:], xc_d[:, h * 64:(h + 1) * 64].rearrange("(t p) d -> p t d", p=128), writes=[tk], join=True)
            return {"B": d["B"], "C": d["C"], "R": d["R"], "L": d["L"], "V": d["V"], "tks": [tk]}

        def score_mm(hd, kt, q0, q1, j):
            return hd["B"][:, kt * 128:(kt + 1) * 128], hd["C"][:, q0:q1]

        out_tks = self.attn_core(16, 4 if NT >= 4 else NT, load_head, score_mm, y_d, c, linear=True, VW=64)
        self.phase_barrier(out_tks + [d["tk"] for d in slots])

    def phase_ssd_post(self, ysc_d, xs_d, zs_d, dsk_d, gn_d, y0_d):
        sc, S, NT = self.sc, self.S, self.NT
        self.sb_reset()
        dsk, gn, cst_tk = self.sb([128, 1024], F32), self.sb([128, 1024], F32), Tk()
        sc.dma("sp", cst_tk, dsk[:], dsk_d.partition_broadcast(128), writes=[cst_tk])
        sc.dma("sp", cst_tk, gn[:], gn_d.partition_broadcast(128), writes=[cst_tk], join=True)
        NB = 2
        ld = [[(self.sb([128, 1024], BF16), Tk()) for _ in range(3)] for _ in range(NB)]
        ys = [(self.sb([128, 1024], F32), Tk()) for _ in range(NB)]
        yo = [(self.sb([128, 1024], BF16), Tk()) for _ in range(NB)]
        st = [(self.sb([128, 16], F32), Tk()) for _ in range(NB)]
        junk, junk_tk = self.sb([128, 256], F32), Tk()
        for t in range(NT):
            (a, a_tk), (x_, x_tk), (z, z_tk) = ld[t % NB]
            y, y_tk = ys[t % NB]
            o, o_tk = yo[t % NB]
            s_, s_tk = st[t % NB]
            def post_load(tt):
                (a_, a_tk_), (x__, x_tk_), (z_, z_tk_) = ld[tt % NB]
                sc.dma("sp", a_tk_, a_[:], ysc_d[tt * 128:(tt + 1) * 128, :], writes=[a_tk_])
                sc.dma("sp", x_tk_, x__[:], xs_d[tt * 128:(tt + 1) * 128, :], writes=[x_tk_])
                sc.dma("sp", z_tk_, z_[:], zs_d[tt * 128:(tt + 1) * 128, :], writes=[z_tk_])
            if t == 0:
                post_load(0)
            if t + 1 < NT:
                post_load(t + 1)
            sc.op("dve", lambda e: e.tensor_tensor(out=y[:], in0=x_[:], in1=dsk[:], op=ALU.mult), reads=[x_tk, cst_tk], writes=[y_tk])
            sc.op("pool", lambda e: e.tensor_tensor(out=y[:], in0=y[:], in1=a[:], op=ALU.add), reads=[y_tk, a_tk], writes=[y_tk])
            sc.op("dve", lambda e: e.tensor_tensor(out=y[:], in0=y[:], in1=z[:], op=ALU.mult), reads=[y_tk, z_tk], writes=[y_tk])
            for g in range(4):
                sc.op("act", lambda e: e.activation(out=junk[:], in_=y[:, g * 256:(g + 1) * 256], func=AF.Square, scale=1.0 / 16.0,
                                                    accum_out=s_[:, g:g + 1]),
                      reads=[y_tk], writes=[junk_tk, s_tk], join=(g > 0))
            sc.op("dve", lambda e: e.tensor_scalar_add(out=s_[:, 4:8], in0=s_[:, 0:4], scalar1=EPS), reads=[s_tk], writes=[s_tk])
            sc.op("act", lambda e: e.activation(out=s_[:, 8:12], in_=s_[:, 4:8], func=AF.Sqrt), reads=[s_tk], writes=[s_tk])
            sc.op("dve", lambda e: e.reciprocal(out=s_[:, 12:16], in_=s_[:, 8:12]), reads=[s_tk], writes=[s_tk])
            for g in range(4):
                sc.op("dve", lambda e: e.scalar_tensor_tensor(
                    out=o[:, g * 256:(g + 1) * 256], in0=y[:, g * 256:(g + 1) * 256], scalar=s_[:, 12 + g:13 + g],
                    in1=gn[:, g * 256:(g + 1) * 256], op0=ALU.mult, op1=ALU.mult),
                    reads=[y_tk, s_tk, cst_tk], writes=[o_tk], join=(g > 0))
            sc.dma("sp", o_tk, y0_d[t * 128:(t + 1) * 128, 0:1024], o[:], reads=[o_tk])
        self.phase_barrier([tk for _, tk in yo])

    def phase_moba_gate(self, qkT_d, mot_d, m2t_d, o1t_d, ns_d, c):
        sc, S, NT = self.sc, self.S, self.NT
        self.sb_reset()
        NBLK = S // 256
        NF = NT * 16
        mot, m2t, mk_tk = self.sb([128, NF], F32), self.sb([128, NF], F32), Tk()
        sc.dma("sp", mk_tk, mot[:], mot_d[:, 0:NF], writes=[mk_tk])
        sc.dma("sp", mk_tk, m2t[:], m2t_d[:, 0:NF], writes=[mk_tk], join=True)
        o1t = self.sb([128, NF], F32)
        sc.dma("sp", mk_tk, o1t[:], o1t_d[:, 0:NF], writes=[mk_tk], join=True)
        slots = [{"q": self.sb([64, S], BF16), "k": self.sb([64, S], BF16), "tk": Tk()} for _ in range(2)]
        wk = [{"km": self.sb([64, 16], F32), "kmb": self.sb([64, 16], BF16), "gm": self.sb([128, NF], F32), "m8": self.sb([128, NT * 8], F32),
               "ns": self.sb([128, NF], F32), "ns2": self.sb([128, NF], BF16), "nsT": self.sb([16, S], BF16),
               "km_tk": Tk(), "g_tk": Tk(), "nsT_tk": Tk()} for _ in range(2)]

        def load(h):
            d = slots[h % 2]
            sc.dma("sp", d["tk"], d["q"][:], qkT_d[h * 64:(h + 1) * 64, :], writes=[d["tk"]])
            sc.dma("sp", d["tk"], d["k"][:], qkT_d[512 + h * 64:512 + (h + 1) * 64, :], writes=[d["tk"]], join=True)

        load(0)
        for h in range(8):
            if h + 1 < 8:
                load(h + 1)
            d, w = slots[h % 2], wk[h % 2]
            tk = d["tk"]
            sc.op("pool", lambda e: e.memset(w["km"][:], 0.0), writes=[w["km_tk"]])
            sc.op("dve", lambda e: e.tensor_reduce(out=w["km"][:, 0:NBLK], in_=d["k"].rearrange("p (b j) -> p b j", j=256), axis=AX.X, op=ALU.add),
                  reads=[tk], writes=[w["km_tk"]])
            sc.op("act", lambda e: e.activation(out=w["kmb"][:], in_=w["km"][:], func=AF.Copy, scale=1.0 / 256.0),
                  reads=[w["km_tk"]], writes=[w["km_tk"]])
            bg_ = self.bank()
            for t in range(NT):
                sc.op("pe", lambda e: e.matmul(self.ps[bg_][:, t * 16:(t + 1) * 16], lhsT=d["q"][:, t * 128:(t + 1) * 128], rhs=w["kmb"][:],
                                               start=True, stop=True, skip_group_check=True),
                      reads=[tk, w["km_tk"]], writes=[self.pst[bg_]], join=(t > 0))
            sc.op("dve", lambda e: e.tensor_tensor(out=w["gm"][:], in0=self.ps[bg_][:, 0:NF], in1=mot[:], op=ALU.add),
                  reads=[self.pst[bg_], mk_tk], writes=[w["g_tk"]])
            for t in range(NT):
                sc.op("dve", lambda e: e.max(out=w["m8"][:, t * 8:(t + 1) * 8], in_=w["gm"][:, t * 16:(t + 1) * 16]),
                      reads=[w["g_tk"]], writes=[w["g_tk"]])
            for t in range(NT):
                sc.op("dve", lambda e: e.tensor_scalar(out=w["ns"][:, t * 16:(t + 1) * 16], in0=w["gm"][:, t * 16:(t + 1) * 16],
                                                       scalar1=w["m8"][:, t * 8 + 2:t * 8 + 3], scalar2=-NEG, op0=ALU.is_ge, op1=ALU.mult),
                      reads=[w["g_tk"]], writes=[w["g_tk"]])
            sc.op("dve", lambda e: e.tensor_tensor(out=w["ns"][:], in0=w["ns"][:], in1=o1t[:], op=ALU.add),
                  reads=[w["g_tk"], mk_tk], writes=[w["g_tk"]])
            sc.op("dve", lambda e: e.tensor_tensor(out=w["ns2"][:], in0=w["ns"][:], in1=m2t[:], op=ALU.min),
                  reads=[w["g_tk"], mk_tk], writes=[w["g_tk"]])
            for t0 in range(0, NT, 4):
                b2 = self.bank()
                n4 = min(4, NT - t0)
                for t in range(t0, t0 + n4):
                    sc.op("pe", lambda e: e.matmul(self.ps[b2][0:16, (t - t0) * 128:(t - t0 + 1) * 128], lhsT=w["ns2"][:, t * 16:(t + 1) * 16],
                                                   rhs=self.ident[:], start=True, stop=True, skip_group_check=True),
                          reads=[w["g_tk"], self.ident_t], writes=[self.pst[b2]], join=(t > t0))
                sc.op("act", lambda e: e.activation(out=w["nsT"][:, t0 * 128:(t0 + n4) * 128], in_=self.ps[b2][0:16, 0:n4 * 128], func=AF.Copy),
                      reads=[self.pst[b2]], writes=[w["nsT_tk"]], join=(t0 > 0))
            sc.dma("sp", w["nsT_tk"], ns_d[h, :, :], w["nsT"][:], reads=[w["nsT_tk"]])
        self.phase_barrier([w["nsT_tk"] for w in wk] + [d["tk"] for d in slots])

    def phase_moba(self, qkT_d, vp_d, ns_d, eblk_d, y0_d, c, wjobs=None):
        sc, S, NT = self.sc, self.S, self.NT
        self.sb_reset()
        slots = []
        for sl in range(2):
            d = {"q": self.sb([128, S], BF16), "k": self.sb([128, S], BF16), "V": self.sb([128, NT, 65], BF16), "tk": Tk()}
            sc.op("pool", lambda e: e.memset(d["q"][64:128, :], 0.0), writes=[d["tk"]])
            sc.op("pool", lambda e: e.memset(d["k"][64:128, :], 0.0), writes=[d["tk"]], join=True)
            sc.dma("sp", d["tk"], d["k"][64:80, :], eblk_d[:, :], writes=[d["tk"]])
            slots.append(d)

        def load_head(h, sl):
            d = slots[sl]
            tk = d["tk"]
            sc.dma("sp", tk, d["q"][0:64, :], qkT_d[h * 64:(h + 1) * 64, :], writes=[tk])
            sc.dma("sp", tk, d["k"][0:64, :], qkT_d[512 + h * 64:512 + (h + 1) * 64, :], writes=[tk], join=True)
            sc.dma("sp", tk, d["q"][64:80, :], ns_d[h, :, :], writes=[tk], join=True)
            sc.dma("sp", tk, d["V"][:], vp_d[:, h * 65:(h + 1) * 65].rearrange("(t p) d -> p t d", p=128), writes=[tk], join=True)
            return {"q": d["q"], "k": d["k"], "V": d["V"], "tks": [tk]}

        def score_mm(hd, kt, q0, q1, j):
            return hd["k"][:, kt * 128:(kt + 1) * 128], hd["q"][:, q0:q1]

        wtks = self.weight_preconvert(wjobs) if wjobs else []
        out_tks = self.attn_core(8, 4 if NT >= 4 else NT, load_head, score_mm, y0_d, c, bias=False, selmask=False, ycol0=1024)
        self.phase_barrier(out_tks + [d["tk"] for d in slots] + wtks)

    def phase_barrier(self, tks):
        for en in ("pe", "act", "dve", "pool", "sp"):
            self.sc.wait_all(en, tks)
            self.sc.wait_all(en, self.pst)
        self.sc.end_phase()


def host_consts(S=S_FULL):
    i = np.arange(128)
    tri = np.where(i[:, None] > i[None, :], NEG, 0.0).astype(ml_dtypes.bfloat16)
    onehot = np.zeros((16, 16, 128), dtype=ml_dtypes.bfloat16)
    for b in range(16):
        onehot[b, b, :] = 1
    p = np.arange(128)
    dd = p % 64
    jj = dd % 32
    inv = np.power(np.float32(10000.0), -(jj.astype(np.float32)) / np.float32(32)).astype(np.float32)
    ang = (np.arange(S, dtype=np.float32)[None, :] * inv[:, None]).astype(np.float32)
    cosT = np.cos(ang).astype(np.float32)
    sinT = (np.sin(ang) * np.where(dd < 32, -1.0, 1.0)[:, None]).astype(np.float32)
    own = np.arange(16)[:, None]
    blk = np.arange(16)[None, :]
    mo = np.broadcast_to(np.where(blk >= own, -1e30, 0.0).astype(np.float32)[None], (128, 16, 16)).copy()
    m2 = np.broadcast_to(np.where(blk >= own, NEG, 0.0).astype(np.float32)[None], (128, 16, 16)).copy()
    eblk = (np.arange(16)[:, None] == (np.arange(S)[None, :] // 256)).astype(ml_dtypes.bfloat16)
    NT_ = S // 128
    own_t = (np.arange(NT_) // 2)[:, None]
    mot = np.zeros((128, 512), np.float32)
    m2t = np.zeros((128, 512), np.float32)
    mot[:, :NT_ * 16] = np.where(blk >= own_t, -1e30, 0.0).astype(np.float32).reshape(1, -1)
    m2t[:, :NT_ * 16] = np.where(blk > own_t, NEG, 0.0).astype(np.float32).reshape(1, -1)
    o1t = np.zeros((128, 512), np.float32)
    o1t[:, :NT_ * 16] = np.where(blk == own_t, 0.0, NEG).astype(np.float32).reshape(1, -1)
    return {
        "c_mot": mot, "c_m2t": m2t, "c_o1t": o1t,
        "c_eblk": eblk,
        "c_cosT": cosT, "c_sinT": sinT, "c_mo": mo, "c_m2": m2,
        "ident": np.eye(128, dtype=ml_dtypes.bfloat16),
        "c_ident32": np.eye(128, dtype=np.float32),
        "c_triu": (i[:, None] <= i[None, :]).astype(np.float32),
        "c_ones32": np.ones((128, 128), np.float32),
        "c_sel127": np.where(i[:, None] == 127, 1.0, 0.0).astype(np.float32) * np.ones((128, 128), np.float32),
        "c_tri": tri,
        "c_onehot": onehot,
    }


def build(S, phases, dbg=False):
    nc = bass.Bass("TRN2", target_bir_lowering=False)
    EI = "ExternalInput"
    SCR = "ExternalOutput" if dbg else "Internal"
    x_in = nc.dram_tensor("x", [S, D], F32, kind=EI).ap()
    ident_d = nc.dram_tensor("ident", [128, 128], BF16, kind=EI).ap()
    cd = {
        "ident32": (nc.dram_tensor("c_ident32", [128, 128], F32, kind=EI).ap(), [128, 128], F32),
        "triu": (nc.dram_tensor("c_triu", [128, 128], F32, kind=EI).ap(), [128, 128], F32),
        "ones32": (nc.dram_tensor("c_ones32", [128, 128], F32, kind=EI).ap(), [128, 128], F32),
        "sel127": (nc.dram_tensor("c_sel127", [128, 128], F32, kind=EI).ap(), [128, 128], F32),
        "tri": (nc.dram_tensor("c_tri", [128, 128], BF16, kind=EI).ap(), [128, 128], BF16),
        "onehot": (nc.dram_tensor("c_onehot", [16, 16, 128], BF16, kind=EI).ap(), [16, 16, 128], BF16),
    }
    cos_d = nc.dram_tensor("c_cosT", [128, S], F32, kind=EI).ap()
    sin_d = nc.dram_tensor("c_sinT", [128, S], F32, kind=EI).ap()
    mo_d = nc.dram_tensor("c_mo", [128, 16, 16], F32, kind=EI).ap()
    m2_d = nc.dram_tensor("c_m2", [128, 16, 16], F32, kind=EI).ap()
    eblk_d = nc.dram_tensor("c_eblk", [16, S], BF16, kind=EI).ap()
    mot_d = nc.dram_tensor("c_mot", [128, 512], F32, kind=EI).ap()
    m2t_d = nc.dram_tensor("c_m2t", [128, 512], F32, kind=EI).ap()
    o1t_d = nc.dram_tensor("c_o1t", [128, 512], F32, kind=EI).ap()
    nsd = nc.dram_tensor("nsd", [8, 16, S], BF16, kind=SCR).ap()
    norm_mix_even = nc.dram_tensor("norm_mix_even", [1, D], F32, kind=EI).ap()
    w_in_even = nc.dram_tensor("w_in_even", [1, D, 4624], F32, kind=EI).ap()
    conv_w = nc.dram_tensor("conv_w", [1, 4, 2048], F32, kind=EI).ap()
    conv_b = nc.dram_tensor("conv_b", [1, 2048], F32, kind=EI).ap()
    dt_bias = nc.dram_tensor("dt_bias", [1, 16], F32, kind=EI).ap()
    a_log = nc.dram_tensor("a_log", [1, 16], F32, kind=EI).ap()
    d_skip_rep = nc.dram_tensor("d_skip_rep", [1, 1024], F32, kind=EI).ap()
    ssd_gate_norm = nc.dram_tensor("ssd_gate_norm", [1, 1024], F32, kind=EI).ap()
    w_out_even = nc.dram_tensor("w_out_even", [1, 1536, D], F32, kind=EI).ap()
    xbcT = nc.dram_tensor("xbcT", [2048, S], BF16, kind=SCR).ap()
    qkT0 = nc.dram_tensor("qkT0", [1024, S], BF16, kind=SCR).ap()
    zs = nc.dram_tensor("zs", [S, 1024], BF16, kind=SCR).ap()
    vp0 = nc.dram_tensor("vp0", [S, 8 * 65], BF16, kind=SCR).ap()
    RL0 = nc.dram_tensor("RL0", [16, 12, S], BF16, kind=SCR).ap()
    xs0 = nc.dram_tensor("xs0", [S, 1024], BF16, kind=SCR).ap()
    xc0 = nc.dram_tensor("xc0", [S, 1024], BF16, kind=SCR).ap()
    ysc = nc.dram_tensor("ysc", [S, 1024], BF16, kind=SCR).ap()
    y0 = nc.dram_tensor("y0", [S, 1536], BF16, kind=SCR).ap()
    btok = nc.dram_tensor("btok", [S, 512], BF16, kind=SCR).ap()
    awd = nc.dram_tensor("awd", [S, 16], F32, kind=SCR).ap()
    wub = nc.dram_tensor("wub", [2, D, DFF], BF16, kind="Internal").ap()
    wdb = nc.dram_tensor("wdb", [2, DFF, D], BF16, kind="Internal").ap()
    wob0 = nc.dram_tensor("wob0", [1536, D], BF16, kind="Internal").ap()
    wob1 = nc.dram_tensor("wob1", [D, D], BF16, kind="Internal").ap()
    PRECONV = os.environ.get("PRECONV", "1") == "1"
    done_conv = set()
    norm_mlp = nc.dram_tensor("norm_mlp", [2, D], F32, kind=EI).ap()
    w_up = nc.dram_tensor("w_up", [2, D, DFF], F32, kind=EI).ap()
    w_down = nc.dram_tensor("w_down", [2, DFF, D], F32, kind=EI).ap()
    final_norm = nc.dram_tensor("final_norm", [1, D], F32, kind=EI).ap()
    norm_mix_odd = nc.dram_tensor("norm_mix_odd", [1, D], F32, kind=EI).ap()
    w_in_odd = nc.dram_tensor("w_in_odd", [1, D, 3088], F32, kind=EI).ap()
    fgate_bias = nc.dram_tensor("fgate_bias", [1, 16], F32, kind=EI).ap()
    w_out_odd = nc.dram_tensor("w_out_odd", [1, D, D], F32, kind=EI).ap()
    out = nc.dram_tensor("out", [S, D], F32, kind="ExternalOutput").ap()
    xs = nc.dram_tensor("xs", [S, D], F32, kind=SCR).ap()
    qkT = nc.dram_tensor("qkT", [2048, S], BF16, kind=SCR).ap()
    vp = nc.dram_tensor("vp", [S, 16 * 65], BF16, kind=SCR).ap()
    RL = nc.dram_tensor("RL", [16, 12, S], BF16, kind=SCR).ap()
    y1 = nc.dram_tensor("y1", [S, D], BF16, kind=SCR).ap()
    kb = KB(nc, S)
    kb.setup_consts(ident_d, cd)
    c = kb.c
    cur = x_in
    for ph in phases:
        if ph == "mlp0":
            kb.phase_mlp(cur, norm_mlp[0:1, :], w_up[0], w_down[0], xout_dram=xs, wbf=(wub[0], wdb[0]) if 0 in done_conv else None)
            cur = xs
        elif ph == "mlp1":
            kb.phase_mlp(cur, norm_mlp[1:2, :], w_up[1], w_down[1], xout_dram=xs, wbf=(wub[1], wdb[1]) if 1 in done_conv else None)
            cur = xs
        elif ph == "l0":
            LS = int(os.environ.get("L0_STOP", "9"))
            kb.phase_l0_proj(cur, norm_mix_even[0:1, :], w_in_even[0], conv_w[0], conv_b, dt_bias[0:1, :], a_log[0:1, :], cos_d, sin_d,
                             xbcT, qkT0, zs, vp0, RL0, c, aw_d=awd)
            if LS >= 2:
                kb.phase_ssd_prep(xbcT, xs0, xc0, btok_d=btok)
            if LS >= 3:
                if os.environ.get("SSD_QUAD", "0") == "1":
                    kb.phase_ssd_attn(xbcT, xc0, RL0, ysc, c)
                else:
                    kb.phase_ssd_chunk(xbcT, btok, xc0, RL0, awd, ysc, c)
            if LS >= 4:
                kb.phase_ssd_post(ysc, xs0, zs, d_skip_rep[0:1, :], ssd_gate_norm[0:1, :], y0)
            if LS >= 5:
                kb.phase_moba_gate(qkT0, mot_d, m2t_d, o1t_d, nsd, c)
                wj = [(w_out_even[0], wob0, 1024)] if PRECONV else []
                wj += [(w_up[0], wub[0], 2048), (w_down[0], wdb[0], 1024)] if (PRECONV and "mlp0" in phases) else []
                kb.phase_moba(qkT0, vp0, nsd, eblk_d, y0, c, wjobs=wj)
                if PRECONV:
                    done_conv.add("o0")
                if PRECONV and "mlp0" in phases:
                    done_conv.add(0)
            if LS >= 6:
                kb.phase_out_proj(cur, y0, w_out_even[0], 12, xout_dram=xs, wbf=wob0 if "o0" in done_conv else None)
                cur = xs
        elif ph == "fox":
            kb.phase_fox_proj(cur, norm_mix_odd[0:1, :], w_in_odd[0], fgate_bias[0:1, :], qkT, vp, RL, c)
            FS = int(os.environ.get("FOX_STOP", "3"))
            if FS >= 2:
                wj = [(w_out_odd[0], wob1, 1024)] if PRECONV else []
                wj += [(w_up[1], wub[1], 2048), (w_down[1], wdb[1], 1024)] if (PRECONV and "mlp1" in phases) else []
                kb.phase_fox_attn(qkT, vp, RL, y1, c, wjobs=wj)
                if PRECONV:
                    done_conv.add("o1")
                if PRECONV and "mlp1" in phases:
                    done_conv.add(1)
            if FS >= 3:
                kb.phase_out_proj(cur, y1, w_out_odd[0], 8, xout_dram=xs, wbf=wob1 if "o1" in done_conv else None)
                cur = xs
        elif ph == "final":
            kb.phase_final_norm(cur, final_norm[0:1, :], out)
    print("instructions:", kb.sc.nins, "waits:", kb.sc.nwait, "sems left:", len(kb.sc.pool))
    return nc


PHASES = ["l0", "mlp0", "fox", "mlp1", "final"]
_NC_CACHE = {}


def kernel(**inputs):
    S = S_FULL
    x = np.ascontiguousarray(np.asarray(inputs["x"], dtype=np.float32))
    B = x.shape[0]
    if "nc" not in _NC_CACHE:
        _NC_CACHE["nc"] = build(S, PHASES)
    nc = _NC_CACHE["nc"]
    shared = dict(host_consts(S))
    f = lambda k: np.ascontiguousarray(np.asarray(inputs[k], dtype=np.float32))
    for k in ("norm_mix_even", "w_in_even", "conv_w", "conv_b", "dt_bias", "a_log", "ssd_gate_norm", "w_out_even",
              "norm_mix_odd", "w_in_odd", "fgate_bias", "w_out_odd", "norm_mlp", "w_up", "w_down"):
        shared[k] = f(k)
    shared["final_norm"] = f("final_norm").reshape(1, D)
    shared["d_skip_rep"] = np.ascontiguousarray(np.repeat(f("d_skip"), 64, axis=1))
    in_maps = []
    for b in range(B):
        m = dict(shared)
        m["x"] = x[b]
        in_maps.append(m)
    res = run_bass_kernel_spmd(nc, in_maps, core_ids=list(range(B)))
    return np.stack([np.asarray(r["out"], dtype=np.float32) for r in res.results], axis=0)
```

```python
import math
import os
import numpy as np
import ml_dtypes
import concourse.bass as bass
import concourse.mybir as mybir
from concourse.bass_utils import run_bass_kernel_spmd

F32 = mybir.dt.float32
BF16 = mybir.dt.bfloat16
ALU = mybir.AluOpType
AF = mybir.ActivationFunctionType
AX = mybir.AxisListType

D = 1024
S_FULL = 4096
DFF = 4096
EPS = 1e-5
NEG = -30000.0


class Tk:
    __slots__ = ("w", "r", "name", "excl")

    def __init__(self, name="", excl=False):
        self.w = {}
        self.r = {}
        self.name = name
        self.excl = excl


class Sched:
    CH = 24000

    def __init__(self, nc):
        self.nc = nc
        self.h = {"pe": nc.tensor, "act": nc.scalar, "dve": nc.vector, "pool": nc.gpsimd, "sp": nc.sync}
        self.sems = {k: [] for k in self.h}
        self.cnt = {k: 0 for k in self.h}
        self.seen = {k: {} for k in self.h}
        self.pool = [nc.alloc_semaphore(f"s{i}") for i in range(96)]
        self.dma_free = []
        self.dma_sems = {}
        self.nwait = 0
        self.nins = 0

    def _next_tok(self, en):
        i = self.cnt[en]
        if i % self.CH == 0:
            self.sems[en].append(self.pool.pop())
        self.cnt[en] = i + 1
        return (self.sems[en][-1], i % self.CH + 1, en)

    def _dma_ent(self, key):
        if key not in self.dma_sems:
            if self.dma_free:
                self.dma_sems[key] = self.dma_free.pop()
            else:
                self.dma_sems[key] = [self.pool.pop(), 0]
        return self.dma_sems[key]

    def end_phase(self):
        for ent in self.dma_sems.values():
            if ent[1] < 20000:
                self.dma_free.append(ent)
        self.dma_sems = {}

    def _wait(self, en, tok):
        sem, val, src = tok
        if src == en and en == "pe":
            return
        sid = id(sem)
        if self.seen[en].get(sid, 0) >= val:
            return
        self.h[en].wait_ge(sem, val)
        self.seen[en][sid] = val
        self.nwait += 1

    def _deps(self, en, reads, writes, join):
        for t in reads:
            for tok in t.w.values():
                self._wait(en, tok)
            if t.excl:
                for tok in t.r.values():
                    if tok[2] != en:
                        self._wait(en, tok)
        for t in writes:
            if not join:
                for tok in t.w.values():
                    self._wait(en, tok)
            for tok in t.r.values():
                self._wait(en, tok)

    def _record(self, tok, reads, writes, join):
        sid = id(tok[0])
        for t in reads:
            t.r[sid] = tok
        for t in writes:
            if join:
                t.w[sid] = tok
            else:
                t.w = {sid: tok}
                t.r = {}

    def op(self, en, fn, reads=(), writes=(), join=False):
        self._deps(en, reads, writes, join)
        ins = fn(self.h[en])
        tok = self._next_tok(en)
        ins.then_inc(tok[0], 1)
        self._record(tok, reads, writes, join)
        self.nins += 1
        return tok

    def dma(self, en, key, out, in_, reads=(), writes=(), join=False, **kw):
        self._deps(en, reads, writes, join)
        ent = self._dma_ent(key)
        ins = self.h[en].dma_start(out=out, in_=in_, **kw)
        ent[1] += 16
        ins.then_inc(ent[0], 16)
        tok = (ent[0], ent[1], "dma")
        self._record(tok, reads, writes, join)
        self.nins += 1
        return tok

    def wait_all(self, en, tks):
        for t in tks:
            for tok in list(t.w.values()) + list(t.r.values()):
                self._wait(en, tok)


class KB:
    def __init__(self, nc, S):
        self.nc = nc
        self.S = S
        self.NT = S // 128
        self.sc = Sched(nc)
        self.uid = 0
        self.ps = [nc.alloc_psum_tensor(f"ps{i}", [128, 512], F32) for i in range(8)]
        self.pst = [Tk(f"ps{i}", excl=True) for i in range(8)]
        self.ps_rr = 0
        self.SB_BYTES = 207 * 1024
        self.big = nc.alloc_sbuf_tensor("big", [128, self.SB_BYTES // 4], F32)
        self.sb_off = 0
        self.sb_base = 0

    def sb(self, shape, dt, name=None):
        esz = 4 if dt == F32 else 2
        n = 1
        for d_ in shape[1:]:
            n *= d_
        nbytes = (n * esz + 31) // 32 * 32
        off = self.sb_off
        self.sb_off += nbytes
        assert self.sb_off <= self.SB_BYTES, f"SBUF overflow {self.sb_off}"
        ap = self.big[0:shape[0], off // 4:(off + nbytes) // 4]
        if dt != F32:
            ap = ap.bitcast(dt)
        ap = ap[:, 0:n]
        if len(shape) == 3:
            ap = ap.rearrange("p (a b) -> p a b", a=shape[1])
        elif len(shape) == 4:
            ap = ap.rearrange("p (a b c) -> p a b c", a=shape[1], b=shape[2])
        return ap

    def sb_reset(self):
        self.sb_off = self.sb_base

    def bank(self):
        i = self.ps_rr
        self.ps_rr = (i + 1) % 8
        return i

    def setup_consts(self, ident_d, consts=None):
        nc, sc = self.nc, self.sc
        self.ident = self.sb([128, 128], BF16, "ident")
        self.ident_t = Tk("ident")
        sc.dma("sp", "const_ident", self.ident[:], ident_d[:, :], writes=[self.ident_t])
        c = {"tk": Tk()}
        for nm, (ap_d, shape, dt) in (consts or {}).items():
            c[nm] = self.sb(shape, dt)
            sc.dma("sp", "const", c[nm][:], ap_d, writes=[c["tk"]], join=True)
        self.c = c
        self.sb_base = self.sb_off

    def load_weight(self, w_sb, w_tk, w_dram, K, N, stage, stage_tk, col_chunk=2048, dram_col0=0, sb_col0=0):
        sc = self.sc
        kt = K // 128
        i = 0
        for k in range(kt):
            for c0 in range(0, N, col_chunk):
                cw = min(col_chunk, N - c0)
                st, stk = stage[i % len(stage)], stage_tk[i % len(stage)]
                sc.dma("sp", stk, st[:, 0:cw],
                       w_dram[k * 128:(k + 1) * 128, dram_col0 + c0:dram_col0 + c0 + cw], writes=[stk])
                en = "pool" if i % 2 == 0 else "act"
                if en == "pool":
                    sc.op("pool", lambda e, st=st, k=k, c0=c0, cw=cw: e.tensor_copy(
                        out=w_sb[:, k, sb_col0 + c0:sb_col0 + c0 + cw], in_=st[:, 0:cw]),
                        reads=[stk], writes=[w_tk], join=True)
                else:
                    sc.op("act", lambda e, st=st, k=k, c0=c0, cw=cw: e.activation(
                        out=w_sb[:, k, sb_col0 + c0:sb_col0 + c0 + cw], in_=st[:, 0:cw], func=AF.Copy),
                        reads=[stk], writes=[w_tk], join=True)
                i += 1

    def rstd(self, st, st_tk):
        sc = self.sc
        sc.op("dve", lambda e: e.tensor_scalar_add(out=st[:, 1:2], in0=st[:, 0:1], scalar1=EPS), reads=[st_tk], writes=[st_tk])
        sc.op("act", lambda e: e.activation(out=st[:, 3:4], in_=st[:, 1:2], func=AF.Sqrt), reads=[st_tk], writes=[st_tk])
        sc.op("dve", lambda e: e.reciprocal(out=st[:, 2:3], in_=st[:, 3:4]), reads=[st_tk], writes=[st_tk])

    def norm_tile(self, x_dram_rows, g_sb, g_tk, bufs, hT, hT_tk, col0, idx, preloaded=False):
        sc = self.sc
        xin, xin_tk = bufs["xin"][idx % len(bufs["xin"])]
        hb, hb_tk = bufs["hb"][idx % len(bufs["hb"])]
        st, st_tk = bufs["st"][idx % len(bufs["st"])]
        junk, junk_tk = bufs["junk"]
        if not preloaded:
            sc.dma("sp", xin_tk, xin[:], x_dram_rows, writes=[xin_tk])
        sc.op("act", lambda e: e.activation(out=junk[:], in_=xin[:], func=AF.Square, scale=float(1.0 / math.sqrt(D)), accum_out=st[:, 0:1]),
              reads=[xin_tk], writes=[junk_tk, st_tk])
        self.rstd(st, st_tk)
        sc.op("dve", lambda e: e.scalar_tensor_tensor(out=hb[:], in0=xin[:], scalar=st[:, 2:3], in1=g_sb[:],
                                                      op0=ALU.mult, op1=ALU.mult),
              reads=[xin_tk, st_tk, g_tk], writes=[hb_tk])
        b = self.bank()
        pt = self.ps[b][:].bitcast(BF16)
        for k in range(8):
            sc.op("pe", lambda e, k=k: e.transpose(out=pt[:, k * 128:(k + 1) * 128], in_=hb[:, k * 128:(k + 1) * 128],
                                                   identity=self.ident[:]),
                  reads=[hb_tk, self.ident_t], writes=[self.pst[b]], join=(k > 0))
        if idx % 2 == 0:
            sc.op("act", lambda e: e.activation(out=hT[:, 0:8, col0:col0 + 128],
                                                in_=pt.rearrange("p (k t) -> p k t", k=8), func=AF.Copy),
                  reads=[self.pst[b]], writes=[hT_tk], join=True)
        else:
            sc.op("dve", lambda e: e.tensor_copy(out=hT[:, 0:8, col0:col0 + 128], in_=pt.rearrange("p (k t) -> p k t", k=8)),
                  reads=[self.pst[b]], writes=[hT_tk], join=True)

    def phase_final_norm(self, x_dram, g_dram, out_dram):
        nc, sc = self.nc, self.sc
        if True:
            self.sb_reset()
            g_sb = self.sb([128, D], F32, "g")
            g_tk = Tk()
            sc.dma("sp", "gload", g_sb[:], g_dram.partition_broadcast(128), writes=[g_tk])
            NB = 3
            xin = [(self.sb([128, D], F32, "xin"), Tk()) for _ in range(NB)]
            xo = [(self.sb([128, D], F32, "xo"), Tk()) for _ in range(NB)]
            st = [(self.sb([128, 4], F32, "st"), Tk()) for _ in range(NB)]
            junk, junk_tk = self.sb([128, D], F32, "junk"), Tk()
            for t in range(self.NT):
                xi, xi_tk = xin[t % NB]
                xot, xo_tk = xo[t % NB]
                s_, s_tk = st[t % NB]
                if t == 0:
                    for tt in range(min(2, self.NT)):
                        sc.dma("sp", xin[tt % NB][1], xin[tt % NB][0][:], x_dram[tt * 128:(tt + 1) * 128, :], writes=[xin[tt % NB][1]])
                if t + 2 < self.NT:
                    sc.dma("sp", xin[(t + 2) % NB][1], xin[(t + 2) % NB][0][:], x_dram[(t + 2) * 128:(t + 3) * 128, :], writes=[xin[(t + 2) % NB][1]])
                sc.op("act", lambda e: e.activation(out=junk[:], in_=xi[:], func=AF.Square, scale=float(1.0 / math.sqrt(D)), accum_out=s_[:, 0:1]),
                      reads=[xi_tk], writes=[junk_tk, s_tk])
                self.rstd(s_, s_tk)
                sc.op("dve", lambda e: e.scalar_tensor_tensor(out=xot[:], in0=xi[:], scalar=s_[:, 2:3], in1=g_sb[:],
                                                              op0=ALU.mult, op1=ALU.mult),
                      reads=[xi_tk, s_tk, g_tk], writes=[xo_tk])
                sc.dma("sp", xo_tk, out_dram[t * 128:(t + 1) * 128, :], xot[:], reads=[xo_tk])
            self.phase_barrier([tk for _, tk in xo])

    def phase_mlp(self, x_dram, g_dram, wup_dram, wdn_dram, xout_dram=None, wbf=None):
        nc, sc = self.nc, self.sc
        xout_dram = x_dram if xout_dram is None else xout_dram
        if True:
            self.sb_reset()
            g_sb, g_tk = self.sb([128, D], F32, "g"), Tk()
            sc.dma("sp", "gload", g_sb[:], g_dram.partition_broadcast(128), writes=[g_tk])
            wu, wu_tk = self.sb([128, 8, DFF], BF16, "wu"), Tk()
            wd, wd_tk = self.sb([128, 32, D], BF16, "wd"), Tk()
            hmid, hmid_tk = self.sb([128, 32, 512], BF16, "hmid"), [Tk() for _ in range(32)]
            hm32 = hmid.rearrange("p a b -> p (a b)").bitcast(F32)
            stage = [hm32[:, i * 2048:(i + 1) * 2048] for i in range(4)]
            stage_tk = [Tk() for _ in range(4)]
            if wbf is None:
                self.load_weight(wu, wu_tk, wup_dram, D, DFF, stage, stage_tk)
                self.load_weight(wd, wd_tk, wdn_dram, DFF, D, stage, stage_tk, col_chunk=1024)
                for tk in hmid_tk:
                    for s in stage_tk:
                        tk.w.update(s.w)
                        tk.r.update(s.r)
            else:
                for k in range(8):
                    sc.dma("sp", wu_tk, wu[:, k, :], wbf[0][k * 128:(k + 1) * 128, :], writes=[wu_tk], join=(k > 0))
                for k in range(32):
                    sc.dma("sp", wd_tk, wd[:, k, :], wbf[1][k * 128:(k + 1) * 128, :], writes=[wd_tk], join=(k > 0))
            hT = [(self.sb([128, 8, 512], BF16, "hT"), Tk()) for _ in range(1)]
            import os
            STOP = int(os.environ.get("MLP_STOP", "9"))
            bufs = {
                "xin": [(self.sb([128, D], F32, "xin"), Tk()) for _ in range(2)],
                "hb": [(self.sb([128, D], BF16, "hb"), Tk()) for _ in range(2)],
                "st": [(self.sb([128, 4], F32, "st"), Tk()) for _ in range(2)],
                "junk": (self.sb([128, D], F32, "junk"), Tk()),
            }
            xres = [(self.sb([128, 512], F32, "xres"), Tk()) for _ in range(4)]
            NG = self.S // 512 if STOP >= 1 else 0
            for g in range(NG):
                hTg, hTg_tk = hT[0]
                for t in range(4):
                    r0 = g * 512 + t * 128
                    self.norm_tile(x_dram[r0:r0 + 128, :], g_sb, g_tk, bufs, hTg, hTg_tk, t * 128, g * 4 + t,
                                   preloaded=(g > 0 and t < 2))
                if STOP < 2:
                    continue
                for f in range(32):
                    b = self.bank()
                    for k in range(8):
                        sc.op("pe", lambda e, k=k, f=f, b=b: e.matmul(self.ps[b][:], lhsT=wu[:, k, f * 128:(f + 1) * 128],
                                                                       rhs=hTg[:, k, :], start=(k == 0), stop=(k == 7)),
                              reads=[wu_tk, hTg_tk], writes=[self.pst[b]], join=(k > 0))
                    sc.op("act", lambda e, f=f, b=b: e.activation(out=hmid[:, f, :], in_=self.ps[b][:], func=AF.Relu),
                          reads=[self.pst[b]], writes=[hmid_tk[f]])
                    sc.op("dve" if f % 2 == 0 else "pool", lambda e, f=f: e.tensor_tensor(
                        out=hmid[:, f, :], in0=hmid[:, f, :], in1=hmid[:, f, :], op=ALU.mult),
                        reads=[hmid_tk[f]], writes=[hmid_tk[f]])
                if STOP < 3:
                    continue
                if g + 1 < NG:
                    for t in range(2):
                        r1 = (g + 1) * 512 + t * 128
                        xi_, xi_tk_ = bufs["xin"][((g + 1) * 4 + t) % len(bufs["xin"])]
                        sc.dma("sp", xi_tk_, xi_[:], x_dram[r1:r1 + 128, :], writes=[xi_tk_])

                def xr_load(p):
                    t_, c_ = p // 2, p % 2
                    xr_, xr_tk_ = xres[p % 4]
                    sc.dma("sp", xr_tk_, xr_[:], x_dram[g * 512 + t_ * 128:g * 512 + (t_ + 1) * 128, c_ * 512:(c_ + 1) * 512], writes=[xr_tk_])

                xr_load(0)
                for t in range(4):
                    r0 = g * 512 + t * 128
                    for c in range(2):
                        xr, xr_tk = xres[(t * 2 + c) % 4]
                        if t * 2 + c + 1 < 8:
                            xr_load(t * 2 + c + 1)
                        b = self.bank()
                        for f in range(32):
                            sc.op("pe", lambda e, f=f, t=t, c=c, b=b: e.matmul(
                                self.ps[b][:], lhsT=hmid[:, f, t * 128:(t + 1) * 128], rhs=wd[:, f, c * 512:(c + 1) * 512],
                                start=(f == 0), stop=(f == 31)),
                                reads=[wd_tk, hmid_tk[f]], writes=[self.pst[b]], join=(f > 0))
                        sc.op("dve", lambda e, b=b, xr=xr: e.tensor_tensor(out=xr[:], in0=self.ps[b][:], in1=xr[:], op=ALU.add),
                              reads=[self.pst[b], xr_tk], writes=[xr_tk])
                        sc.dma("sp", xr_tk, xout_dram[r0:r0 + 128, c * 512:(c + 1) * 512], xr[:], reads=[xr_tk])
            self.phase_barrier([tk for _, tk in xres])

    def norm_all(self, x_dram, g_dram):
        sc = self.sc
        g_sb, g_tk = self.sb([128, D], F32, "g"), Tk()
        sc.dma("sp", g_tk, g_sb[:], g_dram.partition_broadcast(128), writes=[g_tk])
        hT, hT_tk = self.sb([128, 8, self.S], BF16, "hTall"), Tk()
        save = self.sb_off
        bufs = {
            "xin": [(self.sb([128, D], F32), Tk()) for _ in range(2)],
            "hb": [(self.sb([128, D], BF16), Tk()) for _ in range(2)],
            "st": [(self.sb([128, 4], F32), Tk()) for _ in range(2)],
            "junk": (self.sb([128, D], F32), Tk()),
        }
        for t in range(self.NT):
            self.norm_tile(x_dram[t * 128:(t + 1) * 128, :], g_sb, g_tk, bufs, hT, hT_tk, t * 128, t)
        self.norm_bufs_tks = [tk for _, tk in bufs["xin"]] + [tk for _, tk in bufs["hb"]] + [bufs["junk"][1]]
        return hT, hT_tk, save

    def load_wcols(self, w_dram, c0, ncols, wt, wt_tk, stg, stg_tk, perm_heads=False):
        sc = self.sc
        if not perm_heads:
            sc.dma("sp", stg_tk, stg[:, :, 0:ncols], w_dram[:, c0:c0 + ncols].rearrange("(k p) c -> p k c", p=128),
                   writes=[stg_tk])
        else:
            first = True
            for hh in range(ncols // 64):
                for half in range(2):
                    src = w_dram[:, c0 + hh * 64 + (1 - half) * 32:c0 + hh * 64 + (1 - half) * 32 + 32]
                    sc.dma("sp", stg_tk, stg[:, :, hh * 64 + half * 32:hh * 64 + half * 32 + 32],
                           src.rearrange("(k p) c -> p k c", p=128), writes=[stg_tk], join=not first)
                    first = False
        sc.op("pool", lambda e: e.tensor_copy(out=wt[:, :, 0:ncols], in_=stg[:, :, 0:ncols]), reads=[stg_tk], writes=[wt_tk])

    def phase_fox_proj(self, x_dram, g_dram, w_in, fb_dram, qkT_d, vp_d, RL_d, c):
        sc, S, NT = self.sc, self.S, self.NT
        self.sb_reset()
        nl, nl_tk = self.sb([128, NT, 16], F32), Tk()
        hT, hT_tk, _ = self.norm_all(x_dram, g_dram)
        NG = S // 512
        wts = [(self.sb([128, 8, 128], BF16), Tk()) for _ in range(2)]
        stgs = [(self.sb([128, 8, 128], F32), Tk()) for _ in range(2)]
        rows = [(self.sb([128, S], BF16), Tk()) for _ in range(2)]
        out_tks = []
        self.load_wcols(w_in, 0, 128, wts[0][0], wts[0][1], stgs[0][0], stgs[0][1])
        for f in range(16):
            wt, wt_tk = wts[f % 2]
            row, row_tk = rows[f % 2]
            if f + 1 < 16:
                self.load_wcols(w_in, (f + 1) * 128, 128, wts[(f + 1) % 2][0], wts[(f + 1) % 2][1], stgs[(f + 1) % 2][0], stgs[(f + 1) % 2][1])
            for tg in range(NG):
                b = self.bank()
                for k in range(8):
                    sc.op("pe", lambda e: e.matmul(self.ps[b][:], lhsT=wt[:, k, :], rhs=hT[:, k, tg * 512:(tg + 1) * 512],
                                                   start=(k == 0), stop=(k == 7)),
                          reads=[wt_tk, hT_tk], writes=[self.pst[b]], join=(k > 0))
                sc.op("act", lambda e: e.activation(out=row[:, tg * 512:(tg + 1) * 512], in_=self.ps[b][:], func=AF.Copy,
                                                    scale=(0.125 if f < 8 else 1.0)),
                      reads=[self.pst[b]], writes=[row_tk], join=(tg > 0))
            sc.dma("sp", row_tk, qkT_d[f * 128:(f + 1) * 128, :], row[:], reads=[row_tk])
            out_tks.append(row_tk)
        wv, wv_tk = self.sb([128, 8, 1024], BF16), Tk()
        stv = [(self.sb([128, 8, 512], F32), Tk()) for _ in range(1)]
        for cc in range(2):
            sc.dma("sp", stv[0][1], stv[0][0][:], w_in[:, 2048 + cc * 512:2048 + (cc + 1) * 512].rearrange("(k p) c -> p k c", p=128),
                   writes=[stv[0][1]])
            sc.op("pool", lambda e: e.tensor_copy(out=wv[:, :, cc * 512:(cc + 1) * 512], in_=stv[0][0][:]),
                  reads=[stv[0][1]], writes=[wv_tk], join=(cc > 0))
        vts = [(self.sb([128, 16, 65], BF16), Tk()) for _ in range(2)]
        for vt, vt_tk in vts:
            sc.op("pool", lambda e: e.memset(vt[:], 1.0), writes=[vt_tk])
        for t in range(NT):
            vt, vt_tk = vts[t % 2]
            for cc in range(2):
                b = self.bank()
                for k in range(8):
                    sc.op("pe", lambda e: e.matmul(self.ps[b][:], lhsT=hT[:, k, t * 128:(t + 1) * 128], rhs=wv[:, k, cc * 512:(cc + 1) * 512],
                                                   start=(k == 0), stop=(k == 7)),
                          reads=[wv_tk, hT_tk], writes=[self.pst[b]], join=(k > 0))
                sc.op("act", lambda e: e.activation(out=vt[:, cc * 8:(cc + 1) * 8, 0:64],
                                                    in_=self.ps[b][:].rearrange("p (h d) -> p h d", h=8), func=AF.Copy),
                      reads=[self.pst[b]], writes=[vt_tk], join=(cc > 0))
            sc.dma("sp", vt_tk, vp_d[t * 128:(t + 1) * 128, :], vt.rearrange("p h d -> p (h d)"), reads=[vt_tk])
            out_tks.append(vt_tk)
        wf, wf_tk = self.sb([128, 8, 16], BF16), Tk()
        stf, stf_tk = self.sb([128, 8, 16], F32), Tk()
        self.load_wcols(w_in, 3072, 16, wf, wf_tk, stf, stf_tk)
        fb, fb_tk = self.sb([128, 16], F32), Tk()
        sc.dma("sp", fb_tk, fb[:], fb_dram.partition_broadcast(128), writes=[fb_tk])
        tmp, tmp_tk = self.sb([128, NT, 16], F32), Tk()
        for t in range(NT):
            b = self.bank()
            for k in range(8):
                sc.op("pe", lambda e: e.matmul(self.ps[b][:, 0:16], lhsT=hT[:, k, t * 128:(t + 1) * 128], rhs=wf[:, k, :],
                                               start=(k == 0), stop=(k == 7)),
                      reads=[wf_tk, hT_tk], writes=[self.pst[b]], join=(k > 0))
            sc.op("dve", lambda e: e.tensor_tensor(out=tmp[:, t, :], in0=self.ps[b][:, 0:16], in1=fb[:], op=ALU.add),
                  reads=[self.pst[b], fb_tk], writes=[tmp_tk], join=True)
        sc.op("act", lambda e: e.activation(out=tmp[:], in_=tmp[:], func=AF.Exp, scale=-1.0), reads=[tmp_tk], writes=[tmp_tk])
        sc.op("dve", lambda e: e.tensor_scalar_add(out=tmp[:], in0=tmp[:], scalar1=1.0), reads=[tmp_tk], writes=[tmp_tk])
        sc.op("act", lambda e: e.activation(out=nl[:], in_=tmp[:], func=AF.Ln), reads=[tmp_tk], writes=[nl_tk])
        self.phase_barrier(out_tks + [nl_tk])
        self.sb_reset()
        nl, nl_tk = self.sb([128, NT, 16], F32), Tk()
        out_tks = []
        self.cumsum_rows(nl, nl_tk, RL_d, c, out_tks)
        self.phase_barrier(out_tks)

    def cumsum_rows(self, nl, nl_tk, RL_d, c, out_tks, aw_d=None):
        sc, S, NT = self.sc, self.S, self.NT
        runs, runs_tk = self.sb([128, NT, 16], F32), Tk()
        sc.op("dve", lambda e: e.memset(runs[:, 0, :], 0.0), writes=[runs_tk])
        for t in range(1, NT):
            sc.op("dve", lambda e: e.tensor_tensor(out=runs[:, t, :], in0=runs[:, t - 1, :], in1=nl[:, t - 1, :], op=ALU.add),
                  reads=[nl_tk, runs_tk], writes=[runs_tk])
        cT, cT_tk = self.sb([16, S], F32), Tk()
        cn, cn_tk = self.sb([128, 32], F32), [Tk() for _ in range(2)]
        cn2 = self.sb([128, 32], F32)
        cns = [cn, cn2]
        for t in range(NT):
            b = self.bank()
            if aw_d is not None:
                sc.op("pe", lambda e: e.matmul(self.ps[b][:, 16:32], lhsT=c["triu"][:], rhs=nl[:, t, :], start=True, stop=True),
                      reads=[nl_tk, c["tk"]], writes=[self.pst[b]])
            sc.op("pe", lambda e: e.matmul(self.ps[b][:, 0:16], lhsT=c["triu"][:], rhs=nl[:, t, :], start=True, stop=False),
                  reads=[nl_tk, c["tk"]], writes=[self.pst[b]], join=(aw_d is not None))
            sc.op("pe", lambda e: e.matmul(self.ps[b][:, 0:16], lhsT=c["ones32"][:], rhs=runs[:, t, :], start=False, stop=True),
                  reads=[runs_tk, c["tk"]], writes=[self.pst[b]], join=True)
            cur, cur_tk = cns[t % 2], cn_tk[t % 2]
            ncp = 32 if aw_d is not None else 16
            sc.op("dve", lambda e: e.tensor_copy(out=cur[:, 0:ncp], in_=self.ps[b][:, 0:ncp]), reads=[self.pst[b]], writes=[cur_tk])
            if aw_d is not None:
                sc.dma("sp", cur_tk, aw_d[t * 128:(t + 1) * 128, :], cur[:, 16:32], reads=[cur_tk])
            b2 = self.bank()
            sc.op("pe", lambda e: e.transpose(out=self.ps[b2][0:16, 0:128], in_=cur[:, 0:16], identity=c["ident32"][:]),
                  reads=[cur_tk, c["tk"]], writes=[self.pst[b2]])
            sc.op("act", lambda e: e.activation(out=cT[:, t * 128:(t + 1) * 128], in_=self.ps[b2][0:16, 0:128], func=AF.Copy),
                  reads=[self.pst[b2]], writes=[cT_tk], join=True)
        pb = [(self.sb([16, S], BF16), Tk()) for _ in range(3)]
        nb = [(self.sb([16, S], BF16), Tk()) for _ in range(3)]
        f32a, f32a_tk = self.sb([16, S], F32), Tk()
        rem, rem_tk = cT, cT_tk
        for i in range(3):
            p_, p_tk = pb[i]
            n_, n_tk = nb[i]
            sc.op("dve", lambda e: e.tensor_copy(out=p_[:], in_=rem[:]), reads=[rem_tk], writes=[p_tk])
            sc.op("act", lambda e: e.activation(out=n_[:], in_=p_[:], func=AF.Copy, scale=-1.0), reads=[p_tk], writes=[n_tk])
            if i < 2:
                sc.op("pool", lambda e: e.tensor_copy(out=f32a[:], in_=p_[:]), reads=[p_tk], writes=[f32a_tk])
                sc.op("dve", lambda e: e.tensor_tensor(out=rem[:], in0=rem[:], in1=f32a[:], op=ALU.subtract),
                      reads=[rem_tk, f32a_tk], writes=[rem_tk])
        onesb, onesb_tk = self.sb([16, S], BF16), Tk()
        sc.op("pool", lambda e: e.memset(onesb[:], 1.0), writes=[onesb_tk])
        for i in range(3):
            sc.dma("sp", nb[i][1], RL_d[:, i, :], nb[i][0][:], reads=[nb[i][1]])
            sc.dma("sp", pb[i][1], RL_d[:, 9 + i, :], pb[i][0][:], reads=[pb[i][1]])
            sc.dma("sp", onesb_tk, RL_d[:, 3 + i, :], onesb[:], reads=[onesb_tk])
            sc.dma("sp", onesb_tk, RL_d[:, 6 + i, :], onesb[:], reads=[onesb_tk])
        out_tks += [onesb_tk] + [tk for _, tk in pb] + [tk for _, tk in nb] + cn_tk

    def attn_core(self, H, GQ, load_head, score_mm, y_d, c, bias=True, selmask=False, linear=False, VW=65, ycol0=0):
        sc, S, NT = self.sc, self.S, self.NT
        NGR = NT // GQ
        W = GQ * 128
        NPT = 4
        pts = [(self.sb([128, W], BF16), Tk()) for _ in range(NPT)]
        ets = [(self.sb([128, W], F32), Tk()) for _ in range(3)] if linear else None
        yos = [(self.sb([128, GQ, 64], BF16), Tk()) for _ in range(2)]
        recs = [(self.sb([128, 4], F32), Tk()) for _ in range(2)]
        obanks2 = [6, 7] if linear else [4, 5]
        sbanks = [0, 1, 2, 3, 4, 5] if linear else [0, 1, 2, 3]
        NSB = len(sbanks)
        LAG = 2
        st = {"sb": 0, "et": 0}
        out_tks = [tk for _, tk in yos]
        heads = {}

        def get_head(h):
            if h not in heads:
                heads[h] = load_head(h, h % 2)
            return heads[h]

        blocks = [(h, G, kt) for h in range(H) for G in range(NGR) for kt in range((G + 1) * GQ)]

        def stage_a(i):
            h, G, kt = blocks[i]
            hd = get_head(h)
            htks = hd["tks"]
            j = kt - G * GQ
            c0 = max(j, 0) * 128
            q0 = G * W + c0
            q1 = (G + 1) * W
            b = sbanks[st["sb"] % NSB]
            st["sb"] += 1
            lhsT, rhs = score_mm(hd, kt, q0, q1, j)
            last_is_score = (not (bias or j >= 0 or selmask)) or linear
            sc.op("pe", lambda e: e.matmul(self.ps[b][:, c0:W], lhsT=lhsT, rhs=rhs, start=True, stop=last_is_score),
                  reads=htks, writes=[self.pst[b]])
            if linear:
                bd = sbanks[st["sb"] % NSB]
                st["sb"] += 1
                first = True
            else:
                bd = b
                first = False
            if bias:
                lastb = not (j >= 0 or (selmask and j < 0))
                sc.op("pe", lambda e: e.matmul(self.ps[bd][:, c0:W], lhsT=hd["L"][:, kt * 128:(kt + 1) * 128], rhs=hd["R"][:, q0:q1],
                                               start=first, stop=lastb),
                      reads=htks, writes=[self.pst[bd]], join=not first)
                first = False
            if selmask and j < 0:
                blk = kt // 2
                sc.op("pe", lambda e: e.matmul(self.ps[bd][:, c0:W], lhsT=c["onehot"][:, blk, :], rhs=hd["NS"][:, q0:q1],
                                               start=False, stop=True),
                      reads=htks + [c["tk"]], writes=[self.pst[bd]], join=True)
            if j >= 0:
                sc.op("pe", lambda e: e.matmul(self.ps[bd][:, c0:c0 + 128], lhsT=self.ident[:], rhs=c["tri"][:],
                                               start=first, stop=True),
                      reads=[self.ident_t, c["tk"]], writes=[self.pst[bd]], join=True)
            return (b, bd, c0, j)

        def stage_b(i, info):
            b, bd, c0, j = info
            pt, pt_tk = pts[i % NPT]
            if not linear:
                sc.op("act", lambda e: e.activation(out=pt[:, c0:W], in_=self.ps[b][:, c0:W], func=AF.Exp),
                      reads=[self.pst[b]], writes=[pt_tk])
            else:
                et, et_tk = ets[st["et"] % 3]
                st["et"] += 1
                sc.op("act", lambda e: e.activation(out=et[:, c0:W], in_=self.ps[bd][:, c0:W], func=AF.Exp),
                      reads=[self.pst[bd]], writes=[et_tk])
                sc.op("dve", lambda e: e.tensor_tensor(out=pt[:, c0:W], in0=self.ps[b][:, c0:W], in1=et[:, c0:W], op=ALU.mult),
                      reads=[self.pst[b], et_tk], writes=[pt_tk])

        def stage_c(i, info):
            h, G, kt = blocks[i]
            hd = get_head(h)
            htks = hd["tks"]
            b, bd, c0, j = info
            pt, pt_tk = pts[i % NPT]
            gi = h * NGR + G
            ob = obanks2[gi % 2]
            for qi in range(max(j, 0), GQ):
                sc.op("pe", lambda e: e.matmul(self.ps[ob][:, qi * 128:qi * 128 + VW], lhsT=pt[:, qi * 128:(qi + 1) * 128],
                                               rhs=hd["V"][:, kt, 0:VW], start=(kt == 0 and qi == 0), stop=(kt == G * GQ + qi),
                                               skip_group_check=True),
                      reads=[pt_tk] + htks, writes=[self.pst[ob]], join=not (kt == 0 and qi == 0))
            if kt == (G + 1) * GQ - 1:
                yo, yo_tk = yos[gi % 2]
                rec, rec_tk = recs[gi % 2]
                ov = self.ps[ob][:, 0:GQ * 128].rearrange("p (q c) -> p q c", c=128)
                if not linear:
                    sc.op("dve", lambda e: e.reciprocal(out=rec[:, 0:GQ], in_=ov[:, :, 64]), reads=[self.pst[ob]], writes=[rec_tk])
                    for qi in range(GQ):
                        sc.op("dve", lambda e: e.tensor_scalar_mul(out=yo[:, qi, :], in0=ov[:, qi, 0:64], scalar1=rec[:, qi:qi + 1]),
                              reads=[self.pst[ob], rec_tk], writes=[yo_tk], join=(qi > 0))
                else:
                    sc.op("dve", lambda e: e.tensor_copy(out=yo[:], in_=ov[:, :, 0:64]), reads=[self.pst[ob]], writes=[yo_tk])
                sc.dma("sp", yo_tk, y_d[G * W:(G + 1) * W, ycol0 + h * 64:ycol0 + (h + 1) * 64].rearrange("(q p) d -> p q d", p=128),
                       yo[:], reads=[yo_tk])

        infos = {}
        n = len(blocks)
        first_blk = {}
        for i, (h, G, kt) in enumerate(blocks):
            if G == 0 and kt == 0:
                first_blk[i + LAG - 1] = h
        nblk_head = n // H
        bgq = {"q": [], "rate": 0}

        def flush_bg(k=None):
            m = len(bgq["q"]) if k is None else min(k, len(bgq["q"]))
            for _ in range(m):
                bgq["q"].pop(0)()

        for i in range(n + LAG):
            if i < n:
                h, G, kt = blocks[i]
                if G == 0 and kt == 0:
                    if h not in heads:
                        hd0 = get_head(h)
                        bgq["q"] = list(hd0.get("bg", []))
                    flush_bg()
                infos[i] = stage_a(i)
                stage_b(i, infos[i])
            if i >= LAG:
                stage_c(i - LAG, infos.pop(i - LAG))
            if i in first_blk and first_blk[i] + 1 < H:
                hdn = get_head(first_blk[i] + 1)
                bgq["q"] = list(hdn.get("bg", []))
                bgq["rate"] = -(-len(bgq["q"]) // max(nblk_head - LAG - 2, 1))
            elif bgq["q"]:
                flush_bg(bgq["rate"])
        return out_tks

    def weight_preconvert(self, jobs):
        sc = self.sc
        stg = [(self.sb([128, 2048], F32), Tk()) for _ in range(2)]
        obs = [(self.sb([128, 2048], BF16), Tk()) for _ in range(2)]
        i = 0
        for w32, wbf, cc in jobs:
            K_, N_ = w32.shape[0], w32.shape[1]
            for k in range(K_ // 128):
                for c0 in range(0, N_, cc):
                    st_, st_tk = stg[i % 2]
                    ob, ob_tk = obs[i % 2]
                    sc.dma("pool", st_tk, st_[:, 0:cc], w32[k * 128:(k + 1) * 128, c0:c0 + cc], writes=[st_tk])
                    sc.op("pool", lambda e: e.tensor_copy(out=ob[:, 0:cc], in_=st_[:, 0:cc]), reads=[st_tk], writes=[ob_tk])
                    sc.dma("pool", ob_tk, wbf[k * 128:(k + 1) * 128, c0:c0 + cc], ob[:, 0:cc], reads=[ob_tk])
                    i += 1
        return [tk for _, tk in stg] + [tk for _, tk in obs]

    def phase_fox_attn(self, qkT_d, vp_d, RL_d, y_d, c, wjobs=None):
        sc, S, NT = self.sc, self.S, self.NT
        self.sb_reset()
        slots = []
        for sl in range(2):
            d = {"q": self.sb([128, S], BF16), "k": self.sb([128, S], BF16), "V": self.sb([128, NT, 65], BF16), "tk": Tk()}
            sc.op("pool", lambda e: e.memset(d["q"][64:128, :], 0.0), writes=[d["tk"]])
            sc.op("pool", lambda e: e.memset(d["k"][64:128, :], 0.0), writes=[d["tk"]], join=True)
            slots.append(d)

        def load_head(h, sl):
            d = slots[sl]
            tk = d["tk"]
            sc.dma("sp", tk, d["q"][0:64, :], qkT_d[h * 64:(h + 1) * 64, :], writes=[tk])
            sc.dma("sp", tk, d["k"][0:64, :], qkT_d[1024 + h * 64:1024 + (h + 1) * 64, :], writes=[tk], join=True)
            sc.dma("sp", tk, d["q"][64:70, :], RL_d[h, 0:6, :], writes=[tk], join=True)
            sc.dma("sp", tk, d["k"][64:70, :], RL_d[h, 6:12, :], writes=[tk], join=True)
            sc.dma("sp", tk, d["V"][:], vp_d[:, h * 65:(h + 1) * 65].rearrange("(t p) d -> p t d", p=128), writes=[tk], join=True)
            return {"q": d["q"], "k": d["k"], "V": d["V"], "tks": [tk]}

        def score_mm(hd, kt, q0, q1, j):
            return hd["k"][:, kt * 128:(kt + 1) * 128], hd["q"][:, q0:q1]

        wtks = self.weight_preconvert(wjobs) if wjobs else []
        out_tks = self.attn_core(16, 4 if NT >= 4 else NT, load_head, score_mm, y_d, c, bias=False)
        self.phase_barrier(out_tks + [d["tk"] for d in slots] + wtks)

    def phase_out_proj(self, x_dram, y_d, w_out, KT, xout_dram=None):
        sc, S, NT = self.sc, self.S, self.NT
        xout_dram = x_dram if xout_dram is None else xout_dram
        self.sb_reset()
        wo, wo_tk = self.sb([128, KT, D], BF16), Tk()
        stage = [self.sb([128, 1024], F32) for _ in range(2)]
        stage_tk = [Tk() for _ in range(2)]
        self.load_weight(wo, wo_tk, w_out, KT * 128, D, stage, stage_tk, col_chunk=1024)
        yts = [(self.sb([128, KT * 128], BF16), Tk()) for _ in range(2)]
        yTs = [(self.sb([128, KT, 128], BF16), Tk()) for _ in range(2)]
        xres = [(self.sb([128, 512], F32), Tk()) for _ in range(4)]
        sc.dma("sp", yts[0][1], yts[0][0][:], y_d[0:128, :], writes=[yts[0][1]])
        for t in range(NT):
            yt, yt_tk = yts[t % 2]
            yT, yT_tk = yTs[t % 2]
            if t + 1 < NT:
                sc.dma("sp", yts[(t + 1) % 2][1], yts[(t + 1) % 2][0][:], y_d[(t + 1) * 128:(t + 2) * 128, :], writes=[yts[(t + 1) % 2][1]])
            for cc in range(2):
                xr, xr_tk = xres[(t * 2 + cc) % 4]
                sc.dma("sp", xr_tk, xr[:], x_dram[t * 128:(t + 1) * 128, cc * 512:(cc + 1) * 512], writes=[xr_tk])
            for k0 in range(0, KT, 8):
                kn = min(8, KT - k0)
                b = self.bank()
                pt = self.ps[b][:].bitcast(BF16)
                for k in range(kn):
                    sc.op("pe", lambda e: e.transpose(out=pt[:, k * 128:(k + 1) * 128], in_=yt[:, (k0 + k) * 128:(k0 + k + 1) * 128],
                                                      identity=self.ident[:]),
                          reads=[yt_tk, self.ident_t], writes=[self.pst[b]], join=(k > 0))
                sc.op("act", lambda e: e.activation(out=yT[:, k0:k0 + kn, :], in_=pt[:, 0:kn * 128].rearrange("p (k t) -> p k t", k=kn),
                                                    func=AF.Copy),
                      reads=[self.pst[b]], writes=[yT_tk], join=(k0 > 0))
            for cc in range(2):
                xr, xr_tk = xres[(t * 2 + cc) % 4]
                b = self.bank()
                for k in range(KT):
                    sc.op("pe", lambda e: e.matmul(self.ps[b][:], lhsT=yT[:, k, :], rhs=wo[:, k, cc * 512:(cc + 1) * 512],
                                                   start=(k == 0), stop=(k == KT - 1)),
                          reads=[wo_tk, yT_tk], writes=[self.pst[b]], join=(k > 0))
                sc.op("dve", lambda e: e.tensor_tensor(out=xr[:], in0=self.ps[b][:], in1=xr[:], op=ALU.add),
                      reads=[self.pst[b], xr_tk], writes=[xr_tk])
                sc.dma("sp", xr_tk, xout_dram[t * 128:(t + 1) * 128, cc * 512:(cc + 1) * 512], xr[:], reads=[xr_tk])
        self.phase_barrier([tk for _, tk in xres])

    def phase_l0_proj(self, x_dram, g_dram, w_in, conv_w, conv_b, dtb_d, alog_d, cos_d, sin_d,
                      xbcT_d, qkT_d, zs_d, vp_d, RL_d, c, aw_d=None):
        sc, S, NT = self.sc, self.S, self.NT
        NG = S // 512
        self.sb_reset()
        nl, nl_tk = self.sb([128, NT, 16], F32), Tk()
        dtk, dtk_tk = self.sb([128, NT, 16], F32), Tk()
        self.persist_end = self.sb_off
        self.dtk_view = dtk
        hT, hT_tk, mark = self.norm_all(x_dram, g_dram)
        self.phase_barrier([hT_tk])
        self.sb_off = mark
        wts = [(self.sb([128, 8, 128], BF16), Tk()) for _ in range(4)]
        stgs = [(self.sb([128, 8, 128], F32), Tk()) for _ in range(2)]
        rows = [(self.sb([128, S], BF16), Tk()) for _ in range(2)]
        out_tks = []
        cw, cw_tk = self.sb([128, 4, 16], F32), Tk()
        cb, cb_tk = self.sb([128, 16], F32), Tk()
        for k in range(4):
            sc.dma("sp", cw_tk, cw[:, k, :], conv_w[k, :].rearrange("(f p) -> p f", p=128), writes=[cw_tk], join=(k > 0),
                   allow_slow_non_contiguous=True)
        sc.dma("sp", cb_tk, cb[:], conv_b[0, :].rearrange("(f p) -> p f", p=128), writes=[cb_tk], allow_slow_non_contiguous=True)
        u, u_tk = self.sb([128, S + 8], F32), Tk()
        acc, acc_tk = self.sb([128, S], F32), Tk()
        sc.op("pool", lambda e: e.memset(u[:, 0:8], 0.0), writes=[u_tk])
        self.load_wcols(w_in, 1024, 128, wts[0][0], wts[0][1], stgs[0][0], stgs[0][1])
        for f in range(16):
            wt, wt_tk = wts[f % 2]
            row, row_tk = rows[f % 2]
            if f + 1 < 16:
                self.load_wcols(w_in, 1024 + (f + 1) * 128, 128, wts[(f + 1) % 2][0], wts[(f + 1) % 2][1], stgs[(f + 1) % 2][0], stgs[(f + 1) % 2][1])
            for tg in range(NG):
                b = self.bank()
                for k in range(8):
                    sc.op("pe", lambda e: e.matmul(self.ps[b][:], lhsT=wt[:, k, :], rhs=hT[:, k, tg * 512:(tg + 1) * 512],
                                                   start=(k == 0), stop=(k == 7)),
                          reads=[wt_tk, hT_tk], writes=[self.pst[b]], join=(k > 0))
                sc.op("act", lambda e: e.activation(out=u[:, 3 + tg * 512:3 + (tg + 1) * 512], in_=self.ps[b][:], func=AF.Copy),
                      reads=[self.pst[b]], writes=[u_tk], join=True)
            sc.op("act", lambda e: e.activation(out=acc[:], in_=u[:, 0:S], func=AF.Copy, scale=cw[:, 0, f:f + 1]),
                  reads=[u_tk, cw_tk], writes=[acc_tk])
            for k in range(1, 4):
                sc.op("dve", lambda e: e.scalar_tensor_tensor(out=acc[:], in0=u[:, k:k + S], scalar=cw[:, k, f:f + 1], in1=acc[:],
                                                              op0=ALU.mult, op1=ALU.add),
                      reads=[u_tk, cw_tk, acc_tk], writes=[acc_tk])
            sc.op("act", lambda e: e.activation(out=row[:], in_=acc[:], func=AF.Silu, bias=cb[:, f:f + 1]),
                  reads=[acc_tk, cb_tk], writes=[row_tk])
            sc.dma("sp", row_tk, xbcT_d[f * 128:(f + 1) * 128, :], row[:], reads=[row_tk])
            out_tks.append(row_tk)
        self.phase_barrier(out_tks)
        self.sb_off = mark
        wts = [(self.sb([128, 8, 128], BF16), Tk()) for _ in range(4)]
        stgs = [(self.sb([128, 8, 128], F32), Tk()) for _ in range(2)]
        rows = [(self.sb([128, S], BF16), Tk()) for _ in range(2)]
        out_tks = []
        cosT, sinT, cs_tk = self.sb([128, S], F32), self.sb([128, S], F32), Tk()
        sc.dma("sp", cs_tk, cosT[:], cos_d[:, :], writes=[cs_tk])
        sc.dma("sp", cs_tk, sinT[:], sin_d[:, :], writes=[cs_tk], join=True)
        t1s = [(self.sb([128, 512], F32), Tk()) for _ in range(2)]
        t2s = [(self.sb([128, 512], F32), Tk()) for _ in range(2)]
        i = 0

        def load_qk(ii):
            col_ = 3088 + (ii // 4) * 512 + (ii % 4) * 128
            self.load_wcols(w_in, col_, 128, wts[ii % 2][0], wts[ii % 2][1], stgs[0][0], stgs[0][1])
            self.load_wcols(w_in, col_, 128, wts[2 + ii % 2][0], wts[2 + ii % 2][1], stgs[1][0], stgs[1][1], perm_heads=True)

        load_qk(0)
        for which in range(2):
            for f in range(4):
                wt, wt_tk = wts[i % 2]
                wp, wp_tk = wts[2 + i % 2]
                row, row_tk = rows[i % 2]
                if i + 1 < 8:
                    load_qk(i + 1)
                for tg in range(NG):
                    ba, bb = self.bank(), self.bank()
                    for k in range(8):
                        sc.op("pe", lambda e: e.matmul(self.ps[ba][:], lhsT=wt[:, k, :], rhs=hT[:, k, tg * 512:(tg + 1) * 512],
                                                       start=(k == 0), stop=(k == 7)),
                              reads=[wt_tk, hT_tk], writes=[self.pst[ba]], join=(k > 0))
                    for k in range(8):
                        sc.op("pe", lambda e: e.matmul(self.ps[bb][:], lhsT=wp[:, k, :], rhs=hT[:, k, tg * 512:(tg + 1) * 512],
                                                       start=(k == 0), stop=(k == 7)),
                              reads=[wp_tk, hT_tk], writes=[self.pst[bb]], join=(k > 0))
                    t1, t1_tk = t1s[tg % 2]
                    t2, t2_tk = t2s[tg % 2]
                    sc.op("dve", lambda e: e.tensor_tensor(out=t1[:], in0=self.ps[ba][:], in1=cosT[:, tg * 512:(tg + 1) * 512], op=ALU.mult),
                          reads=[self.pst[ba], cs_tk], writes=[t1_tk])
                    sc.op("dve", lambda e: e.tensor_tensor(out=t2[:], in0=self.ps[bb][:], in1=sinT[:, tg * 512:(tg + 1) * 512], op=ALU.mult),
                          reads=[self.pst[bb], cs_tk], writes=[t2_tk])
                    sc.op("pool", lambda e: e.tensor_tensor(out=t1[:], in0=t1[:], in1=t2[:], op=ALU.add),
                          reads=[t1_tk, t2_tk], writes=[t1_tk])
                    sc.op("act", lambda e: e.activation(out=row[:, tg * 512:(tg + 1) * 512], in_=t1[:], func=AF.Copy,
                                                        scale=(0.125 if which == 0 else 1.0)),
                          reads=[t1_tk], writes=[row_tk], join=(tg > 0))
                sc.dma("sp", row_tk, qkT_d[which * 512 + f * 128:which * 512 + (f + 1) * 128, :], row[:], reads=[row_tk])
                out_tks.append(row_tk)
                i += 1
        self.phase_barrier(out_tks)
        self.sb_off = mark
        out_tks = []
        wz, wz_tk = self.sb([128, 8, 1024], BF16), Tk()
        wv, wv_tk = self.sb([128, 8, 512], BF16), Tk()
        wd, wd_tk = self.sb([128, 8, 16], BF16), Tk()
        stz, stz_tk = self.sb([128, 8, 512], F32), Tk()
        for cc in range(2):
            self.load_wcols(w_in, cc * 512, 512, wz[:, :, cc * 512:(cc + 1) * 512], wz_tk, stz, stz_tk)
        self.load_wcols(w_in, 4112, 512, wv, wv_tk, stz, stz_tk)
        self.load_wcols(w_in, 3072, 16, wd, wd_tk, stz, stz_tk)
        dtb, ea, sm_tk = self.sb([128, 16], F32), self.sb([128, 16], F32), Tk()
        sc.dma("sp", sm_tk, dtb[:], dtb_d.partition_broadcast(128), writes=[sm_tk])
        sc.dma("sp", sm_tk, ea[:], alog_d.partition_broadcast(128), writes=[sm_tk], join=True)
        sc.op("act", lambda e: e.activation(out=ea[:], in_=ea[:], func=AF.Exp), reads=[sm_tk], writes=[sm_tk])
        zts = [(self.sb([128, 1024], BF16), Tk()) for _ in range(2)]
        vts = [(self.sb([128, 8, 65], BF16), Tk()) for _ in range(2)]
        for vt, vt_tk in vts:
            sc.op("pool", lambda e: e.memset(vt[:], 1.0), writes=[vt_tk])
        for t in range(NT):
            zt, zt_tk = zts[t % 2]
            vt, vt_tk = vts[t % 2]
            for cc in range(2):
                b = self.bank()
                for k in range(8):
                    sc.op("pe", lambda e: e.matmul(self.ps[b][:], lhsT=hT[:, k, t * 128:(t + 1) * 128], rhs=wz[:, k, cc * 512:(cc + 1) * 512],
                                                   start=(k == 0), stop=(k == 7)),
                          reads=[wz_tk, hT_tk], writes=[self.pst[b]], join=(k > 0))
                sc.op("act", lambda e: e.activation(out=zt[:, cc * 512:(cc + 1) * 512], in_=self.ps[b][:], func=AF.Silu),
                      reads=[self.pst[b]], writes=[zt_tk], join=(cc > 0))
            sc.dma("sp", zt_tk, zs_d[t * 128:(t + 1) * 128, :], zt[:], reads=[zt_tk])
            b = self.bank()
            for k in range(8):
                sc.op("pe", lambda e: e.matmul(self.ps[b][:], lhsT=hT[:, k, t * 128:(t + 1) * 128], rhs=wv[:, k, :],
                                               start=(k == 0), stop=(k == 7)),
                      reads=[wv_tk, hT_tk], writes=[self.pst[b]], join=(k > 0))
            sc.op("act", lambda e: e.activation(out=vt[:, :, 0:64], in_=self.ps[b][:].rearrange("p (h d) -> p h d", h=8), func=AF.Copy),
                  reads=[self.pst[b]], writes=[vt_tk])
            sc.dma("sp", vt_tk, vp_d[t * 128:(t + 1) * 128, :], vt.rearrange("p h d -> p (h d)"), reads=[vt_tk])
            b = self.bank()
            for k in range(8):
                sc.op("pe", lambda e: e.matmul(self.ps[b][:, 0:16], lhsT=hT[:, k, t * 128:(t + 1) * 128], rhs=wd[:, k, :],
                                               start=(k == 0), stop=(k == 7)),
                      reads=[wd_tk, hT_tk], writes=[self.pst[b]], join=(k > 0))
            sc.op("dve", lambda e: e.tensor_tensor(out=dtk[:, t, :], in0=self.ps[b][:, 0:16], in1=dtb[:], op=ALU.add),
                  reads=[self.pst[b], sm_tk], writes=[dtk_tk], join=True)
            out_tks += [zt_tk, vt_tk]
        sc.op("act", lambda e: e.activation(out=dtk[:], in_=dtk[:], func=AF.Exp), reads=[dtk_tk], writes=[dtk_tk])
        sc.op("dve", lambda e: e.tensor_scalar_add(out=dtk[:], in0=dtk[:], scalar1=1.0), reads=[dtk_tk], writes=[dtk_tk])
        sc.op("act", lambda e: e.activation(out=dtk[:], in_=dtk[:], func=AF.Ln), reads=[dtk_tk], writes=[dtk_tk])
        for t in range(NT):
            sc.op("dve", lambda e: e.tensor_tensor(out=nl[:, t, :], in0=dtk[:, t, :], in1=ea[:], op=ALU.mult),
                  reads=[dtk_tk, sm_tk], writes=[nl_tk], join=True)
        self.phase_barrier(out_tks + [nl_tk, dtk_tk])
        self.sb_off = self.persist_end
        out_tks = []
        self.cumsum_rows(nl, nl_tk, RL_d, c, out_tks, aw_d=aw_d)
        self.phase_barrier(out_tks)

    def phase_ssd_prep(self, xbcT_d, xs_d, xc_d, btok_d=None):
        sc, S, NT = self.sc, self.S, self.NT
        self.sb_off = self.persist_end
        dtk = self.dtk_view
        TG = 4 if NT >= 4 else NT
        ins = [(self.sb([128, 8, TG * 128], BF16), Tk()) for _ in range(2)]
        xss = [(self.sb([128, 16, 64], BF16), Tk()) for _ in range(2)]
        xcs = [(self.sb([128, 16, 64], BF16), Tk()) for _ in range(2)]
        inb = [(self.sb([128, 4, TG * 128], BF16), Tk()) for _ in range(2)]
        bts = [(self.sb([128, 512], BF16), Tk()) for _ in range(2)]

        def prep_load(gg):
            cs = slice(gg * TG * 128, (gg + 1) * TG * 128)
            sc.dma("sp", ins[gg % 2][1], ins[gg % 2][0][:], xbcT_d[0:1024, cs].rearrange("(k p) t -> p k t", p=128), writes=[ins[gg % 2][1]])
            if btok_d is not None:
                sc.dma("sp", inb[gg % 2][1], inb[gg % 2][0][:], xbcT_d[1024:1536, cs].rearrange("(k p) t -> p k t", p=128), writes=[inb[gg % 2][1]])

        for t in range(NT):
            tg, tt = t // TG, t % TG
            it_, it_tk = ins[tg % 2]
            it_ = it_[:, :, tt * 128:(tt + 1) * 128]
            xs, xs_tk = xss[t % 2]
            xc, xc_tk = xcs[t % 2]
            if tt == 0:
                if tg == 0:
                    prep_load(0)
                if (tg + 1) * TG < NT:
                    prep_load(tg + 1)
            b = self.bank()
            pt = self.ps[b][:].bitcast(BF16)
            for k in range(8):
                sc.op("pe", lambda e: e.transpose(out=pt[:, k * 128:(k + 1) * 128], in_=it_[:, k, :], identity=self.ident[:]),
                      reads=[it_tk, self.ident_t], writes=[self.pst[b]], join=(k > 0))
            sc.op("act", lambda e: e.activation(out=xs.rearrange("p h d -> p (h d)"), in_=pt, func=AF.Copy),
                  reads=[self.pst[b]], writes=[xs_tk])
            for h in range(16):
                sc.op("dve" if h % 2 == 0 else "pool", lambda e: e.tensor_scalar_mul(out=xc[:, h, :], in0=xs[:, h, :], scalar1=dtk[:, t, h:h + 1]),
                      reads=[xs_tk], writes=[xc_tk], join=(h > 0))
            sc.dma("sp", xs_tk, xs_d[t * 128:(t + 1) * 128, :], xs.rearrange("p h d -> p (h d)"), reads=[xs_tk])
            sc.dma("sp", xc_tk, xc_d[t * 128:(t + 1) * 128, :], xc.rearrange("p h d -> p (h d)"), reads=[xc_tk])
            if btok_d is not None:
                ib, ib_tk = inb[tg % 2]
                ib = ib[:, :, tt * 128:(tt + 1) * 128]
                bt, bt_tk = bts[t % 2]
                b = self.bank()
                pt = self.ps[b][:].bitcast(BF16)
                for k in range(4):
                    sc.op("pe", lambda e: e.transpose(out=pt[:, k * 128:(k + 1) * 128], in_=ib[:, k, :], identity=self.ident[:]),
                          reads=[ib_tk, self.ident_t], writes=[self.pst[b]], join=(k > 0))
                sc.op("act", lambda e: e.activation(out=bt[:], in_=pt[:, 0:512], func=AF.Copy), reads=[self.pst[b]], writes=[bt_tk])
                sc.dma("sp", bt_tk, btok_d[t * 128:(t + 1) * 128, :], bt[:], reads=[bt_tk])
        self.phase_barrier([tk for _, tk in xss] + [tk for _, tk in xcs] + [tk for _, tk in bts])

    def phase_ssd_chunk(self, xbcT_d, btok_d, xc_d, RL_d, aw_d, y_d, c):
        sc, S, NT = self.sc, self.S, self.NT
        self.sb_reset()
        aw, ea, dte, cd = (self.sb([128, NT, 16], F32) for _ in range(4))
        f_tk = Tk()
        sc.dma("sp", f_tk, aw[:], aw_d.rearrange("(t p) h -> p t h", p=128), writes=[f_tk])
        awf = aw.rearrange("p t h -> p (t h)")
        NF = NT * 16
        bt_ = self.bank()
        sc.op("pe", lambda e: e.matmul(self.ps[bt_][:, 0:NF], lhsT=c["sel127"][:], rhs=awf, start=True, stop=True),
              reads=[f_tk, c["tk"]], writes=[self.pst[bt_]])
        tot = self.sb([128, NF], F32)
        sc.op("dve", lambda e: e.tensor_copy(out=tot[:], in_=self.ps[bt_][:, 0:NF]), reads=[self.pst[bt_]], writes=[f_tk], join=True)
        sc.op("act", lambda e: e.activation(out=ea.rearrange("p t h -> p (t h)"), in_=awf, func=AF.Exp, scale=-1.0),
              reads=[f_tk], writes=[f_tk], join=True)
        sc.op("dve", lambda e: e.tensor_tensor(out=dte.rearrange("p t h -> p (t h)"), in0=awf, in1=tot[:], op=ALU.subtract),
              reads=[f_tk], writes=[f_tk])
        sc.op("act", lambda e: e.activation(out=dte.rearrange("p t h -> p (t h)"), in_=dte.rearrange("p t h -> p (t h)"), func=AF.Exp),
              reads=[f_tk], writes=[f_tk])
        sc.op("act", lambda e: e.activation(out=cd.rearrange("p t h -> p (t h)"), in_=tot[:], func=AF.Exp, scale=-1.0),
              reads=[f_tk], writes=[f_tk])
        H = self.sb([128, 16, 64], F32)
        Hb = self.sb([128, 1024], BF16)
        H_tk = [Tk() for _ in range(4)]
        Hb_tk = [Tk() for _ in range(4)]
        sc.op("pool", lambda e: e.memset(H[:], 0.0), writes=H_tk)
        sc.op("pool", lambda e: e.memset(Hb[:], 0.0), writes=Hb_tk)
        ld = []
        for i in range(2):
            d = {"CT": self.sb([128, 4, 128], BF16), "BT": self.sb([128, 4, 128], BF16), "Bk": self.sb([128, 512], BF16),
                 "xc": self.sb([128, 1024], BF16), "L": self.sb([128, 16, 128], BF16), "R": self.sb([128, 16, 128], BF16), "tk": Tk()}
            sc.op("pool", lambda e: e.memset(d["L"][:], 0.0), writes=[d["tk"]])
            sc.op("pool", lambda e: e.memset(d["R"][:], 0.0), writes=[d["tk"]], join=True)
            ld.append(d)

        def load(ci):
            d = ld[ci % 2]
            tk = d["tk"]
            cs = slice(ci * 128, (ci + 1) * 128)
            sc.dma("sp", tk, d["CT"][:], xbcT_d[1536:2048, cs].rearrange("(g n) t -> n g t", n=128), writes=[tk])
            sc.dma("sp", tk, d["BT"][:], xbcT_d[1024:1536, cs].rearrange("(g n) t -> n g t", n=128), writes=[tk], join=True)
            sc.dma("sp", tk, d["Bk"][:], btok_d[cs, :], writes=[tk], join=True)
            sc.dma("sp", tk, d["xc"][:], xc_d[cs, :], writes=[tk], join=True)
            sc.dma("sp", tk, d["R"][0:6, :, :], RL_d[:, 0:6, cs].rearrange("h j t -> j h t"), writes=[tk], join=True)
            sc.dma("sp", tk, d["L"][0:6, :, :], RL_d[:, 6:12, cs].rearrange("h j t -> j h t"), writes=[tk], join=True)

        ets = [(self.sb([128, 512], F32), Tk()) for _ in range(2)]
        mts = [(self.sb([128, 4, 128], BF16), Tk()) for _ in range(2)]
        yds = [(self.sb([128, 256], F32), Tk()) for _ in range(2)]
        xcds = [(self.sb([128, 256], BF16), Tk()) for _ in range(2)]
        yts = [(self.sb([128, 1024], BF16), Tk()) for _ in range(2)]
        load(0)
        if NT > 1:
            load(1)
        steps = [(ci, g) for ci in range(NT) for g in range(4)]

        def bufs_for(i):
            return {"db": 2 + i % 2, "yb": 4 + i % 2, "sbk": 6 + i % 2, "et": ets[i % 2], "mt": mts[i % 2], "yd": yds[i % 2], "xcd": xcds[i % 2]}

        def front(i):
            ci, g = steps[i]
            d = ld[ci % 2]
            tk = d["tk"]
            cbk = ci % 2
            B = bufs_for(i)
            db = B["db"]
            et, et_tk = B["et"]
            mt, mt_tk = B["mt"]
            sc.op("pe", lambda e: e.matmul(self.ps[cbk][:, g * 128:(g + 1) * 128], lhsT=d["BT"][:, g, :], rhs=d["CT"][:, g, :],
                                           start=True, stop=True, skip_group_check=True),
                  reads=[tk], writes=[self.pst[cbk]], join=(g > 0))
            for r in range(4):
                h = 4 * g + r
                sc.op("pe", lambda e: e.matmul(self.ps[db][:, r * 128:(r + 1) * 128], lhsT=d["L"][:, h, :], rhs=d["R"][:, h, :],
                                               start=True, stop=False, skip_group_check=True),
                      reads=[tk], writes=[self.pst[db]], join=(r > 0))
                sc.op("pe", lambda e: e.matmul(self.ps[db][:, r * 128:(r + 1) * 128], lhsT=self.ident[:], rhs=c["tri"][:],
                                               start=False, stop=True, skip_group_check=True),
                      reads=[self.ident_t, c["tk"]], writes=[self.pst[db]], join=True)
            sc.op("act", lambda e: e.activation(out=et[:], in_=self.ps[db][:], func=AF.Exp), reads=[self.pst[db]], writes=[et_tk])
            for r in range(4):
                sc.op("dve", lambda e: e.tensor_tensor(out=mt[:, r, :], in0=self.ps[cbk][:, g * 128:(g + 1) * 128],
                                                       in1=et[:, r * 128:(r + 1) * 128], op=ALU.mult),
                      reads=[self.pst[cbk], et_tk], writes=[mt_tk], join=(r > 0))

        def back(i):
            ci, g = steps[i]
            d = ld[ci % 2]
            tk = d["tk"]
            yt, yt_tk = yts[ci % 2]
            B = bufs_for(i)
            yb, sbk = B["yb"], B["sbk"]
            mt, mt_tk = B["mt"]
            yd, yd_tk = B["yd"]
            xcd, xcd_tk = B["xcd"]
            for r in range(4):
                h = 4 * g + r
                sc.op("pe", lambda e: e.matmul(self.ps[yb][:, r * 64:(r + 1) * 64], lhsT=mt[:, r, :], rhs=d["xc"][:, h * 64:(h + 1) * 64],
                                               start=True, stop=True, skip_group_check=True),
                      reads=[mt_tk, tk], writes=[self.pst[yb]], join=(r > 0))
            sc.op("pe", lambda e: e.matmul(self.ps[yb][:, 256:512], lhsT=d["CT"][:, g, :], rhs=Hb[:, g * 256:(g + 1) * 256],
                                           start=True, stop=True, skip_group_check=True),
                  reads=[tk, Hb_tk[g]], writes=[self.pst[yb]], join=True)
            sc.op("act", lambda e: e.activation(out=yd[:], in_=self.ps[yb][:, 0:256], func=AF.Copy), reads=[self.pst[yb]], writes=[yd_tk])
            for r in range(4):
                h = 4 * g + r
                sc.op("dve", lambda e: e.scalar_tensor_tensor(out=yt[:, h * 64:(h + 1) * 64], in0=self.ps[yb][:, 256 + r * 64:256 + (r + 1) * 64],
                                                              scalar=ea[:, ci, h:h + 1], in1=yd[:, r * 64:(r + 1) * 64],
                                                              op0=ALU.mult, op1=ALU.add),
                      reads=[self.pst[yb], yd_tk, f_tk], writes=[yt_tk], join=not (g == 0 and r == 0))
            for r in range(4):
                h = 4 * g + r
                sc.op("pool" if r % 2 == 0 else "dve", lambda e: e.tensor_scalar_mul(out=xcd[:, r * 64:(r + 1) * 64], in0=d["xc"][:, h * 64:(h + 1) * 64],
                                                                                      scalar1=dte[:, ci, h:h + 1]),
                      reads=[tk, f_tk], writes=[xcd_tk], join=(r > 0))
            sc.op("pe", lambda e: e.matmul(self.ps[sbk][:, 0:256], lhsT=d["Bk"][:, g * 128:(g + 1) * 128], rhs=xcd[:],
                                           start=True, stop=True),
                  reads=[tk, xcd_tk], writes=[self.pst[sbk]])
            for r in range(4):
                h = 4 * g + r
                sc.op("dve", lambda e: e.scalar_tensor_tensor(out=H[:, h, :], in0=H[:, h, :], scalar=cd[:, ci, h:h + 1],
                                                              in1=self.ps[sbk][:, r * 64:(r + 1) * 64], op0=ALU.mult, op1=ALU.add),
                      reads=[self.pst[sbk], f_tk, H_tk[g]], writes=[H_tk[g]])
            sc.op("act", lambda e: e.activation(out=Hb[:, g * 256:(g + 1) * 256], in_=H[:, 4 * g:4 * g + 4, :].rearrange("p h d -> p (h d)"),
                                                func=AF.Copy),
                  reads=[H_tk[g]], writes=[Hb_tk[g]])
            if g == 3:
                sc.dma("sp", yt_tk, y_d[ci * 128:(ci + 1) * 128, :], yt[:], reads=[yt_tk])
                if ci + 2 < NT:
                    load(ci + 2)

        nst = len(steps)
        for i in range(nst + 1):
            if i < nst:
                front(i)
            if i >= 1:
                back(i - 1)
        self.phase_barrier([tk for _, tk in yts] + [d["tk"] for d in ld])

    def phase_ssd_attn(self, xbcT_d, xc_d, RL_d, y_d, c):
        sc, S, NT = self.sc, self.S, self.NT
        self.sb_reset()
        slots = []
        for sl in range(2):
            d = {"B": self.sb([128, S], BF16), "C": self.sb([128, S], BF16), "R": self.sb([128, S], BF16),
                 "L": self.sb([128, S], BF16), "V": self.sb([128, NT, 64], BF16), "tk": Tk()}
            sc.op("pool", lambda e: e.memset(d["R"][:], 0.0), writes=[d["tk"]])
            sc.op("pool", lambda e: e.memset(d["L"][:], 0.0), writes=[d["tk"]], join=True)
            slots.append(d)

        def load_head(h, sl):
            d = slots[sl]
            tk = d["tk"]
            g = h // 4
            sc.dma("sp", tk, d["B"][:], xbcT_d[1024 + g * 128:1024 + (g + 1) * 128, :], writes=[tk])
            sc.dma("sp", tk, d["C"][:], xbcT_d[1536 + g * 128:1536 + (g + 1) * 128, :], writes=[tk], join=True)
            sc.dma("sp", tk, d["R"][0:6, :], RL_d[h, 0:6, :], writes=[tk], join=True)
            sc.dma("sp", tk, d["L"][0:6, :], RL_d[h, 6:12, :], writes=[tk], join=True)
            sc.dma("sp", tk, d["V"][:], xc_d[:, h * 64:(h + 1) * 64].rearrange("(t p) d -> p t d", p=128), writes=[tk], join=True)
            return {"B": d["B"], "C": d["C"], "R": d["R"], "L": d["L"], "V": d["V"], "tks": [tk]}

        def score_mm(hd, kt, q0, q1, j):
            return hd["B"][:, kt * 128:(kt + 1) * 128], hd["C"][:, q0:q1]

        out_tks = self.attn_core(16, 4 if NT >= 4 else NT, load_head, score_mm, y_d, c, linear=True, VW=64)
        self.phase_barrier(out_tks + [d["tk"] for d in slots])

    def phase_ssd_post(self, ysc_d, xs_d, zs_d, dsk_d, gn_d, y0_d):
        sc, S, NT = self.sc, self.S, self.NT
        self.sb_reset()
        dsk, gn, cst_tk = self.sb([128, 1024], F32), self.sb([128, 1024], F32), Tk()
        sc.dma("sp", cst_tk, dsk[:], dsk_d.partition_broadcast(128), writes=[cst_tk])
        sc.dma("sp", cst_tk, gn[:], gn_d.partition_broadcast(128), writes=[cst_tk], join=True)
        NB = 2
        ld = [[(self.sb([128, 1024], BF16), Tk()) for _ in range(3)] for _ in range(NB)]
        ys = [(self.sb([128, 1024], F32), Tk()) for _ in range(NB)]
        yo = [(self.sb([128, 1024], BF16), Tk()) for _ in range(NB)]
        st = [(self.sb([128, 16], F32), Tk()) for _ in range(NB)]
        junk, junk_tk = self.sb([128, 256], F32), Tk()
        for t in range(NT):
            (a, a_tk), (x_, x_tk), (z, z_tk) = ld[t % NB]
            y, y_tk = ys[t % NB]
            o, o_tk = yo[t % NB]
            s_, s_tk = st[t % NB]
            def post_load(tt):
                (a_, a_tk_), (x__, x_tk_), (z_, z_tk_) = ld[tt % NB]
                sc.dma("sp", a_tk_, a_[:], ysc_d[tt * 128:(tt + 1) * 128, :], writes=[a_tk_])
                sc.dma("sp", x_tk_, x__[:], xs_d[tt * 128:(tt + 1) * 128, :], writes=[x_tk_])
                sc.dma("sp", z_tk_, z_[:], zs_d[tt * 128:(tt + 1) * 128, :], writes=[z_tk_])
            if t == 0:
                post_load(0)
            if t + 1 < NT:
                post_load(t + 1)
            sc.op("dve", lambda e: e.tensor_tensor(out=y[:], in0=x_[:], in1=dsk[:], op=ALU.mult), reads=[x_tk, cst_tk], writes=[y_tk])
            sc.op("pool", lambda e: e.tensor_tensor(out=y[:], in0=y[:], in1=a[:], op=ALU.add), reads=[y_tk, a_tk], writes=[y_tk])
            sc.op("dve", lambda e: e.tensor_tensor(out=y[:], in0=y[:], in1=z[:], op=ALU.mult), reads=[y_tk, z_tk], writes=[y_tk])
            for g in range(4):
                sc.op("act", lambda e: e.activation(out=junk[:], in_=y[:, g * 256:(g + 1) * 256], func=AF.Square, scale=1.0 / 16.0,
                                                    accum_out=s_[:, g:g + 1]),
                      reads=[y_tk], writes=[junk_tk, s_tk], join=(g > 0))
            sc.op("dve", lambda e: e.tensor_scalar_add(out=s_[:, 4:8], in0=s_[:, 0:4], scalar1=EPS), reads=[s_tk], writes=[s_tk])
            sc.op("act", lambda e: e.activation(out=s_[:, 8:12], in_=s_[:, 4:8], func=AF.Sqrt), reads=[s_tk], writes=[s_tk])
            sc.op("dve", lambda e: e.reciprocal(out=s_[:, 12:16], in_=s_[:, 8:12]), reads=[s_tk], writes=[s_tk])
            for g in range(4):
                sc.op("dve", lambda e: e.scalar_tensor_tensor(
                    out=o[:, g * 256:(g + 1) * 256], in0=y[:, g * 256:(g + 1) * 256], scalar=s_[:, 12 + g:13 + g],
                    in1=gn[:, g * 256:(g + 1) * 256], op0=ALU.mult, op1=ALU.mult),
                    reads=[y_tk, s_tk, cst_tk], writes=[o_tk], join=(g > 0))
            sc.dma("sp", o_tk, y0_d[t * 128:(t + 1) * 128, 0:1024], o[:], reads=[o_tk])
        self.phase_barrier([tk for _, tk in yo])

    def phase_moba_gate(self, qkT_d, mot_d, m2t_d, o1t_d, ns_d, c):
        sc, S, NT = self.sc, self.S, self.NT
        self.sb_reset()
        NBLK = S // 256
        NF = NT * 16
        mot, m2t, mk_tk = self.sb([128, NF], F32), self.sb([128, NF], F32), Tk()
        sc.dma("sp", mk_tk, mot[:], mot_d[:, 0:NF], writes=[mk_tk])
        sc.dma("sp", mk_tk, m2t[:], m2t_d[:, 0:NF], writes=[mk_tk], join=True)
        o1t = self.sb([128, NF], F32)
        sc.dma("sp", mk_tk, o1t[:], o1t_d[:, 0:NF], writes=[mk_tk], join=True)
        slots = [{"q": self.sb([64, S], BF16), "k": self.sb([64, S], BF16), "tk": Tk()} for _ in range(2)]
        wk = [{"km": self.sb([64, 16], F32), "kmb": self.sb([64, 16], BF16), "gm": self.sb([128, NF], F32), "m8": self.sb([128, NT * 8], F32),
               "ns": self.sb([128, NF], F32), "ns2": self.sb([128, NF], BF16), "nsT": self.sb([16, S], BF16),
               "km_tk": Tk(), "g_tk": Tk(), "nsT_tk": Tk()} for _ in range(2)]

        def load(h):
            d = slots[h % 2]
            sc.dma("sp", d["tk"], d["q"][:], qkT_d[h * 64:(h + 1) * 64, :], writes=[d["tk"]])
            sc.dma("sp", d["tk"], d["k"][:], qkT_d[512 + h * 64:512 + (h + 1) * 64, :], writes=[d["tk"]], join=True)

        load(0)
        for h in range(8):
            if h + 1 < 8:
                load(h + 1)
            d, w = slots[h % 2], wk[h % 2]
            tk = d["tk"]
            sc.op("pool", lambda e: e.memset(w["km"][:], 0.0), writes=[w["km_tk"]])
            sc.op("dve", lambda e: e.tensor_reduce(out=w["km"][:, 0:NBLK], in_=d["k"].rearrange("p (b j) -> p b j", j=256), axis=AX.X, op=ALU.add),
                  reads=[tk], writes=[w["km_tk"]])
            sc.op("act", lambda e: e.activation(out=w["kmb"][:], in_=w["km"][:], func=AF.Copy, scale=1.0 / 256.0),
                  reads=[w["km_tk"]], writes=[w["km_tk"]])
            bg_ = self.bank()
            for t in range(NT):
                sc.op("pe", lambda e: e.matmul(self.ps[bg_][:, t * 16:(t + 1) * 16], lhsT=d["q"][:, t * 128:(t + 1) * 128], rhs=w["kmb"][:],
                                               start=True, stop=True, skip_group_check=True),
                      reads=[tk, w["km_tk"]], writes=[self.pst[bg_]], join=(t > 0))
            sc.op("dve", lambda e: e.tensor_tensor(out=w["gm"][:], in0=self.ps[bg_][:, 0:NF], in1=mot[:], op=ALU.add),
                  reads=[self.pst[bg_], mk_tk], writes=[w["g_tk"]])
            for t in range(NT):
                sc.op("dve", lambda e: e.max(out=w["m8"][:, t * 8:(t + 1) * 8], in_=w["gm"][:, t * 16:(t + 1) * 16]),
                      reads=[w["g_tk"]], writes=[w["g_tk"]])
            for t in range(NT):
                sc.op("dve", lambda e: e.tensor_scalar(out=w["ns"][:, t * 16:(t + 1) * 16], in0=w["gm"][:, t * 16:(t + 1) * 16],
                                                       scalar1=w["m8"][:, t * 8 + 2:t * 8 + 3], scalar2=-NEG, op0=ALU.is_ge, op1=ALU.mult),
                      reads=[w["g_tk"]], writes=[w["g_tk"]])
            sc.op("dve", lambda e: e.tensor_tensor(out=w["ns"][:], in0=w["ns"][:], in1=o1t[:], op=ALU.add),
                  reads=[w["g_tk"], mk_tk], writes=[w["g_tk"]])
            sc.op("dve", lambda e: e.tensor_tensor(out=w["ns2"][:], in0=w["ns"][:], in1=m2t[:], op=ALU.min),
                  reads=[w["g_tk"], mk_tk], writes=[w["g_tk"]])
            for t0 in range(0, NT, 4):
                b2 = self.bank()
                n4 = min(4, NT - t0)
                for t in range(t0, t0 + n4):
                    sc.op("pe", lambda e: e.matmul(self.ps[b2][0:16, (t - t0) * 128:(t - t0 + 1) * 128], lhsT=w["ns2"][:, t * 16:(t + 1) * 16],
                                                   rhs=self.ident[:], start=True, stop=True, skip_group_check=True),
                          reads=[w["g_tk"], self.ident_t], writes=[self.pst[b2]], join=(t > t0))
                sc.op("act", lambda e: e.activation(out=w["nsT"][:, t0 * 128:(t0 + n4) * 128], in_=self.ps[b2][0:16, 0:n4 * 128], func=AF.Copy),
                      reads=[self.pst[b2]], writes=[w["nsT_tk"]], join=(t0 > 0))
            sc.dma("sp", w["nsT_tk"], ns_d[h, :, :], w["nsT"][:], reads=[w["nsT_tk"]])
        self.phase_barrier([w["nsT_tk"] for w in wk] + [d["tk"] for d in slots])

    def phase_moba(self, qkT_d, vp_d, ns_d, eblk_d, y0_d, c, wjobs=None):
        sc, S, NT = self.sc, self.S, self.NT
        self.sb_reset()
        slots = []
        for sl in range(2):
            d = {"q": self.sb([128, S], BF16), "k": self.sb([128, S], BF16), "V": self.sb([128, NT, 65], BF16), "tk": Tk()}
            sc.op("pool", lambda e: e.memset(d["q"][64:128, :], 0.0), writes=[d["tk"]])
            sc.op("pool", lambda e: e.memset(d["k"][64:128, :], 0.0), writes=[d["tk"]], join=True)
            sc.dma("sp", d["tk"], d["k"][64:80, :], eblk_d[:, :], writes=[d["tk"]])
            slots.append(d)

        def load_head(h, sl):
            d = slots[sl]
            tk = d["tk"]
            sc.dma("sp", tk, d["q"][0:64, :], qkT_d[h * 64:(h + 1) * 64, :], writes=[tk])
            sc.dma("sp", tk, d["k"][0:64, :], qkT_d[512 + h * 64:512 + (h + 1) * 64, :], writes=[tk], join=True)
            sc.dma("sp", tk, d["q"][64:80, :], ns_d[h, :, :], writes=[tk], join=True)
            sc.dma("sp", tk, d["V"][:], vp_d[:, h * 65:(h + 1) * 65].rearrange("(t p) d -> p t d", p=128), writes=[tk], join=True)
            return {"q": d["q"], "k": d["k"], "V": d["V"], "tks": [tk]}

        def score_mm(hd, kt, q0, q1, j):
            return hd["k"][:, kt * 128:(kt + 1) * 128], hd["q"][:, q0:q1]

        wtks = self.weight_preconvert(wjobs) if wjobs else []
        out_tks = self.attn_core(8, 4 if NT >= 4 else NT, load_head, score_mm, y0_d, c, bias=False, selmask=False, ycol0=1024)
        self.phase_barrier(out_tks + [d["tk"] for d in slots] + wtks)

    def phase_barrier(self, tks):
        for en in ("pe", "act", "dve", "pool", "sp"):
            self.sc.wait_all(en, tks)
            self.sc.wait_all(en, self.pst)
        self.sc.end_phase()


def host_consts(S=S_FULL):
    i = np.arange(128)
    tri = np.where(i[:, None] > i[None, :], NEG, 0.0).astype(ml_dtypes.bfloat16)
    onehot = np.zeros((16, 16, 128), dtype=ml_dtypes.bfloat16)
    for b in range(16):
        onehot[b, b, :] = 1
    p = np.arange(128)
    dd = p % 64
    jj = dd % 32
    inv = np.power(np.float32(10000.0), -(jj.astype(np.float32)) / np.float32(32)).astype(np.float32)
    ang = (np.arange(S, dtype=np.float32)[None, :] * inv[:, None]).astype(np.float32)
    cosT = np.cos(ang).astype(np.float32)
    sinT = (np.sin(ang) * np.where(dd < 32, -1.0, 1.0)[:, None]).astype(np.float32)
    own = np.arange(16)[:, None]
    blk = np.arange(16)[None, :]
    mo = np.broadcast_to(np.where(blk >= own, -1e30, 0.0).astype(np.float32)[None], (128, 16, 16)).copy()
    m2 = np.broadcast_to(np.where(blk >= own, NEG, 0.0).astype(np.float32)[None], (128, 16, 16)).copy()
    eblk = (np.arange(16)[:, None] == (np.arange(S)[None, :] // 256)).astype(ml_dtypes.bfloat16)
    NT_ = S // 128
    own_t = (np.arange(NT_) // 2)[:, None]
    mot = np.zeros((128, 512), np.float32)
    m2t = np.zeros((128, 512), np.float32)
    mot[:, :NT_ * 16] = np.where(blk >= own_t, -1e30, 0.0).astype(np.float32).reshape(1, -1)
    m2t[:, :NT_ * 16] = np.where(blk > own_t, NEG, 0.0).astype(np.float32).reshape(1, -1)
    o1t = np.zeros((128, 512), np.float32)
    o1t[:, :NT_ * 16] = np.where(blk == own_t, 0.0, NEG).astype(np.float32).reshape(1, -1)
    return {
        "c_mot": mot, "c_m2t": m2t, "c_o1t": o1t,
        "c_eblk": eblk,
        "c_cosT": cosT, "c_sinT": sinT, "c_mo": mo, "c_m2": m2,
        "ident": np.eye(128, dtype=ml_dtypes.bfloat16),
        "c_ident32": np.eye(128, dtype=np.float32),
        "c_triu": (i[:, None] <= i[None, :]).astype(np.float32),
        "c_ones32": np.ones((128, 128), np.float32),
        "c_sel127": np.where(i[:, None] == 127, 1.0, 0.0).astype(np.float32) * np.ones((128, 128), np.float32),
        "c_tri": tri,
        "c_onehot": onehot,
    }


def build(S, phases, dbg=False):
    nc = bass.Bass("TRN2", target_bir_lowering=False)
    EI = "ExternalInput"
    SCR = "ExternalOutput" if dbg else "Internal"
    x_in = nc.dram_tensor("x", [S, D], F32, kind=EI).ap()
    ident_d = nc.dram_tensor("ident", [128, 128], BF16, kind=EI).ap()
    cd = {
        "ident32": (nc.dram_tensor("c_ident32", [128, 128], F32, kind=EI).ap(), [128, 128], F32),
        "triu": (nc.dram_tensor("c_triu", [128, 128], F32, kind=EI).ap(), [128, 128], F32),
        "ones32": (nc.dram_tensor("c_ones32", [128, 128], F32, kind=EI).ap(), [128, 128], F32),
        "sel127": (nc.dram_tensor("c_sel127", [128, 128], F32, kind=EI).ap(), [128, 128], F32),
        "tri": (nc.dram_tensor("c_tri", [128, 128], BF16, kind=EI).ap(), [128, 128], BF16),
        "onehot": (nc.dram_tensor("c_onehot", [16, 16, 128], BF16, kind=EI).ap(), [16, 16, 128], BF16),
    }
    cos_d = nc.dram_tensor("c_cosT", [128, S], F32, kind=EI).ap()
    sin_d = nc.dram_tensor("c_sinT", [128, S], F32, kind=EI).ap()
    mo_d = nc.dram_tensor("c_mo", [128, 16, 16], F32, kind=EI).ap()
    m2_d = nc.dram_tensor("c_m2", [128, 16, 16], F32, kind=EI).ap()
    eblk_d = nc.dram_tensor("c_eblk", [16, S], BF16, kind=EI).ap()
    mot_d = nc.dram_tensor("c_mot", [128, 512], F32, kind=EI).ap()
    m2t_d = nc.dram_tensor("c_m2t", [128, 512], F32, kind=EI).ap()
    o1t_d = nc.dram_tensor("c_o1t", [128, 512], F32, kind=EI).ap()
    nsd = nc.dram_tensor("nsd", [8, 16, S], BF16, kind=SCR).ap()
    norm_mix_even = nc.dram_tensor("norm_mix_even", [1, D], F32, kind=EI).ap()
    w_in_even = nc.dram_tensor("w_in_even", [1, D, 4624], F32, kind=EI).ap()
    conv_w = nc.dram_tensor("conv_w", [1, 4, 2048], F32, kind=EI).ap()
    conv_b = nc.dram_tensor("conv_b", [1, 2048], F32, kind=EI).ap()
    dt_bias = nc.dram_tensor("dt_bias", [1, 16], F32, kind=EI).ap()
    a_log = nc.dram_tensor("a_log", [1, 16], F32, kind=EI).ap()
    d_skip_rep = nc.dram_tensor("d_skip_rep", [1, 1024], F32, kind=EI).ap()
    ssd_gate_norm = nc.dram_tensor("ssd_gate_norm", [1, 1024], F32, kind=EI).ap()
    w_out_even = nc.dram_tensor("w_out_even", [1, 1536, D], F32, kind=EI).ap()
    xbcT = nc.dram_tensor("xbcT", [2048, S], BF16, kind=SCR).ap()
    qkT0 = nc.dram_tensor("qkT0", [1024, S], BF16, kind=SCR).ap()
    zs = nc.dram_tensor("zs", [S, 1024], BF16, kind=SCR).ap()
    vp0 = nc.dram_tensor("vp0", [S, 8 * 65], BF16, kind=SCR).ap()
    RL0 = nc.dram_tensor("RL0", [16, 12, S], BF16, kind=SCR).ap()
    xs0 = nc.dram_tensor("xs0", [S, 1024], BF16, kind=SCR).ap()
    xc0 = nc.dram_tensor("xc0", [S, 1024], BF16, kind=SCR).ap()
    ysc = nc.dram_tensor("ysc", [S, 1024], BF16, kind=SCR).ap()
    y0 = nc.dram_tensor("y0", [S, 1536], BF16, kind=SCR).ap()
    btok = nc.dram_tensor("btok", [S, 512], BF16, kind=SCR).ap()
    awd = nc.dram_tensor("awd", [S, 16], F32, kind=SCR).ap()
    wub = nc.dram_tensor("wub", [2, D, DFF], BF16, kind="Internal").ap()
    wdb = nc.dram_tensor("wdb", [2, DFF, D], BF16, kind="Internal").ap()
    PRECONV = os.environ.get("PRECONV", "1") == "1"
    done_conv = set()
    norm_mlp = nc.dram_tensor("norm_mlp", [2, D], F32, kind=EI).ap()
    w_up = nc.dram_tensor("w_up", [2, D, DFF], F32, kind=EI).ap()
    w_down = nc.dram_tensor("w_down", [2, DFF, D], F32, kind=EI).ap()
    final_norm = nc.dram_tensor("final_norm", [1, D], F32, kind=EI).ap()
    norm_mix_odd = nc.dram_tensor("norm_mix_odd", [1, D], F32, kind=EI).ap()
    w_in_odd = nc.dram_tensor("w_in_odd", [1, D, 3088], F32, kind=EI).ap()
    fgate_bias = nc.dram_tensor("fgate_bias", [1, 16], F32, kind=EI).ap()
    w_out_odd = nc.dram_tensor("w_out_odd", [1, D, D], F32, kind=EI).ap()
    out = nc.dram_tensor("out", [S, D], F32, kind="ExternalOutput").ap()
    xs = nc.dram_tensor("xs", [S, D], F32, kind=SCR).ap()
    qkT = nc.dram_tensor("qkT", [2048, S], BF16, kind=SCR).ap()
    vp = nc.dram_tensor("vp", [S, 16 * 65], BF16, kind=SCR).ap()
    RL = nc.dram_tensor("RL", [16, 12, S], BF16, kind=SCR).ap()
    y1 = nc.dram_tensor("y1", [S, D], BF16, kind=SCR).ap()
    kb = KB(nc, S)
    kb.setup_consts(ident_d, cd)
    c = kb.c
    cur = x_in
    for ph in phases:
        if ph == "mlp0":
            kb.phase_mlp(cur, norm_mlp[0:1, :], w_up[0], w_down[0], xout_dram=xs, wbf=(wub[0], wdb[0]) if 0 in done_conv else None)
            cur = xs
        elif ph == "mlp1":
            kb.phase_mlp(cur, norm_mlp[1:2, :], w_up[1], w_down[1], xout_dram=xs, wbf=(wub[1], wdb[1]) if 1 in done_conv else None)
            cur = xs
        elif ph == "l0":
            LS = int(os.environ.get("L0_STOP", "9"))
            kb.phase_l0_proj(cur, norm_mix_even[0:1, :], w_in_even[0], conv_w[0], conv_b, dt_bias[0:1, :], a_log[0:1, :], cos_d, sin_d,
                             xbcT, qkT0, zs, vp0, RL0, c, aw_d=awd)
            if LS >= 2:
                kb.phase_ssd_prep(xbcT, xs0, xc0, btok_d=btok)
            if LS >= 3:
                if os.environ.get("SSD_QUAD", "0") == "1":
                    kb.phase_ssd_attn(xbcT, xc0, RL0, ysc, c)
                else:
                    kb.phase_ssd_chunk(xbcT, btok, xc0, RL0, awd, ysc, c)
            if LS >= 4:
                kb.phase_ssd_post(ysc, xs0, zs, d_skip_rep[0:1, :], ssd_gate_norm[0:1, :], y0)
            if LS >= 5:
                kb.phase_moba_gate(qkT0, mot_d, m2t_d, o1t_d, nsd, c)
                wj = [(w_up[0], wub[0], 2048), (w_down[0], wdb[0], 1024)] if (PRECONV and "mlp0" in phases) else None
                kb.phase_moba(qkT0, vp0, nsd, eblk_d, y0, c, wjobs=wj)
                if wj:
                    done_conv.add(0)
            if LS >= 6:
                kb.phase_out_proj(cur, y0, w_out_even[0], 12, xout_dram=xs)
                cur = xs
        elif ph == "fox":
            kb.phase_fox_proj(cur, norm_mix_odd[0:1, :], w_in_odd[0], fgate_bias[0:1, :], qkT, vp, RL, c)
            FS = int(os.environ.get("FOX_STOP", "3"))
            if FS >= 2:
                wj = [(w_up[1], wub[1], 2048), (w_down[1], wdb[1], 1024)] if (PRECONV and "mlp1" in phases) else None
                kb.phase_fox_attn(qkT, vp, RL, y1, c, wjobs=wj)
                if wj:
                    done_conv.add(1)
            if FS >= 3:
                kb.phase_out_proj(cur, y1, w_out_odd[0], 8, xout_dram=xs)
                cur = xs
        elif ph == "final":
            kb.phase_final_norm(cur, final_norm[0:1, :], out)
    print("instructions:", kb.sc.nins, "waits:", kb.sc.nwait, "sems left:", len(kb.sc.pool))
    return nc


PHASES = ["l0", "mlp0", "fox", "mlp1", "final"]
_NC_CACHE = {}


def kernel(**inputs):
    S = S_FULL
    x = np.ascontiguousarray(np.asarray(inputs["x"], dtype=np.float32))
    B = x.shape[0]
    if "nc" not in _NC_CACHE:
        _NC_CACHE["nc"] = build(S, PHASES)
    nc = _NC_CACHE["nc"]
    shared = dict(host_consts(S))
    f = lambda k: np.ascontiguousarray(np.asarray(inputs[k], dtype=np.float32))
    for k in ("norm_mix_even", "w_in_even", "conv_w", "conv_b", "dt_bias", "a_log", "ssd_gate_norm", "w_out_even",
              "norm_mix_odd", "w_in_odd", "fgate_bias", "w_out_odd", "norm_mlp", "w_up", "w_down"):
        shared[k] = f(k)
    shared["final_norm"] = f("final_norm").reshape(1, D)
    shared["d_skip_rep"] = np.ascontiguousarray(np.repeat(f("d_skip"), 64, axis=1))
    in_maps = []
    for b in range(B):
        m = dict(shared)
        m["x"] = x[b]
        in_maps.append(m)
    res = run_bass_kernel_spmd(nc, in_maps, core_ids=list(range(B)))
    return np.stack([np.asarray(r["out"], dtype=np.float32) for r in res.results], axis=0)
```

```python
import math
import os
import numpy as np
import ml_dtypes
import concourse.bass as bass
import concourse.mybir as mybir
from concourse.bass_utils import run_bass_kernel_spmd

F32 = mybir.dt.float32
BF16 = mybir.dt.bfloat16
ALU = mybir.AluOpType
AF = mybir.ActivationFunctionType
AX = mybir.AxisListType

D = 1024
S_FULL = 4096
DFF = 4096
EPS = 1e-5
NEG = -30000.0


class Tk:
    __slots__ = ("w", "r", "name", "excl")

    def __init__(self, name="", excl=False):
        self.w = {}
        self.r = {}
        self.name = name
        self.excl = excl


class Sched:
    CH = 24000

    def __init__(self, nc):
        self.nc = nc
        self.h = {"pe": nc.tensor, "act": nc.scalar, "dve": nc.vector, "pool": nc.gpsimd, "sp": nc.sync}
        self.sems = {k: [] for k in self.h}
        self.cnt = {k: 0 for k in self.h}
        self.seen = {k: {} for k in self.h}
        self.pool = [nc.alloc_semaphore(f"s{i}") for i in range(96)]
        self.dma_free = []
        self.dma_sems = {}
        self.nwait = 0
        self.nins = 0

    def _next_tok(self, en):
        i = self.cnt[en]
        if i % self.CH == 0:
            self.sems[en].append(self.pool.pop())
        self.cnt[en] = i + 1
        return (self.sems[en][-1], i % self.CH + 1, en)

    def _dma_ent(self, key):
        if key not in self.dma_sems:
            if self.dma_free:
                self.dma_sems[key] = self.dma_free.pop()
            else:
                self.dma_sems[key] = [self.pool.pop(), 0]
        return self.dma_sems[key]

    def end_phase(self):
        for ent in self.dma_sems.values():
            if ent[1] < 20000:
                self.dma_free.append(ent)
        self.dma_sems = {}

    def _wait(self, en, tok):
        sem, val, src = tok
        if src == en and en == "pe":
            return
        sid = id(sem)
        if self.seen[en].get(sid, 0) >= val:
            return
        self.h[en].wait_ge(sem, val)
        self.seen[en][sid] = val
        self.nwait += 1

    def _deps(self, en, reads, writes, join):
        for t in reads:
            for tok in t.w.values():
                self._wait(en, tok)
            if t.excl:
                for tok in t.r.values():
                    if tok[2] != en:
                        self._wait(en, tok)
        for t in writes:
            if not join:
                for tok in t.w.values():
                    self._wait(en, tok)
            for tok in t.r.values():
                self._wait(en, tok)

    def _record(self, tok, reads, writes, join):
        sid = id(tok[0])
        for t in reads:
            t.r[sid] = tok
        for t in writes:
            if join:
                t.w[sid] = tok
            else:
                t.w = {sid: tok}
                t.r = {}

    def op(self, en, fn, reads=(), writes=(), join=False):
        self._deps(en, reads, writes, join)
        ins = fn(self.h[en])
        tok = self._next_tok(en)
        ins.then_inc(tok[0], 1)
        self._record(tok, reads, writes, join)
        self.nins += 1
        return tok

    def dma(self, en, key, out, in_, reads=(), writes=(), join=False, **kw):
        self._deps(en, reads, writes, join)
        ent = self._dma_ent(key)
        ins = self.h[en].dma_start(out=out, in_=in_, **kw)
        ent[1] += 16
        ins.then_inc(ent[0], 16)
        tok = (ent[0], ent[1], "dma")
        self._record(tok, reads, writes, join)
        self.nins += 1
        return tok

    def wait_all(self, en, tks):
        for t in tks:
            for tok in list(t.w.values()) + list(t.r.values()):
                self._wait(en, tok)


class KB:
    def __init__(self, nc, S):
        self.nc = nc
        self.S = S
        self.NT = S // 128
        self.sc = Sched(nc)
        self.uid = 0
        self.ps = [nc.alloc_psum_tensor(f"ps{i}", [128, 512], F32) for i in range(8)]
        self.pst = [Tk(f"ps{i}", excl=True) for i in range(8)]
        self.ps_rr = 0
        self.SB_BYTES = 207 * 1024
        self.big = nc.alloc_sbuf_tensor("big", [128, self.SB_BYTES // 4], F32)
        self.sb_off = 0
        self.sb_base = 0

    def sb(self, shape, dt, name=None):
        esz = 4 if dt == F32 else 2
        n = 1
        for d_ in shape[1:]:
            n *= d_
        nbytes = (n * esz + 31) // 32 * 32
        off = self.sb_off
        self.sb_off += nbytes
        assert self.sb_off <= self.SB_BYTES, f"SBUF overflow {self.sb_off}"
        ap = self.big[0:shape[0], off // 4:(off + nbytes) // 4]
        if dt != F32:
            ap = ap.bitcast(dt)
        ap = ap[:, 0:n]
        if len(shape) == 3:
            ap = ap.rearrange("p (a b) -> p a b", a=shape[1])
        elif len(shape) == 4:
            ap = ap.rearrange("p (a b c) -> p a b c", a=shape[1], b=shape[2])
        return ap

    def sb_reset(self):
        self.sb_off = self.sb_base

    def bank(self):
        i = self.ps_rr
        self.ps_rr = (i + 1) % 8
        return i

    def setup_consts(self, ident_d, consts=None):
        nc, sc = self.nc, self.sc
        self.ident = self.sb([128, 128], BF16, "ident")
        self.ident_t = Tk("ident")
        sc.dma("sp", "const_ident", self.ident[:], ident_d[:, :], writes=[self.ident_t])
        c = {"tk": Tk()}
        for nm, (ap_d, shape, dt) in (consts or {}).items():
            c[nm] = self.sb(shape, dt)
            sc.dma("sp", "const", c[nm][:], ap_d, writes=[c["tk"]], join=True)
        self.c = c
        self.sb_base = self.sb_off

    def load_weight(self, w_sb, w_tk, w_dram, K, N, stage, stage_tk, col_chunk=2048, dram_col0=0, sb_col0=0):
        sc = self.sc
        kt = K // 128
        i = 0
        for k in range(kt):
            for c0 in range(0, N, col_chunk):
                cw = min(col_chunk, N - c0)
                st, stk = stage[i % len(stage)], stage_tk[i % len(stage)]
                sc.dma("sp", stk, st[:, 0:cw],
                       w_dram[k * 128:(k + 1) * 128, dram_col0 + c0:dram_col0 + c0 + cw], writes=[stk])
                en = "pool" if i % 2 == 0 else "act"
                if en == "pool":
                    sc.op("pool", lambda e, st=st, k=k, c0=c0, cw=cw: e.tensor_copy(
                        out=w_sb[:, k, sb_col0 + c0:sb_col0 + c0 + cw], in_=st[:, 0:cw]),
                        reads=[stk], writes=[w_tk], join=True)
                else:
                    sc.op("act", lambda e, st=st, k=k, c0=c0, cw=cw: e.activation(
                        out=w_sb[:, k, sb_col0 + c0:sb_col0 + c0 + cw], in_=st[:, 0:cw], func=AF.Copy),
                        reads=[stk], writes=[w_tk], join=True)
                i += 1

    def rstd(self, st, st_tk):
        sc = self.sc
        sc.op("dve", lambda e: e.tensor_scalar_add(out=st[:, 1:2], in0=st[:, 0:1], scalar1=EPS), reads=[st_tk], writes=[st_tk])
        sc.op("act", lambda e: e.activation(out=st[:, 3:4], in_=st[:, 1:2], func=AF.Sqrt), reads=[st_tk], writes=[st_tk])
        sc.op("dve", lambda e: e.reciprocal(out=st[:, 2:3], in_=st[:, 3:4]), reads=[st_tk], writes=[st_tk])

    def norm_tile(self, x_dram_rows, g_sb, g_tk, bufs, hT, hT_tk, col0, idx, preloaded=False):
        sc = self.sc
        xin, xin_tk = bufs["xin"][idx % len(bufs["xin"])]
        hb, hb_tk = bufs["hb"][idx % len(bufs["hb"])]
        st, st_tk = bufs["st"][idx % len(bufs["st"])]
        junk, junk_tk = bufs["junk"]
        if not preloaded:
            sc.dma("sp", xin_tk, xin[:], x_dram_rows, writes=[xin_tk])
        sc.op("act", lambda e: e.activation(out=junk[:], in_=xin[:], func=AF.Square, scale=float(1.0 / math.sqrt(D)), accum_out=st[:, 0:1]),
              reads=[xin_tk], writes=[junk_tk, st_tk])
        self.rstd(st, st_tk)
        sc.op("dve", lambda e: e.scalar_tensor_tensor(out=hb[:], in0=xin[:], scalar=st[:, 2:3], in1=g_sb[:],
                                                      op0=ALU.mult, op1=ALU.mult),
              reads=[xin_tk, st_tk, g_tk], writes=[hb_tk])
        b = self.bank()
        pt = self.ps[b][:].bitcast(BF16)
        for k in range(8):
            sc.op("pe", lambda e, k=k: e.transpose(out=pt[:, k * 128:(k + 1) * 128], in_=hb[:, k * 128:(k + 1) * 128],
                                                   identity=self.ident[:]),
                  reads=[hb_tk, self.ident_t], writes=[self.pst[b]], join=(k > 0))
        if idx % 2 == 0:
            sc.op("act", lambda e: e.activation(out=hT[:, 0:8, col0:col0 + 128],
                                                in_=pt.rearrange("p (k t) -> p k t", k=8), func=AF.Copy),
                  reads=[self.pst[b]], writes=[hT_tk], join=True)
        else:
            sc.op("dve", lambda e: e.tensor_copy(out=hT[:, 0:8, col0:col0 + 128], in_=pt.rearrange("p (k t) -> p k t", k=8)),
                  reads=[self.pst[b]], writes=[hT_tk], join=True)

    def phase_final_norm(self, x_dram, g_dram, out_dram):
        nc, sc = self.nc, self.sc
        if True:
            self.sb_reset()
            g_sb = self.sb([128, D], F32, "g")
            g_tk = Tk()
            sc.dma("sp", "gload", g_sb[:], g_dram.partition_broadcast(128), writes=[g_tk])
            NB = 3
            xin = [(self.sb([128, D], F32, "xin"), Tk()) for _ in range(NB)]
            xo = [(self.sb([128, D], F32, "xo"), Tk()) for _ in range(NB)]
            st = [(self.sb([128, 4], F32, "st"), Tk()) for _ in range(NB)]
            junk, junk_tk = self.sb([128, D], F32, "junk"), Tk()
            for t in range(self.NT):
                xi, xi_tk = xin[t % NB]
                xot, xo_tk = xo[t % NB]
                s_, s_tk = st[t % NB]
                if t == 0:
                    for tt in range(min(2, self.NT)):
                        sc.dma("sp", xin[tt % NB][1], xin[tt % NB][0][:], x_dram[tt * 128:(tt + 1) * 128, :], writes=[xin[tt % NB][1]])
                if t + 2 < self.NT:
                    sc.dma("sp", xin[(t + 2) % NB][1], xin[(t + 2) % NB][0][:], x_dram[(t + 2) * 128:(t + 3) * 128, :], writes=[xin[(t + 2) % NB][1]])
                sc.op("act", lambda e: e.activation(out=junk[:], in_=xi[:], func=AF.Square, scale=float(1.0 / math.sqrt(D)), accum_out=s_[:, 0:1]),
                      reads=[xi_tk], writes=[junk_tk, s_tk])
                self.rstd(s_, s_tk)
                sc.op("dve", lambda e: e.scalar_tensor_tensor(out=xot[:], in0=xi[:], scalar=s_[:, 2:3], in1=g_sb[:],
                                                              op0=ALU.mult, op1=ALU.mult),
                      reads=[xi_tk, s_tk, g_tk], writes=[xo_tk])
                sc.dma("sp", xo_tk, out_dram[t * 128:(t + 1) * 128, :], xot[:], reads=[xo_tk])
            self.phase_barrier([tk for _, tk in xo])

    def phase_mlp(self, x_dram, g_dram, wup_dram, wdn_dram, xout_dram=None, wbf=None):
        nc, sc = self.nc, self.sc
        xout_dram = x_dram if xout_dram is None else xout_dram
        if True:
            self.sb_reset()
            g_sb, g_tk = self.sb([128, D], F32, "g"), Tk()
            sc.dma("sp", "gload", g_sb[:], g_dram.partition_broadcast(128), writes=[g_tk])
            wu, wu_tk = self.sb([128, 8, DFF], BF16, "wu"), Tk()
            wd, wd_tk = self.sb([128, 32, D], BF16, "wd"), Tk()
            hmid, hmid_tk = self.sb([128, 32, 512], BF16, "hmid"), [Tk() for _ in range(32)]
            hm32 = hmid.rearrange("p a b -> p (a b)").bitcast(F32)
            stage = [hm32[:, i * 2048:(i + 1) * 2048] for i in range(4)]
            stage_tk = [Tk() for _ in range(4)]
            if wbf is None:
                self.load_weight(wu, wu_tk, wup_dram, D, DFF, stage, stage_tk)
                self.load_weight(wd, wd_tk, wdn_dram, DFF, D, stage, stage_tk, col_chunk=1024)
                for tk in hmid_tk:
                    for s in stage_tk:
                        tk.w.update(s.w)
                        tk.r.update(s.r)
            else:
                for k in range(8):
                    sc.dma("sp", wu_tk, wu[:, k, :], wbf[0][k * 128:(k + 1) * 128, :], writes=[wu_tk], join=(k > 0))
                for k in range(32):
                    sc.dma("sp", wd_tk, wd[:, k, :], wbf[1][k * 128:(k + 1) * 128, :], writes=[wd_tk], join=(k > 0))
            hT = [(self.sb([128, 8, 512], BF16, "hT"), Tk()) for _ in range(1)]
            import os
            STOP = int(os.environ.get("MLP_STOP", "9"))
            bufs = {
                "xin": [(self.sb([128, D], F32, "xin"), Tk()) for _ in range(2)],
                "hb": [(self.sb([128, D], BF16, "hb"), Tk()) for _ in range(2)],
                "st": [(self.sb([128, 4], F32, "st"), Tk()) for _ in range(2)],
                "junk": (self.sb([128, D], F32, "junk"), Tk()),
            }
            xres = [(self.sb([128, 512], F32, "xres"), Tk()) for _ in range(4)]
            NG = self.S // 512 if STOP >= 1 else 0
            for g in range(NG):
                hTg, hTg_tk = hT[0]
                for t in range(4):
                    r0 = g * 512 + t * 128
                    self.norm_tile(x_dram[r0:r0 + 128, :], g_sb, g_tk, bufs, hTg, hTg_tk, t * 128, g * 4 + t,
                                   preloaded=(g > 0 and t < 2))
                if STOP < 2:
                    continue
                for f in range(32):
                    b = self.bank()
                    for k in range(8):
                        sc.op("pe", lambda e, k=k, f=f, b=b: e.matmul(self.ps[b][:], lhsT=wu[:, k, f * 128:(f + 1) * 128],
                                                                       rhs=hTg[:, k, :], start=(k == 0), stop=(k == 7)),
                              reads=[wu_tk, hTg_tk], writes=[self.pst[b]], join=(k > 0))
                    sc.op("act", lambda e, f=f, b=b: e.activation(out=hmid[:, f, :], in_=self.ps[b][:], func=AF.Relu),
                          reads=[self.pst[b]], writes=[hmid_tk[f]])
                    sc.op("dve" if f % 2 == 0 else "pool", lambda e, f=f: e.tensor_tensor(
                        out=hmid[:, f, :], in0=hmid[:, f, :], in1=hmid[:, f, :], op=ALU.mult),
                        reads=[hmid_tk[f]], writes=[hmid_tk[f]])
                if STOP < 3:
                    continue
                if g + 1 < NG:
                    for t in range(2):
                        r1 = (g + 1) * 512 + t * 128
                        xi_, xi_tk_ = bufs["xin"][((g + 1) * 4 + t) % len(bufs["xin"])]
                        sc.dma("sp", xi_tk_, xi_[:], x_dram[r1:r1 + 128, :], writes=[xi_tk_])

                def xr_load(p):
                    t_, c_ = p // 2, p % 2
                    xr_, xr_tk_ = xres[p % 4]
                    sc.dma("sp", xr_tk_, xr_[:], x_dram[g * 512 + t_ * 128:g * 512 + (t_ + 1) * 128, c_ * 512:(c_ + 1) * 512], writes=[xr_tk_])

                xr_load(0)
                for t in range(4):
                    r0 = g * 512 + t * 128
                    for c in range(2):
                        xr, xr_tk = xres[(t * 2 + c) % 4]
                        if t * 2 + c + 1 < 8:
                            xr_load(t * 2 + c + 1)
                        b = self.bank()
                        for f in range(32):
                            sc.op("pe", lambda e, f=f, t=t, c=c, b=b: e.matmul(
                                self.ps[b][:], lhsT=hmid[:, f, t * 128:(t + 1) * 128], rhs=wd[:, f, c * 512:(c + 1) * 512],
                                start=(f == 0), stop=(f == 31)),
                                reads=[wd_tk, hmid_tk[f]], writes=[self.pst[b]], join=(f > 0))
                        sc.op("dve", lambda e, b=b, xr=xr: e.tensor_tensor(out=xr[:], in0=self.ps[b][:], in1=xr[:], op=ALU.add),
                              reads=[self.pst[b], xr_tk], writes=[xr_tk])
                        sc.dma("sp", xr_tk, xout_dram[r0:r0 + 128, c * 512:(c + 1) * 512], xr[:], reads=[xr_tk])
            self.phase_barrier([tk for _, tk in xres])

    def norm_all(self, x_dram, g_dram):
        sc = self.sc
        g_sb, g_tk = self.sb([128, D], F32, "g"), Tk()
        sc.dma("sp", g_tk, g_sb[:], g_dram.partition_broadcast(128), writes=[g_tk])
        hT, hT_tk = self.sb([128, 8, self.S], BF16, "hTall"), Tk()
        save = self.sb_off
        bufs = {
            "xin": [(self.sb([128, D], F32), Tk()) for _ in range(2)],
            "hb": [(self.sb([128, D], BF16), Tk()) for _ in range(2)],
            "st": [(self.sb([128, 4], F32), Tk()) for _ in range(2)],
            "junk": (self.sb([128, D], F32), Tk()),
        }
        for t in range(self.NT):
            self.norm_tile(x_dram[t * 128:(t + 1) * 128, :], g_sb, g_tk, bufs, hT, hT_tk, t * 128, t)
        self.norm_bufs_tks = [tk for _, tk in bufs["xin"]] + [tk for _, tk in bufs["hb"]] + [bufs["junk"][1]]
        return hT, hT_tk, save

    def load_wcols(self, w_dram, c0, ncols, wt, wt_tk, stg, stg_tk, perm_heads=False):
        sc = self.sc
        if not perm_heads:
            sc.dma("sp", stg_tk, stg[:, :, 0:ncols], w_dram[:, c0:c0 + ncols].rearrange("(k p) c -> p k c", p=128),
                   writes=[stg_tk])
        else:
            first = True
            for hh in range(ncols // 64):
                for half in range(2):
                    src = w_dram[:, c0 + hh * 64 + (1 - half) * 32:c0 + hh * 64 + (1 - half) * 32 + 32]
                    sc.dma("sp", stg_tk, stg[:, :, hh * 64 + half * 32:hh * 64 + half * 32 + 32],
                           src.rearrange("(k p) c -> p k c", p=128), writes=[stg_tk], join=not first)
                    first = False
        sc.op("pool", lambda e: e.tensor_copy(out=wt[:, :, 0:ncols], in_=stg[:, :, 0:ncols]), reads=[stg_tk], writes=[wt_tk])

    def phase_fox_proj(self, x_dram, g_dram, w_in, fb_dram, qkT_d, vp_d, RL_d, c):
        sc, S, NT = self.sc, self.S, self.NT
        self.sb_reset()
        nl, nl_tk = self.sb([128, NT, 16], F32), Tk()
        hT, hT_tk, _ = self.norm_all(x_dram, g_dram)
        NG = S // 512
        wts = [(self.sb([128, 8, 128], BF16), Tk()) for _ in range(2)]
        stgs = [(self.sb([128, 8, 128], F32), Tk()) for _ in range(2)]
        rows = [(self.sb([128, S], BF16), Tk()) for _ in range(2)]
        out_tks = []
        self.load_wcols(w_in, 0, 128, wts[0][0], wts[0][1], stgs[0][0], stgs[0][1])
        for f in range(16):
            wt, wt_tk = wts[f % 2]
            row, row_tk = rows[f % 2]
            if f + 1 < 16:
                self.load_wcols(w_in, (f + 1) * 128, 128, wts[(f + 1) % 2][0], wts[(f + 1) % 2][1], stgs[(f + 1) % 2][0], stgs[(f + 1) % 2][1])
            for tg in range(NG):
                b = self.bank()
                for k in range(8):
                    sc.op("pe", lambda e: e.matmul(self.ps[b][:], lhsT=wt[:, k, :], rhs=hT[:, k, tg * 512:(tg + 1) * 512],
                                                   start=(k == 0), stop=(k == 7)),
                          reads=[wt_tk, hT_tk], writes=[self.pst[b]], join=(k > 0))
                sc.op("act", lambda e: e.activation(out=row[:, tg * 512:(tg + 1) * 512], in_=self.ps[b][:], func=AF.Copy,
                                                    scale=(0.125 if f < 8 else 1.0)),
                      reads=[self.pst[b]], writes=[row_tk], join=(tg > 0))
            sc.dma("sp", row_tk, qkT_d[f * 128:(f + 1) * 128, :], row[:], reads=[row_tk])
            out_tks.append(row_tk)
        wv, wv_tk = self.sb([128, 8, 1024], BF16), Tk()
        stv = [(self.sb([128, 8, 512], F32), Tk()) for _ in range(1)]
        for cc in range(2):
            sc.dma("sp", stv[0][1], stv[0][0][:], w_in[:, 2048 + cc * 512:2048 + (cc + 1) * 512].rearrange("(k p) c -> p k c", p=128),
                   writes=[stv[0][1]])
            sc.op("pool", lambda e: e.tensor_copy(out=wv[:, :, cc * 512:(cc + 1) * 512], in_=stv[0][0][:]),
                  reads=[stv[0][1]], writes=[wv_tk], join=(cc > 0))
        vts = [(self.sb([128, 16, 65], BF16), Tk()) for _ in range(2)]
        for vt, vt_tk in vts:
            sc.op("pool", lambda e: e.memset(vt[:], 1.0), writes=[vt_tk])
        for t in range(NT):
            vt, vt_tk = vts[t % 2]
            for cc in range(2):
                b = self.bank()
                for k in range(8):
                    sc.op("pe", lambda e: e.matmul(self.ps[b][:], lhsT=hT[:, k, t * 128:(t + 1) * 128], rhs=wv[:, k, cc * 512:(cc + 1) * 512],
                                                   start=(k == 0), stop=(k == 7)),
                          reads=[wv_tk, hT_tk], writes=[self.pst[b]], join=(k > 0))
                sc.op("act", lambda e: e.activation(out=vt[:, cc * 8:(cc + 1) * 8, 0:64],
                                                    in_=self.ps[b][:].rearrange("p (h d) -> p h d", h=8), func=AF.Copy),
                      reads=[self.pst[b]], writes=[vt_tk], join=(cc > 0))
            sc.dma("sp", vt_tk, vp_d[t * 128:(t + 1) * 128, :], vt.rearrange("p h d -> p (h d)"), reads=[vt_tk])
            out_tks.append(vt_tk)
        wf, wf_tk = self.sb([128, 8, 16], BF16), Tk()
        stf, stf_tk = self.sb([128, 8, 16], F32), Tk()
        self.load_wcols(w_in, 3072, 16, wf, wf_tk, stf, stf_tk)
        fb, fb_tk = self.sb([128, 16], F32), Tk()
        sc.dma("sp", fb_tk, fb[:], fb_dram.partition_broadcast(128), writes=[fb_tk])
        tmp, tmp_tk = self.sb([128, NT, 16], F32), Tk()
        for t in range(NT):
            b = self.bank()
            for k in range(8):
                sc.op("pe", lambda e: e.matmul(self.ps[b][:, 0:16], lhsT=hT[:, k, t * 128:(t + 1) * 128], rhs=wf[:, k, :],
                                               start=(k == 0), stop=(k == 7)),
                      reads=[wf_tk, hT_tk], writes=[self.pst[b]], join=(k > 0))
            sc.op("dve", lambda e: e.tensor_tensor(out=tmp[:, t, :], in0=self.ps[b][:, 0:16], in1=fb[:], op=ALU.add),
                  reads=[self.pst[b], fb_tk], writes=[tmp_tk], join=True)
        sc.op("act", lambda e: e.activation(out=tmp[:], in_=tmp[:], func=AF.Exp, scale=-1.0), reads=[tmp_tk], writes=[tmp_tk])
        sc.op("dve", lambda e: e.tensor_scalar_add(out=tmp[:], in0=tmp[:], scalar1=1.0), reads=[tmp_tk], writes=[tmp_tk])
        sc.op("act", lambda e: e.activation(out=nl[:], in_=tmp[:], func=AF.Ln), reads=[tmp_tk], writes=[nl_tk])
        self.phase_barrier(out_tks + [nl_tk])
        self.sb_reset()
        nl, nl_tk = self.sb([128, NT, 16], F32), Tk()
        out_tks = []
        self.cumsum_rows(nl, nl_tk, RL_d, c, out_tks)
        self.phase_barrier(out_tks)

    def cumsum_rows(self, nl, nl_tk, RL_d, c, out_tks, aw_d=None):
        sc, S, NT = self.sc, self.S, self.NT
        runs, runs_tk = self.sb([128, NT, 16], F32), Tk()
        sc.op("dve", lambda e: e.memset(runs[:, 0, :], 0.0), writes=[runs_tk])
        for t in range(1, NT):
            sc.op("dve", lambda e: e.tensor_tensor(out=runs[:, t, :], in0=runs[:, t - 1, :], in1=nl[:, t - 1, :], op=ALU.add),
                  reads=[nl_tk, runs_tk], writes=[runs_tk])
        cT, cT_tk = self.sb([16, S], F32), Tk()
        cn, cn_tk = self.sb([128, 32], F32), [Tk() for _ in range(2)]
        cn2 = self.sb([128, 32], F32)
        cns = [cn, cn2]
        for t in range(NT):
            b = self.bank()
            if aw_d is not None:
                sc.op("pe", lambda e: e.matmul(self.ps[b][:, 16:32], lhsT=c["triu"][:], rhs=nl[:, t, :], start=True, stop=True),
                      reads=[nl_tk, c["tk"]], writes=[self.pst[b]])
            sc.op("pe", lambda e: e.matmul(self.ps[b][:, 0:16], lhsT=c["triu"][:], rhs=nl[:, t, :], start=True, stop=False),
                  reads=[nl_tk, c["tk"]], writes=[self.pst[b]], join=(aw_d is not None))
            sc.op("pe", lambda e: e.matmul(self.ps[b][:, 0:16], lhsT=c["ones32"][:], rhs=runs[:, t, :], start=False, stop=True),
                  reads=[runs_tk, c["tk"]], writes=[self.pst[b]], join=True)
            cur, cur_tk = cns[t % 2], cn_tk[t % 2]
            ncp = 32 if aw_d is not None else 16
            sc.op("dve", lambda e: e.tensor_copy(out=cur[:, 0:ncp], in_=self.ps[b][:, 0:ncp]), reads=[self.pst[b]], writes=[cur_tk])
            if aw_d is not None:
                sc.dma("sp", cur_tk, aw_d[t * 128:(t + 1) * 128, :], cur[:, 16:32], reads=[cur_tk])
            b2 = self.bank()
            sc.op("pe", lambda e: e.transpose(out=self.ps[b2][0:16, 0:128], in_=cur[:, 0:16], identity=c["ident32"][:]),
                  reads=[cur_tk, c["tk"]], writes=[self.pst[b2]])
            sc.op("act", lambda e: e.activation(out=cT[:, t * 128:(t + 1) * 128], in_=self.ps[b2][0:16, 0:128], func=AF.Copy),
                  reads=[self.pst[b2]], writes=[cT_tk], join=True)
        pb = [(self.sb([16, S], BF16), Tk()) for _ in range(3)]
        nb = [(self.sb([16, S], BF16), Tk()) for _ in range(3)]
        f32a, f32a_tk = self.sb([16, S], F32), Tk()
        rem, rem_tk = cT, cT_tk
        for i in range(3):
            p_, p_tk = pb[i]
            n_, n_tk = nb[i]
            sc.op("dve", lambda e: e.tensor_copy(out=p_[:], in_=rem[:]), reads=[rem_tk], writes=[p_tk])
            sc.op("act", lambda e: e.activation(out=n_[:], in_=p_[:], func=AF.Copy, scale=-1.0), reads=[p_tk], writes=[n_tk])
            if i < 2:
                sc.op("pool", lambda e: e.tensor_copy(out=f32a[:], in_=p_[:]), reads=[p_tk], writes=[f32a_tk])
                sc.op("dve", lambda e: e.tensor_tensor(out=rem[:], in0=rem[:], in1=f32a[:], op=ALU.subtract),
                      reads=[rem_tk, f32a_tk], writes=[rem_tk])
        onesb, onesb_tk = self.sb([16, S], BF16), Tk()
        sc.op("pool", lambda e: e.memset(onesb[:], 1.0), writes=[onesb_tk])
        for i in range(3):
            sc.dma("sp", nb[i][1], RL_d[:, i, :], nb[i][0][:], reads=[nb[i][1]])
            sc.dma("sp", pb[i][1], RL_d[:, 9 + i, :], pb[i][0][:], reads=[pb[i][1]])
            sc.dma("sp", onesb_tk, RL_d[:, 3 + i, :], onesb[:], reads=[onesb_tk])
            sc.dma("sp", onesb_tk, RL_d[:, 6 + i, :], onesb[:], reads=[onesb_tk])
        out_tks += [onesb_tk] + [tk for _, tk in pb] + [tk for _, tk in nb] + cn_tk

    def attn_core(self, H, GQ, load_head, score_mm, y_d, c, bias=True, selmask=False, linear=False, VW=65, ycol0=0):
        sc, S, NT = self.sc, self.S, self.NT
        NGR = NT // GQ
        W = GQ * 128
        NPT = 4
        pts = [(self.sb([128, W], BF16), Tk()) for _ in range(NPT)]
        ets = [(self.sb([128, W], F32), Tk()) for _ in range(3)] if linear else None
        yos = [(self.sb([128, GQ, 64], BF16), Tk()) for _ in range(2)]
        recs = [(self.sb([128, 4], F32), Tk()) for _ in range(2)]
        obanks2 = [6, 7] if linear else [4, 5]
        sbanks = [0, 1, 2, 3, 4, 5] if linear else [0, 1, 2, 3]
        NSB = len(sbanks)
        LAG = 2
        st = {"sb": 0, "et": 0}
        out_tks = [tk for _, tk in yos]
        heads = {}

        def get_head(h):
            if h not in heads:
                heads[h] = load_head(h, h % 2)
            return heads[h]

        blocks = [(h, G, kt) for h in range(H) for G in range(NGR) for kt in range((G + 1) * GQ)]

        def stage_a(i):
            h, G, kt = blocks[i]
            hd = get_head(h)
            htks = hd["tks"]
            j = kt - G * GQ
            c0 = max(j, 0) * 128
            q0 = G * W + c0
            q1 = (G + 1) * W
            b = sbanks[st["sb"] % NSB]
            st["sb"] += 1
            lhsT, rhs = score_mm(hd, kt, q0, q1, j)
            last_is_score = (not (bias or j >= 0 or selmask)) or linear
            sc.op("pe", lambda e: e.matmul(self.ps[b][:, c0:W], lhsT=lhsT, rhs=rhs, start=True, stop=last_is_score),
                  reads=htks, writes=[self.pst[b]])
            if linear:
                bd = sbanks[st["sb"] % NSB]
                st["sb"] += 1
                first = True
            else:
                bd = b
                first = False
            if bias:
                lastb = not (j >= 0 or (selmask and j < 0))
                sc.op("pe", lambda e: e.matmul(self.ps[bd][:, c0:W], lhsT=hd["L"][:, kt * 128:(kt + 1) * 128], rhs=hd["R"][:, q0:q1],
                                               start=first, stop=lastb),
                      reads=htks, writes=[self.pst[bd]], join=not first)
                first = False
            if selmask and j < 0:
                blk = kt // 2
                sc.op("pe", lambda e: e.matmul(self.ps[bd][:, c0:W], lhsT=c["onehot"][:, blk, :], rhs=hd["NS"][:, q0:q1],
                                               start=False, stop=True),
                      reads=htks + [c["tk"]], writes=[self.pst[bd]], join=True)
            if j >= 0:
                sc.op("pe", lambda e: e.matmul(self.ps[bd][:, c0:c0 + 128], lhsT=self.ident[:], rhs=c["tri"][:],
                                               start=first, stop=True),
                      reads=[self.ident_t, c["tk"]], writes=[self.pst[bd]], join=True)
            return (b, bd, c0, j)

        def stage_b(i, info):
            b, bd, c0, j = info
            pt, pt_tk = pts[i % NPT]
            if not linear:
                sc.op("act", lambda e: e.activation(out=pt[:, c0:W], in_=self.ps[b][:, c0:W], func=AF.Exp),
                      reads=[self.pst[b]], writes=[pt_tk])
            else:
                et, et_tk = ets[st["et"] % 3]
                st["et"] += 1
                sc.op("act", lambda e: e.activation(out=et[:, c0:W], in_=self.ps[bd][:, c0:W], func=AF.Exp),
                      reads=[self.pst[bd]], writes=[et_tk])
                sc.op("dve", lambda e: e.tensor_tensor(out=pt[:, c0:W], in0=self.ps[b][:, c0:W], in1=et[:, c0:W], op=ALU.mult),
                      reads=[self.pst[b], et_tk], writes=[pt_tk])

        def stage_c(i, info):
            h, G, kt = blocks[i]
            hd = get_head(h)
            htks = hd["tks"]
            b, bd, c0, j = info
            pt, pt_tk = pts[i % NPT]
            gi = h * NGR + G
            ob = obanks2[gi % 2]
            for qi in range(max(j, 0), GQ):
                sc.op("pe", lambda e: e.matmul(self.ps[ob][:, qi * 128:qi * 128 + VW], lhsT=pt[:, qi * 128:(qi + 1) * 128],
                                               rhs=hd["V"][:, kt, 0:VW], start=(kt == 0 and qi == 0), stop=(kt == G * GQ + qi),
                                               skip_group_check=True),
                      reads=[pt_tk] + htks, writes=[self.pst[ob]], join=not (kt == 0 and qi == 0))
            if kt == (G + 1) * GQ - 1:
                yo, yo_tk = yos[gi % 2]
                rec, rec_tk = recs[gi % 2]
                ov = self.ps[ob][:, 0:GQ * 128].rearrange("p (q c) -> p q c", c=128)
                if not linear:
                    sc.op("dve", lambda e: e.reciprocal(out=rec[:, 0:GQ], in_=ov[:, :, 64]), reads=[self.pst[ob]], writes=[rec_tk])
                    for qi in range(GQ):
                        sc.op("dve", lambda e: e.tensor_scalar_mul(out=yo[:, qi, :], in0=ov[:, qi, 0:64], scalar1=rec[:, qi:qi + 1]),
                              reads=[self.pst[ob], rec_tk], writes=[yo_tk], join=(qi > 0))
                else:
                    sc.op("dve", lambda e: e.tensor_copy(out=yo[:], in_=ov[:, :, 0:64]), reads=[self.pst[ob]], writes=[yo_tk])
                sc.dma("sp", yo_tk, y_d[G * W:(G + 1) * W, ycol0 + h * 64:ycol0 + (h + 1) * 64].rearrange("(q p) d -> p q d", p=128),
                       yo[:], reads=[yo_tk])

        infos = {}
        n = len(blocks)
        first_blk = {}
        for i, (h, G, kt) in enumerate(blocks):
            if G == 0 and kt == 0:
                first_blk[i + LAG - 1] = h
        nblk_head = n // H
        bgq = {"q": [], "rate": 0}

        def flush_bg(k=None):
            m = len(bgq["q"]) if k is None else min(k, len(bgq["q"]))
            for _ in range(m):
                bgq["q"].pop(0)()

        for i in range(n + LAG):
            if i < n:
                h, G, kt = blocks[i]
                if G == 0 and kt == 0:
                    if h not in heads:
                        hd0 = get_head(h)
                        bgq["q"] = list(hd0.get("bg", []))
                    flush_bg()
                infos[i] = stage_a(i)
                stage_b(i, infos[i])
            if i >= LAG:
                stage_c(i - LAG, infos.pop(i - LAG))
            if i in first_blk and first_blk[i] + 1 < H:
                hdn = get_head(first_blk[i] + 1)
                bgq["q"] = list(hdn.get("bg", []))
                bgq["rate"] = -(-len(bgq["q"]) // max(nblk_head - LAG - 2, 1))
            elif bgq["q"]:
                flush_bg(bgq["rate"])
        return out_tks

    def weight_preconvert(self, jobs):
        sc = self.sc
        stg = [(self.sb([128, 2048], F32), Tk()) for _ in range(2)]
        obs = [(self.sb([128, 2048], BF16), Tk()) for _ in range(2)]
        i = 0
        for w32, wbf, cc in jobs:
            K_, N_ = w32.shape[0], w32.shape[1]
            for k in range(K_ // 128):
                for c0 in range(0, N_, cc):
                    st_, st_tk = stg[i % 2]
                    ob, ob_tk = obs[i % 2]
                    sc.dma("pool", st_tk, st_[:, 0:cc], w32[k * 128:(k + 1) * 128, c0:c0 + cc], writes=[st_tk])
                    sc.op("pool", lambda e: e.tensor_copy(out=ob[:, 0:cc], in_=st_[:, 0:cc]), reads=[st_tk], writes=[ob_tk])
                    sc.dma("pool", ob_tk, wbf[k * 128:(k + 1) * 128, c0:c0 + cc], ob[:, 0:cc], reads=[ob_tk])
                    i += 1
        return [tk for _, tk in stg] + [tk for _, tk in obs]

    def phase_fox_attn(self, qkT_d, vp_d, RL_d, y_d, c, wjobs=None):
        sc, S, NT = self.sc, self.S, self.NT
        self.sb_reset()
        slots = []
        for sl in range(2):
            d = {"q": self.sb([128, S], BF16), "k": self.sb([128, S], BF16), "V": self.sb([128, NT, 65], BF16), "tk": Tk()}
            sc.op("pool", lambda e: e.memset(d["q"][64:128, :], 0.0), writes=[d["tk"]])
            sc.op("pool", lambda e: e.memset(d["k"][64:128, :], 0.0), writes=[d["tk"]], join=True)
            slots.append(d)

        def load_head(h, sl):
            d = slots[sl]
            tk = d["tk"]
            sc.dma("sp", tk, d["q"][0:64, :], qkT_d[h * 64:(h + 1) * 64, :], writes=[tk])
            sc.dma("sp", tk, d["k"][0:64, :], qkT_d[1024 + h * 64:1024 + (h + 1) * 64, :], writes=[tk], join=True)
            sc.dma("sp", tk, d["q"][64:70, :], RL_d[h, 0:6, :], writes=[tk], join=True)
            sc.dma("sp", tk, d["k"][64:70, :], RL_d[h, 6:12, :], writes=[tk], join=True)
            sc.dma("sp", tk, d["V"][:], vp_d[:, h * 65:(h + 1) * 65].rearrange("(t p) d -> p t d", p=128), writes=[tk], join=True)
            return {"q": d["q"], "k": d["k"], "V": d["V"], "tks": [tk]}

        def score_mm(hd, kt, q0, q1, j):
            return hd["k"][:, kt * 128:(kt + 1) * 128], hd["q"][:, q0:q1]

        wtks = self.weight_preconvert(wjobs) if wjobs else []
        out_tks = self.attn_core(16, 4 if NT >= 4 else NT, load_head, score_mm, y_d, c, bias=False)
        self.phase_barrier(out_tks + [d["tk"] for d in slots] + wtks)

    def phase_out_proj(self, x_dram, y_d, w_out, KT, xout_dram=None):
        sc, S, NT = self.sc, self.S, self.NT
        xout_dram = x_dram if xout_dram is None else xout_dram
        self.sb_reset()
        wo, wo_tk = self.sb([128, KT, D], BF16), Tk()
        stage = [self.sb([128, 1024], F32) for _ in range(2)]
        stage_tk = [Tk() for _ in range(2)]
        self.load_weight(wo, wo_tk, w_out, KT * 128, D, stage, stage_tk, col_chunk=1024)
        yts = [(self.sb([128, KT * 128], BF16), Tk()) for _ in range(2)]
        yTs = [(self.sb([128, KT, 128], BF16), Tk()) for _ in range(2)]
        xres = [(self.sb([128, 512], F32), Tk()) for _ in range(4)]
        sc.dma("sp", yts[0][1], yts[0][0][:], y_d[0:128, :], writes=[yts[0][1]])
        for t in range(NT):
            yt, yt_tk = yts[t % 2]
            yT, yT_tk = yTs[t % 2]
            if t + 1 < NT:
                sc.dma("sp", yts[(t + 1) % 2][1], yts[(t + 1) % 2][0][:], y_d[(t + 1) * 128:(t + 2) * 128, :], writes=[yts[(t + 1) % 2][1]])
            for cc in range(2):
                xr, xr_tk = xres[(t * 2 + cc) % 4]
                sc.dma("sp", xr_tk, xr[:], x_dram[t * 128:(t + 1) * 128, cc * 512:(cc + 1) * 512], writes=[xr_tk])
            for k0 in range(0, KT, 8):
                kn = min(8, KT - k0)
                b = self.bank()
                pt = self.ps[b][:].bitcast(BF16)
                for k in range(kn):
                    sc.op("pe", lambda e: e.transpose(out=pt[:, k * 128:(k + 1) * 128], in_=yt[:, (k0 + k) * 128:(k0 + k + 1) * 128],
                                                      identity=self.ident[:]),
                          reads=[yt_tk, self.ident_t], writes=[self.pst[b]], join=(k > 0))
                if t % 2 == 0:
                    sc.op("act", lambda e: e.activation(out=yT[:, k0:k0 + kn, :], in_=pt[:, 0:kn * 128].rearrange("p (k t) -> p k t", k=kn),
                                                        func=AF.Copy),
                          reads=[self.pst[b]], writes=[yT_tk], join=(k0 > 0))
                else:
                    sc.op("dve", lambda e: e.tensor_copy(out=yT[:, k0:k0 + kn, :], in_=pt[:, 0:kn * 128].rearrange("p (k t) -> p k t", k=kn)),
                          reads=[self.pst[b]], writes=[yT_tk], join=(k0 > 0))
            for cc in range(2):
                xr, xr_tk = xres[(t * 2 + cc) % 4]
                b = self.bank()
                for k in range(KT):
                    sc.op("pe", lambda e: e.matmul(self.ps[b][:], lhsT=yT[:, k, :], rhs=wo[:, k, cc * 512:(cc + 1) * 512],
                                                   start=(k == 0), stop=(k == KT - 1)),
                          reads=[wo_tk, yT_tk], writes=[self.pst[b]], join=(k > 0))
                sc.op("dve", lambda e: e.tensor_tensor(out=xr[:], in0=self.ps[b][:], in1=xr[:], op=ALU.add),
                      reads=[self.pst[b], xr_tk], writes=[xr_tk])
                sc.dma("sp", xr_tk, xout_dram[t * 128:(t + 1) * 128, cc * 512:(cc + 1) * 512], xr[:], reads=[xr_tk])
        self.phase_barrier([tk for _, tk in xres])

    def phase_l0_proj(self, x_dram, g_dram, w_in, conv_w, conv_b, dtb_d, alog_d, cos_d, sin_d,
                      xbcT_d, qkT_d, zs_d, vp_d, RL_d, c, aw_d=None):
        sc, S, NT = self.sc, self.S, self.NT
        NG = S // 512
        self.sb_reset()
        nl, nl_tk = self.sb([128, NT, 16], F32), Tk()
        dtk, dtk_tk = self.sb([128, NT, 16], F32), Tk()
        self.persist_end = self.sb_off
        self.dtk_view = dtk
        hT, hT_tk, mark = self.norm_all(x_dram, g_dram)
        self.phase_barrier([hT_tk])
        self.sb_off = mark
        wts = [(self.sb([128, 8, 128], BF16), Tk()) for _ in range(4)]
        stgs = [(self.sb([128, 8, 128], F32), Tk()) for _ in range(2)]
        rows = [(self.sb([128, S], BF16), Tk()) for _ in range(2)]
        out_tks = []
        cw, cw_tk = self.sb([128, 4, 16], F32), Tk()
        cb, cb_tk = self.sb([128, 16], F32), Tk()
        for k in range(4):
            sc.dma("sp", cw_tk, cw[:, k, :], conv_w[k, :].rearrange("(f p) -> p f", p=128), writes=[cw_tk], join=(k > 0),
                   allow_slow_non_contiguous=True)
        sc.dma("sp", cb_tk, cb[:], conv_b[0, :].rearrange("(f p) -> p f", p=128), writes=[cb_tk], allow_slow_non_contiguous=True)
        u, u_tk = self.sb([128, S + 8], F32), Tk()
        acc, acc_tk = self.sb([128, S], F32), Tk()
        sc.op("pool", lambda e: e.memset(u[:, 0:8], 0.0), writes=[u_tk])
        self.load_wcols(w_in, 1024, 128, wts[0][0], wts[0][1], stgs[0][0], stgs[0][1])
        for f in range(16):
            wt, wt_tk = wts[f % 2]
            row, row_tk = rows[f % 2]
            if f + 1 < 16:
                self.load_wcols(w_in, 1024 + (f + 1) * 128, 128, wts[(f + 1) % 2][0], wts[(f + 1) % 2][1], stgs[(f + 1) % 2][0], stgs[(f + 1) % 2][1])
            for tg in range(NG):
                b = self.bank()
                for k in range(8):
                    sc.op("pe", lambda e: e.matmul(self.ps[b][:], lhsT=wt[:, k, :], rhs=hT[:, k, tg * 512:(tg + 1) * 512],
                                                   start=(k == 0), stop=(k == 7)),
                          reads=[wt_tk, hT_tk], writes=[self.pst[b]], join=(k > 0))
                sc.op("act", lambda e: e.activation(out=u[:, 3 + tg * 512:3 + (tg + 1) * 512], in_=self.ps[b][:], func=AF.Copy),
                      reads=[self.pst[b]], writes=[u_tk], join=True)
            sc.op("act", lambda e: e.activation(out=acc[:], in_=u[:, 0:S], func=AF.Copy, scale=cw[:, 0, f:f + 1]),
                  reads=[u_tk, cw_tk], writes=[acc_tk])
            for k in range(1, 4):
                sc.op("dve", lambda e: e.scalar_tensor_tensor(out=acc[:], in0=u[:, k:k + S], scalar=cw[:, k, f:f + 1], in1=acc[:],
                                                              op0=ALU.mult, op1=ALU.add),
                      reads=[u_tk, cw_tk, acc_tk], writes=[acc_tk])
            sc.op("act", lambda e: e.activation(out=row[:], in_=acc[:], func=AF.Silu, bias=cb[:, f:f + 1]),
                  reads=[acc_tk, cb_tk], writes=[row_tk])
            sc.dma("sp", row_tk, xbcT_d[f * 128:(f + 1) * 128, :], row[:], reads=[row_tk])
            out_tks.append(row_tk)
        self.phase_barrier(out_tks)
        self.sb_off = mark
        wts = [(self.sb([128, 8, 128], BF16), Tk()) for _ in range(4)]
        stgs = [(self.sb([128, 8, 128], F32), Tk()) for _ in range(2)]
        rows = [(self.sb([128, S], BF16), Tk()) for _ in range(2)]
        out_tks = []
        cosT, sinT, cs_tk = self.sb([128, S], F32), self.sb([128, S], F32), Tk()
        sc.dma("sp", cs_tk, cosT[:], cos_d[:, :], writes=[cs_tk])
        sc.dma("sp", cs_tk, sinT[:], sin_d[:, :], writes=[cs_tk], join=True)
        t1s = [(self.sb([128, 512], F32), Tk()) for _ in range(2)]
        t2s = [(self.sb([128, 512], F32), Tk()) for _ in range(2)]
        i = 0

        def load_qk(ii):
            col_ = 3088 + (ii // 4) * 512 + (ii % 4) * 128
            self.load_wcols(w_in, col_, 128, wts[ii % 2][0], wts[ii % 2][1], stgs[0][0], stgs[0][1])
            self.load_wcols(w_in, col_, 128, wts[2 + ii % 2][0], wts[2 + ii % 2][1], stgs[1][0], stgs[1][1], perm_heads=True)

        load_qk(0)
        for which in range(2):
            for f in range(4):
                wt, wt_tk = wts[i % 2]
                wp, wp_tk = wts[2 + i % 2]
                row, row_tk = rows[i % 2]
                if i + 1 < 8:
                    load_qk(i + 1)
                for tg in range(NG):
                    ba, bb = self.bank(), self.bank()
                    for k in range(8):
                        sc.op("pe", lambda e: e.matmul(self.ps[ba][:], lhsT=wt[:, k, :], rhs=hT[:, k, tg * 512:(tg + 1) * 512],
                                                       start=(k == 0), stop=(k == 7)),
                              reads=[wt_tk, hT_tk], writes=[self.pst[ba]], join=(k > 0))
                    for k in range(8):
                        sc.op("pe", lambda e: e.matmul(self.ps[bb][:], lhsT=wp[:, k, :], rhs=hT[:, k, tg * 512:(tg + 1) * 512],
                                                       start=(k == 0), stop=(k == 7)),
                              reads=[wp_tk, hT_tk], writes=[self.pst[bb]], join=(k > 0))
                    t1, t1_tk = t1s[tg % 2]
                    t2, t2_tk = t2s[tg % 2]
                    sc.op("dve", lambda e: e.tensor_tensor(out=t1[:], in0=self.ps[ba][:], in1=cosT[:, tg * 512:(tg + 1) * 512], op=ALU.mult),
                          reads=[self.pst[ba], cs_tk], writes=[t1_tk])
                    sc.op("dve", lambda e: e.tensor_tensor(out=t2[:], in0=self.ps[bb][:], in1=sinT[:, tg * 512:(tg + 1) * 512], op=ALU.mult),
                          reads=[self.pst[bb], cs_tk], writes=[t2_tk])
                    sc.op("pool", lambda e: e.tensor_tensor(out=t1[:], in0=t1[:], in1=t2[:], op=ALU.add),
                          reads=[t1_tk, t2_tk], writes=[t1_tk])
                    sc.op("act", lambda e: e.activation(out=row[:, tg * 512:(tg + 1) * 512], in_=t1[:], func=AF.Copy,
                                                        scale=(0.125 if which == 0 else 1.0)),
                          reads=[t1_tk], writes=[row_tk], join=(tg > 0))
                sc.dma("sp", row_tk, qkT_d[which * 512 + f * 128:which * 512 + (f + 1) * 128, :], row[:], reads=[row_tk])
                out_tks.append(row_tk)
                i += 1
        self.phase_barrier(out_tks)
        self.sb_off = mark
        out_tks = []
        wz, wz_tk = self.sb([128, 8, 1024], BF16), Tk()
        wv, wv_tk = self.sb([128, 8, 512], BF16), Tk()
        wd, wd_tk = self.sb([128, 8, 16], BF16), Tk()
        stz, stz_tk = self.sb([128, 8, 512], F32), Tk()
        for cc in range(2):
            self.load_wcols(w_in, cc * 512, 512, wz[:, :, cc * 512:(cc + 1) * 512], wz_tk, stz, stz_tk)
        self.load_wcols(w_in, 4112, 512, wv, wv_tk, stz, stz_tk)
        self.load_wcols(w_in, 3072, 16, wd, wd_tk, stz, stz_tk)
        dtb, ea, sm_tk = self.sb([128, 16], F32), self.sb([128, 16], F32), Tk()
        sc.dma("sp", sm_tk, dtb[:], dtb_d.partition_broadcast(128), writes=[sm_tk])
        sc.dma("sp", sm_tk, ea[:], alog_d.partition_broadcast(128), writes=[sm_tk], join=True)
        sc.op("act", lambda e: e.activation(out=ea[:], in_=ea[:], func=AF.Exp), reads=[sm_tk], writes=[sm_tk])
        zts = [(self.sb([128, 1024], BF16), Tk()) for _ in range(2)]
        vts = [(self.sb([128, 8, 65], BF16), Tk()) for _ in range(2)]
        for vt, vt_tk in vts:
            sc.op("pool", lambda e: e.memset(vt[:], 1.0), writes=[vt_tk])
        for t in range(NT):
            zt, zt_tk = zts[t % 2]
            vt, vt_tk = vts[t % 2]
            for cc in range(2):
                b = self.bank()
                for k in range(8):
                    sc.op("pe", lambda e: e.matmul(self.ps[b][:], lhsT=hT[:, k, t * 128:(t + 1) * 128], rhs=wz[:, k, cc * 512:(cc + 1) * 512],
                                                   start=(k == 0), stop=(k == 7)),
                          reads=[wz_tk, hT_tk], writes=[self.pst[b]], join=(k > 0))
                sc.op("act", lambda e: e.activation(out=zt[:, cc * 512:(cc + 1) * 512], in_=self.ps[b][:], func=AF.Silu),
                      reads=[self.pst[b]], writes=[zt_tk], join=(cc > 0))
            sc.dma("sp", zt_tk, zs_d[t * 128:(t + 1) * 128, :], zt[:], reads=[zt_tk])
            b = self.bank()
            for k in range(8):
                sc.op("pe", lambda e: e.matmul(self.ps[b][:], lhsT=hT[:, k, t * 128:(t + 1) * 128], rhs=wv[:, k, :],
                                               start=(k == 0), stop=(k == 7)),
                      reads=[wv_tk, hT_tk], writes=[self.pst[b]], join=(k > 0))
            sc.op("act", lambda e: e.activation(out=vt[:, :, 0:64], in_=self.ps[b][:].rearrange("p (h d) -> p h d", h=8), func=AF.Copy),
                  reads=[self.pst[b]], writes=[vt_tk])
            sc.dma("sp", vt_tk, vp_d[t * 128:(t + 1) * 128, :], vt.rearrange("p h d -> p (h d)"), reads=[vt_tk])
            b = self.bank()
            for k in range(8):
                sc.op("pe", lambda e: e.matmul(self.ps[b][:, 0:16], lhsT=hT[:, k, t * 128:(t + 1) * 128], rhs=wd[:, k, :],
                                               start=(k == 0), stop=(k == 7)),
                      reads=[wd_tk, hT_tk], writes=[self.pst[b]], join=(k > 0))
            sc.op("dve", lambda e: e.tensor_tensor(out=dtk[:, t, :], in0=self.ps[b][:, 0:16], in1=dtb[:], op=ALU.add),
                  reads=[self.pst[b], sm_tk], writes=[dtk_tk], join=True)
            out_tks += [zt_tk, vt_tk]
        sc.op("act", lambda e: e.activation(out=dtk[:], in_=dtk[:], func=AF.Exp), reads=[dtk_tk], writes=[dtk_tk])
        sc.op("dve", lambda e: e.tensor_scalar_add(out=dtk[:], in0=dtk[:], scalar1=1.0), reads=[dtk_tk], writes=[dtk_tk])
        sc.op("act", lambda e: e.activation(out=dtk[:], in_=dtk[:], func=AF.Ln), reads=[dtk_tk], writes=[dtk_tk])
        for t in range(NT):
            sc.op("dve", lambda e: e.tensor_tensor(out=nl[:, t, :], in0=dtk[:, t, :], in1=ea[:], op=ALU.mult),
                  reads=[dtk_tk, sm_tk], writes=[nl_tk], join=True)
        self.phase_barrier(out_tks + [nl_tk, dtk_tk])
        self.sb_off = self.persist_end
        out_tks = []
        self.cumsum_rows(nl, nl_tk, RL_d, c, out_tks, aw_d=aw_d)
        self.phase_barrier(out_tks)

    def phase_ssd_prep(self, xbcT_d, xs_d, xc_d, btok_d=None):
        sc, S, NT = self.sc, self.S, self.NT
        self.sb_off = self.persist_end
        dtk = self.dtk_view
        TG = 4 if NT >= 4 else NT
        ins = [(self.sb([128, 8, TG * 128], BF16), Tk()) for _ in range(2)]
        xss = [(self.sb([128, 16, 64], BF16), Tk()) for _ in range(2)]
        xcs = [(self.sb([128, 16, 64], BF16), Tk()) for _ in range(2)]
        inb = [(self.sb([128, 4, TG * 128], BF16), Tk()) for _ in range(2)]
        bts = [(self.sb([128, 512], BF16), Tk()) for _ in range(2)]

        def prep_load(gg):
            cs = slice(gg * TG * 128, (gg + 1) * TG * 128)
            sc.dma("sp", ins[gg % 2][1], ins[gg % 2][0][:], xbcT_d[0:1024, cs].rearrange("(k p) t -> p k t", p=128), writes=[ins[gg % 2][1]])
            if btok_d is not None:
                sc.dma("sp", inb[gg % 2][1], inb[gg % 2][0][:], xbcT_d[1024:1536, cs].rearrange("(k p) t -> p k t", p=128), writes=[inb[gg % 2][1]])

        for t in range(NT):
            tg, tt = t // TG, t % TG
            it_, it_tk = ins[tg % 2]
            it_ = it_[:, :, tt * 128:(tt + 1) * 128]
            xs, xs_tk = xss[t % 2]
            xc, xc_tk = xcs[t % 2]
            if tt == 0:
                if tg == 0:
                    prep_load(0)
                if (tg + 1) * TG < NT:
                    prep_load(tg + 1)
            b = self.bank()
            pt = self.ps[b][:].bitcast(BF16)
            for k in range(8):
                sc.op("pe", lambda e: e.transpose(out=pt[:, k * 128:(k + 1) * 128], in_=it_[:, k, :], identity=self.ident[:]),
                      reads=[it_tk, self.ident_t], writes=[self.pst[b]], join=(k > 0))
            sc.op("act", lambda e: e.activation(out=xs.rearrange("p h d -> p (h d)"), in_=pt, func=AF.Copy),
                  reads=[self.pst[b]], writes=[xs_tk])
            for h in range(16):
                sc.op("dve" if h % 2 == 0 else "pool", lambda e: e.tensor_scalar_mul(out=xc[:, h, :], in0=xs[:, h, :], scalar1=dtk[:, t, h:h + 1]),
                      reads=[xs_tk], writes=[xc_tk], join=(h > 0))
            sc.dma("sp", xs_tk, xs_d[t * 128:(t + 1) * 128, :], xs.rearrange("p h d -> p (h d)"), reads=[xs_tk])
            sc.dma("sp", xc_tk, xc_d[t * 128:(t + 1) * 128, :], xc.rearrange("p h d -> p (h d)"), reads=[xc_tk])
            if btok_d is not None:
                ib, ib_tk = inb[tg % 2]
                ib = ib[:, :, tt * 128:(tt + 1) * 128]
                bt, bt_tk = bts[t % 2]
                b = self.bank()
                pt = self.ps[b][:].bitcast(BF16)
                for k in range(4):
                    sc.op("pe", lambda e: e.transpose(out=pt[:, k * 128:(k + 1) * 128], in_=ib[:, k, :], identity=self.ident[:]),
                          reads=[ib_tk, self.ident_t], writes=[self.pst[b]], join=(k > 0))
                sc.op("dve", lambda e: e.tensor_copy(out=bt[:], in_=pt[:, 0:512]), reads=[self.pst[b]], writes=[bt_tk])
                sc.dma("sp", bt_tk, btok_d[t * 128:(t + 1) * 128, :], bt[:], reads=[bt_tk])
        self.phase_barrier([tk for _, tk in xss] + [tk for _, tk in xcs] + [tk for _, tk in bts])

    def phase_ssd_chunk(self, xbcT_d, btok_d, xc_d, RL_d, aw_d, y_d, c):
        sc, S, NT = self.sc, self.S, self.NT
        self.sb_reset()
        aw, ea, dte, cd = (self.sb([128, NT, 16], F32) for _ in range(4))
        f_tk = Tk()
        sc.dma("sp", f_tk, aw[:], aw_d.rearrange("(t p) h -> p t h", p=128), writes=[f_tk])
        awf = aw.rearrange("p t h -> p (t h)")
        NF = NT * 16
        bt_ = self.bank()
        sc.op("pe", lambda e: e.matmul(self.ps[bt_][:, 0:NF], lhsT=c["sel127"][:], rhs=awf, start=True, stop=True),
              reads=[f_tk, c["tk"]], writes=[self.pst[bt_]])
        tot = self.sb([128, NF], F32)
        sc.op("dve", lambda e: e.tensor_copy(out=tot[:], in_=self.ps[bt_][:, 0:NF]), reads=[self.pst[bt_]], writes=[f_tk], join=True)
        sc.op("act", lambda e: e.activation(out=ea.rearrange("p t h -> p (t h)"), in_=awf, func=AF.Exp, scale=-1.0),
              reads=[f_tk], writes=[f_tk], join=True)
        sc.op("dve", lambda e: e.tensor_tensor(out=dte.rearrange("p t h -> p (t h)"), in0=awf, in1=tot[:], op=ALU.subtract),
              reads=[f_tk], writes=[f_tk])
        sc.op("act", lambda e: e.activation(out=dte.rearrange("p t h -> p (t h)"), in_=dte.rearrange("p t h -> p (t h)"), func=AF.Exp),
              reads=[f_tk], writes=[f_tk])
        sc.op("act", lambda e: e.activation(out=cd.rearrange("p t h -> p (t h)"), in_=tot[:], func=AF.Exp, scale=-1.0),
              reads=[f_tk], writes=[f_tk])
        H = self.sb([128, 16, 64], F32)
        Hb = self.sb([128, 1024], BF16)
        H_tk = [Tk() for _ in range(4)]
        Hb_tk = [Tk() for _ in range(4)]
        sc.op("pool", lambda e: e.memset(H[:], 0.0), writes=H_tk)
        sc.op("pool", lambda e: e.memset(Hb[:], 0.0), writes=Hb_tk)
        ld = []
        for i in range(2):
            d = {"CT": self.sb([128, 4, 128], BF16), "BT": self.sb([128, 4, 128], BF16), "Bk": self.sb([128, 512], BF16),
                 "xc": self.sb([128, 1024], BF16), "L": self.sb([128, 16, 128], BF16), "R": self.sb([128, 16, 128], BF16), "tk": Tk()}
            sc.op("pool", lambda e: e.memset(d["L"][:], 0.0), writes=[d["tk"]])
            sc.op("pool", lambda e: e.memset(d["R"][:], 0.0), writes=[d["tk"]], join=True)
            ld.append(d)

        def load(ci):
            d = ld[ci % 2]
            tk = d["tk"]
            cs = slice(ci * 128, (ci + 1) * 128)
            sc.dma("sp", tk, d["CT"][:], xbcT_d[1536:2048, cs].rearrange("(g n) t -> n g t", n=128), writes=[tk])
            sc.dma("sp", tk, d["BT"][:], xbcT_d[1024:1536, cs].rearrange("(g n) t -> n g t", n=128), writes=[tk], join=True)
            sc.dma("sp", tk, d["Bk"][:], btok_d[cs, :], writes=[tk], join=True)
            sc.dma("sp", tk, d["xc"][:], xc_d[cs, :], writes=[tk], join=True)
            sc.dma("sp", tk, d["R"][0:6, :, :], RL_d[:, 0:6, cs].rearrange("h j t -> j h t"), writes=[tk], join=True)
            sc.dma("sp", tk, d["L"][0:6, :, :], RL_d[:, 6:12, cs].rearrange("h j t -> j h t"), writes=[tk], join=True)

        ets = [(self.sb([128, 512], F32), Tk()) for _ in range(2)]
        mts = [(self.sb([128, 4, 128], BF16), Tk()) for _ in range(2)]
        yds = [(self.sb([128, 256], F32), Tk()) for _ in range(2)]
        xcds = [(self.sb([128, 256], BF16), Tk()) for _ in range(2)]
        yts = [(self.sb([128, 1024], BF16), Tk()) for _ in range(2)]
        load(0)
        if NT > 1:
            load(1)
        steps = [(ci, g) for ci in range(NT) for g in range(4)]

        def bufs_for(i):
            return {"db": 2 + i % 2, "yb": 4 + i % 2, "sbk": 6 + i % 2, "et": ets[i % 2], "mt": mts[i % 2], "yd": yds[i % 2], "xcd": xcds[i % 2]}

        def front(i):
            ci, g = steps[i]
            d = ld[ci % 2]
            tk = d["tk"]
            cbk = ci % 2
            B = bufs_for(i)
            db = B["db"]
            et, et_tk = B["et"]
            mt, mt_tk = B["mt"]
            sc.op("pe", lambda e: e.matmul(self.ps[cbk][:, g * 128:(g + 1) * 128], lhsT=d["BT"][:, g, :], rhs=d["CT"][:, g, :],
                                           start=True, stop=True, skip_group_check=True),
                  reads=[tk], writes=[self.pst[cbk]], join=(g > 0))
            for r in range(4):
                h = 4 * g + r
                sc.op("pe", lambda e: e.matmul(self.ps[db][:, r * 128:(r + 1) * 128], lhsT=d["L"][:, h, :], rhs=d["R"][:, h, :],
                                               start=True, stop=False, skip_group_check=True),
                      reads=[tk], writes=[self.pst[db]], join=(r > 0))
                sc.op("pe", lambda e: e.matmul(self.ps[db][:, r * 128:(r + 1) * 128], lhsT=self.ident[:], rhs=c["tri"][:],
                                               start=False, stop=True, skip_group_check=True),
                      reads=[self.ident_t, c["tk"]], writes=[self.pst[db]], join=True)
            sc.op("act", lambda e: e.activation(out=et[:], in_=self.ps[db][:], func=AF.Exp), reads=[self.pst[db]], writes=[et_tk])
            for r in range(4):
                sc.op("dve", lambda e: e.tensor_tensor(out=mt[:, r, :], in0=self.ps[cbk][:, g * 128:(g + 1) * 128],
                                                       in1=et[:, r * 128:(r + 1) * 128], op=ALU.mult),
                      reads=[self.pst[cbk], et_tk], writes=[mt_tk], join=(r > 0))

        def back(i):
            ci, g = steps[i]
            d = ld[ci % 2]
            tk = d["tk"]
            yt, yt_tk = yts[ci % 2]
            B = bufs_for(i)
            yb, sbk = B["yb"], B["sbk"]
            mt, mt_tk = B["mt"]
            yd, yd_tk = B["yd"]
            xcd, xcd_tk = B["xcd"]
            for r in range(4):
                h = 4 * g + r
                sc.op("pe", lambda e: e.matmul(self.ps[yb][:, r * 64:(r + 1) * 64], lhsT=mt[:, r, :], rhs=d["xc"][:, h * 64:(h + 1) * 64],
                                               start=True, stop=True, skip_group_check=True),
                      reads=[mt_tk, tk], writes=[self.pst[yb]], join=(r > 0))
            sc.op("pe", lambda e: e.matmul(self.ps[yb][:, 256:512], lhsT=d["CT"][:, g, :], rhs=Hb[:, g * 256:(g + 1) * 256],
                                           start=True, stop=True, skip_group_check=True),
                  reads=[tk, Hb_tk[g]], writes=[self.pst[yb]], join=True)
            sc.op("act", lambda e: e.activation(out=yd[:], in_=self.ps[yb][:, 0:256], func=AF.Copy), reads=[self.pst[yb]], writes=[yd_tk])
            for r in range(4):
                h = 4 * g + r
                sc.op("dve", lambda e: e.scalar_tensor_tensor(out=yt[:, h * 64:(h + 1) * 64], in0=self.ps[yb][:, 256 + r * 64:256 + (r + 1) * 64],
                                                              scalar=ea[:, ci, h:h + 1], in1=yd[:, r * 64:(r + 1) * 64],
                                                              op0=ALU.mult, op1=ALU.add),
                      reads=[self.pst[yb], yd_tk, f_tk], writes=[yt_tk], join=not (g == 0 and r == 0))
            for r in range(4):
                h = 4 * g + r
                sc.op("pool", lambda e: e.tensor_scalar_mul(out=xcd[:, r * 64:(r + 1) * 64], in0=d["xc"][:, h * 64:(h + 1) * 64],
                                                                                      scalar1=dte[:, ci, h:h + 1]),
                      reads=[tk, f_tk], writes=[xcd_tk], join=(r > 0))
            sc.op("pe", lambda e: e.matmul(self.ps[sbk][:, 0:256], lhsT=d["Bk"][:, g * 128:(g + 1) * 128], rhs=xcd[:],
                                           start=True, stop=True),
                  reads=[tk, xcd_tk], writes=[self.pst[sbk]])
            for r in range(4):
                h = 4 * g + r
                sc.op("dve", lambda e: e.scalar_tensor_tensor(out=H[:, h, :], in0=H[:, h, :], scalar=cd[:, ci, h:h + 1],
                                                              in1=self.ps[sbk][:, r * 64:(r + 1) * 64], op0=ALU.mult, op1=ALU.add),
                      reads=[self.pst[sbk], f_tk, H_tk[g]], writes=[H_tk[g]])
            sc.op("act", lambda e: e.activation(out=Hb[:, g * 256:(g + 1) * 256], in_=H[:, 4 * g:4 * g + 4, :].rearrange("p h d -> p (h d)"),
                                                func=AF.Copy),
                  reads=[H_tk[g]], writes=[Hb_tk[g]])
            if g == 3:
                sc.dma("sp", yt_tk, y_d[ci * 128:(ci + 1) * 128, :], yt[:], reads=[yt_tk])
                if ci + 2 < NT:
                    load(ci + 2)

        nst = len(steps)
        for i in range(nst + 1):
            if i < nst:
                front(i)
            if i >= 1:
                back(i - 1)
        self.phase_barrier([tk for _, tk in yts] + [d["tk"] for d in ld])

    def phase_ssd_attn(self, xbcT_d, xc_d, RL_d, y_d, c):
        sc, S, NT = self.sc, self.S, self.NT
        self.sb_reset()
        slots = []
        for sl in range(2):
            d = {"B": self.sb([128, S], BF16), "C": self.sb([128, S], BF16), "R": self.sb([128, S], BF16),
                 "L": self.sb([128, S], BF16), "V": self.sb([128, NT, 64], BF16), "tk": Tk()}
            sc.op("pool", lambda e: e.memset(d["R"][:], 0.0), writes=[d["tk"]])
            sc.op("pool", lambda e: e.memset(d["L"][:], 0.0), writes=[d["tk"]], join=True)
            slots.append(d)

        def load_head(h, sl):
            d = slots[sl]
            tk = d["tk"]
            g = h // 4
            sc.dma("sp", tk, d["B"][:], xbcT_d[1024 + g * 128:1024 + (g + 1) * 128, :], writes=[tk])
            sc.dma("sp", tk, d["C"][:], xbcT_d[1536 + g * 128:1536 + (g + 1) * 128, :], writes=[tk], join=True)
            sc.dma("sp", tk, d["R"][0:6, :], RL_d[h, 0:6, :], writes=[tk], join=True)
            sc.dma("sp", tk, d["L"][0:6, :], RL_d[h, 6:12, :], writes=[tk], join=True)
            sc.dma("sp", tk, d["V"][:], xc_d[:, h * 64:(h + 1) * 64].rearrange("(t p) d -> p t d", p=128), writes=[tk], join=True)
            return {"B": d["B"], "C": d["C"], "R": d["R"], "L": d["L"], "V": d["V"], "tks": [tk]}

        def score_mm(hd, kt, q0, q1, j):
            return hd["B"][:, kt * 128:(kt + 1) * 128], hd["C"][:, q0:q1]

        out_tks = self.attn_core(16, 4 if NT >= 4 else NT, load_head, score_mm, y_d, c, linear=True, VW=64)
        self.phase_barrier(out_tks + [d["tk"] for d in slots])

    def phase_ssd_post(self, ysc_d, xs_d, zs_d, dsk_d, gn_d, y0_d):
        sc, S, NT = self.sc, self.S, self.NT
        self.sb_reset()
        dsk, gn, cst_tk = self.sb([128, 1024], F32), self.sb([128, 1024], F32), Tk()
        sc.dma("sp", cst_tk, dsk[:], dsk_d.partition_broadcast(128), writes=[cst_tk])
        sc.dma("sp", cst_tk, gn[:], gn_d.partition_broadcast(128), writes=[cst_tk], join=True)
        NB = 2
        ld = [[(self.sb([128, 1024], BF16), Tk()) for _ in range(3)] for _ in range(NB)]
        ys = [(self.sb([128, 1024], F32), Tk()) for _ in range(NB)]
        yo = [(self.sb([128, 1024], BF16), Tk()) for _ in range(NB)]
        st = [(self.sb([128, 16], F32), Tk()) for _ in range(NB)]
        junk, junk_tk = self.sb([128, 256], F32), Tk()
        for t in range(NT):
            (a, a_tk), (x_, x_tk), (z, z_tk) = ld[t % NB]
            y, y_tk = ys[t % NB]
            o, o_tk = yo[t % NB]
            s_, s_tk = st[t % NB]
            def post_load(tt):
                (a_, a_tk_), (x__, x_tk_), (z_, z_tk_) = ld[tt % NB]
                sc.dma("sp", a_tk_, a_[:], ysc_d[tt * 128:(tt + 1) * 128, :], writes=[a_tk_])
                sc.dma("sp", x_tk_, x__[:], xs_d[tt * 128:(tt + 1) * 128, :], writes=[x_tk_])
                sc.dma("sp", z_tk_, z_[:], zs_d[tt * 128:(tt + 1) * 128, :], writes=[z_tk_])
            if t == 0:
                post_load(0)
            if t + 1 < NT:
                post_load(t + 1)
            sc.op("dve", lambda e: e.tensor_tensor(out=y[:], in0=x_[:], in1=dsk[:], op=ALU.mult), reads=[x_tk, cst_tk], writes=[y_tk])
            sc.op("pool", lambda e: e.tensor_tensor(out=y[:], in0=y[:], in1=a[:], op=ALU.add), reads=[y_tk, a_tk], writes=[y_tk])
            sc.op("dve", lambda e: e.tensor_tensor(out=y[:], in0=y[:], in1=z[:], op=ALU.mult), reads=[y_tk, z_tk], writes=[y_tk])
            for g in range(4):
                sc.op("act", lambda e: e.activation(out=junk[:], in_=y[:, g * 256:(g + 1) * 256], func=AF.Square, scale=1.0 / 16.0,
                                                    accum_out=s_[:, g:g + 1]),
                      reads=[y_tk], writes=[junk_tk, s_tk], join=(g > 0))
            sc.op("dve", lambda e: e.tensor_scalar_add(out=s_[:, 4:8], in0=s_[:, 0:4], scalar1=EPS), reads=[s_tk], writes=[s_tk])
            sc.op("act", lambda e: e.activation(out=s_[:, 8:12], in_=s_[:, 4:8], func=AF.Sqrt), reads=[s_tk], writes=[s_tk])
            sc.op("dve", lambda e: e.reciprocal(out=s_[:, 12:16], in_=s_[:, 8:12]), reads=[s_tk], writes=[s_tk])
            for g in range(4):
                sc.op("dve", lambda e: e.scalar_tensor_tensor(
                    out=o[:, g * 256:(g + 1) * 256], in0=y[:, g * 256:(g + 1) * 256], scalar=s_[:, 12 + g:13 + g],
                    in1=gn[:, g * 256:(g + 1) * 256], op0=ALU.mult, op1=ALU.mult),
                    reads=[y_tk, s_tk, cst_tk], writes=[o_tk], join=(g > 0))
            sc.dma("sp", o_tk, y0_d[t * 128:(t + 1) * 128, 0:1024], o[:], reads=[o_tk])
        self.phase_barrier([tk for _, tk in yo])

    def phase_moba_gate(self, qkT_d, mot_d, m2t_d, o1t_d, ns_d, c):
        sc, S, NT = self.sc, self.S, self.NT
        self.sb_reset()
        NBLK = S // 256
        NF = NT * 16
        mot, m2t, mk_tk = self.sb([128, NF], F32), self.sb([128, NF], F32), Tk()
        sc.dma("sp", mk_tk, mot[:], mot_d[:, 0:NF], writes=[mk_tk])
        sc.dma("sp", mk_tk, m2t[:], m2t_d[:, 0:NF], writes=[mk_tk], join=True)
        o1t = self.sb([128, NF], F32)
        sc.dma("sp", mk_tk, o1t[:], o1t_d[:, 0:NF], writes=[mk_tk], join=True)
        slots = [{"q": self.sb([64, S], BF16), "k": self.sb([64, S], BF16), "tk": Tk()} for _ in range(2)]
        wk = [{"km": self.sb([64, 16], F32), "kmb": self.sb([64, 16], BF16), "gm": self.sb([128, NF], F32), "m8": self.sb([128, NT * 8], F32),
               "ns": self.sb([128, NF], F32), "ns2": self.sb([128, NF], BF16), "nsT": self.sb([16, S], BF16),
               "km_tk": Tk(), "g_tk": Tk(), "nsT_tk": Tk()} for _ in range(2)]

        def load(h):
            d = slots[h % 2]
            sc.dma("sp", d["tk"], d["q"][:], qkT_d[h * 64:(h + 1) * 64, :], writes=[d["tk"]])
            sc.dma("sp", d["tk"], d["k"][:], qkT_d[512 + h * 64:512 + (h + 1) * 64, :], writes=[d["tk"]], join=True)

        load(0)
        for h in range(8):
            if h + 1 < 8:
                load(h + 1)
            d, w = slots[h % 2], wk[h % 2]
            tk = d["tk"]
            sc.op("pool", lambda e: e.memset(w["km"][:], 0.0), writes=[w["km_tk"]])
            sc.op("dve", lambda e: e.tensor_reduce(out=w["km"][:, 0:NBLK], in_=d["k"].rearrange("p (b j) -> p b j", j=256), axis=AX.X, op=ALU.add),
                  reads=[tk], writes=[w["km_tk"]])
            sc.op("act", lambda e: e.activation(out=w["kmb"][:], in_=w["km"][:], func=AF.Copy, scale=1.0 / 256.0),
                  reads=[w["km_tk"]], writes=[w["km_tk"]])
            bg_ = self.bank()
            for t in range(NT):
                sc.op("pe", lambda e: e.matmul(self.ps[bg_][:, t * 16:(t + 1) * 16], lhsT=d["q"][:, t * 128:(t + 1) * 128], rhs=w["kmb"][:],
                                               start=True, stop=True, skip_group_check=True),
                      reads=[tk, w["km_tk"]], writes=[self.pst[bg_]], join=(t > 0))
            sc.op("dve", lambda e: e.tensor_tensor(out=w["gm"][:], in0=self.ps[bg_][:, 0:NF], in1=mot[:], op=ALU.add),
                  reads=[self.pst[bg_], mk_tk], writes=[w["g_tk"]])
            for t in range(NT):
                sc.op("dve", lambda e: e.max(out=w["m8"][:, t * 8:(t + 1) * 8], in_=w["gm"][:, t * 16:(t + 1) * 16]),
                      reads=[w["g_tk"]], writes=[w["g_tk"]])
            for t in range(NT):
                sc.op("dve", lambda e: e.tensor_scalar(out=w["ns"][:, t * 16:(t + 1) * 16], in0=w["gm"][:, t * 16:(t + 1) * 16],
                                                       scalar1=w["m8"][:, t * 8 + 2:t * 8 + 3], scalar2=-NEG, op0=ALU.is_ge, op1=ALU.mult),
                      reads=[w["g_tk"]], writes=[w["g_tk"]])
            sc.op("dve", lambda e: e.tensor_tensor(out=w["ns"][:], in0=w["ns"][:], in1=o1t[:], op=ALU.add),
                  reads=[w["g_tk"], mk_tk], writes=[w["g_tk"]])
            sc.op("dve", lambda e: e.tensor_tensor(out=w["ns2"][:], in0=w["ns"][:], in1=m2t[:], op=ALU.min),
                  reads=[w["g_tk"], mk_tk], writes=[w["g_tk"]])
            for t0 in range(0, NT, 4):
                b2 = self.bank()
                n4 = min(4, NT - t0)
                for t in range(t0, t0 + n4):
                    sc.op("pe", lambda e: e.matmul(self.ps[b2][0:16, (t - t0) * 128:(t - t0 + 1) * 128], lhsT=w["ns2"][:, t * 16:(t + 1) * 16],
                                                   rhs=self.ident[:], start=True, stop=True, skip_group_check=True),
                          reads=[w["g_tk"], self.ident_t], writes=[self.pst[b2]], join=(t > t0))
                sc.op("act", lambda e: e.activation(out=w["nsT"][:, t0 * 128:(t0 + n4) * 128], in_=self.ps[b2][0:16, 0:n4 * 128], func=AF.Copy),
                      reads=[self.pst[b2]], writes=[w["nsT_tk"]], join=(t0 > 0))
            sc.dma("sp", w["nsT_tk"], ns_d[h, :, :], w["nsT"][:], reads=[w["nsT_tk"]])
        self.phase_barrier([w["nsT_tk"] for w in wk] + [d["tk"] for d in slots])

    def phase_moba(self, qkT_d, vp_d, ns_d, eblk_d, y0_d, c, wjobs=None):
        sc, S, NT = self.sc, self.S, self.NT
        self.sb_reset()
        slots = []
        for sl in range(2):
            d = {"q": self.sb([128, S], BF16), "k": self.sb([128, S], BF16), "V": self.sb([128, NT, 65], BF16), "tk": Tk()}
            sc.op("pool", lambda e: e.memset(d["q"][64:128, :], 0.0), writes=[d["tk"]])
            sc.op("pool", lambda e: e.memset(d["k"][64:128, :], 0.0), writes=[d["tk"]], join=True)
            sc.dma("sp", d["tk"], d["k"][64:80, :], eblk_d[:, :], writes=[d["tk"]])
            slots.append(d)

        def load_head(h, sl):
            d = slots[sl]
            tk = d["tk"]
            sc.dma("sp", tk, d["q"][0:64, :], qkT_d[h * 64:(h + 1) * 64, :], writes=[tk])
            sc.dma("sp", tk, d["k"][0:64, :], qkT_d[512 + h * 64:512 + (h + 1) * 64, :], writes=[tk], join=True)
            sc.dma("sp", tk, d["q"][64:80, :], ns_d[h, :, :], writes=[tk], join=True)
            sc.dma("sp", tk, d["V"][:], vp_d[:, h * 65:(h + 1) * 65].rearrange("(t p) d -> p t d", p=128), writes=[tk], join=True)
            return {"q": d["q"], "k": d["k"], "V": d["V"], "tks": [tk]}

        def score_mm(hd, kt, q0, q1, j):
            return hd["k"][:, kt * 128:(kt + 1) * 128], hd["q"][:, q0:q1]

        wtks = self.weight_preconvert(wjobs) if wjobs else []
        out_tks = self.attn_core(8, 4 if NT >= 4 else NT, load_head, score_mm, y0_d, c, bias=False, selmask=False, ycol0=1024)
        self.phase_barrier(out_tks + [d["tk"] for d in slots] + wtks)

    def phase_barrier(self, tks):
        for en in ("pe", "act", "dve", "pool", "sp"):
            self.sc.wait_all(en, tks)
            self.sc.wait_all(en, self.pst)
        self.sc.end_phase()


def host_consts(S=S_FULL):
    i = np.arange(128)
    tri = np.where(i[:, None] > i[None, :], NEG, 0.0).astype(ml_dtypes.bfloat16)
    onehot = np.zeros((16, 16, 128), dtype=ml_dtypes.bfloat16)
    for b in range(16):
        onehot[b, b, :] = 1
    p = np.arange(128)
    dd = p % 64
    jj = dd % 32
    inv = np.power(np.float32(10000.0), -(jj.astype(np.float32)) / np.float32(32)).astype(np.float32)
    ang = (np.arange(S, dtype=np.float32)[None, :] * inv[:, None]).astype(np.float32)
    cosT = np.cos(ang).astype(np.float32)
    sinT = (np.sin(ang) * np.where(dd < 32, -1.0, 1.0)[:, None]).astype(np.float32)
    own = np.arange(16)[:, None]
    blk = np.arange(16)[None, :]
    mo = np.broadcast_to(np.where(blk >= own, -1e30, 0.0).astype(np.float32)[None], (128, 16, 16)).copy()
    m2 = np.broadcast_to(np.where(blk >= own, NEG, 0.0).astype(np.float32)[None], (128, 16, 16)).copy()
    eblk = (np.arange(16)[:, None] == (np.arange(S)[None, :] // 256)).astype(ml_dtypes.bfloat16)
    NT_ = S // 128
    own_t = (np.arange(NT_) // 2)[:, None]
    mot = np.zeros((128, 512), np.float32)
    m2t = np.zeros((128, 512), np.float32)
    mot[:, :NT_ * 16] = np.where(blk >= own_t, -1e30, 0.0).astype(np.float32).reshape(1, -1)
    m2t[:, :NT_ * 16] = np.where(blk > own_t, NEG, 0.0).astype(np.float32).reshape(1, -1)
    o1t = np.zeros((128, 512), np.float32)
    o1t[:, :NT_ * 16] = np.where(blk == own_t, 0.0, NEG).astype(np.float32).reshape(1, -1)
    return {
        "c_mot": mot, "c_m2t": m2t, "c_o1t": o1t,
        "c_eblk": eblk,
        "c_cosT": cosT, "c_sinT": sinT, "c_mo": mo, "c_m2": m2,
        "ident": np.eye(128, dtype=ml_dtypes.bfloat16),
        "c_ident32": np.eye(128, dtype=np.float32),
        "c_triu": (i[:, None] <= i[None, :]).astype(np.float32),
        "c_ones32": np.ones((128, 128), np.float32),
        "c_sel127": np.where(i[:, None] == 127, 1.0, 0.0).astype(np.float32) * np.ones((128, 128), np.float32),
        "c_tri": tri,
        "c_onehot": onehot,
    }


def build(S, phases, dbg=False):
    nc = bass.Bass("TRN2", target_bir_lowering=False)
    EI = "ExternalInput"
    SCR = "ExternalOutput" if dbg else "Internal"
    x_in = nc.dram_tensor("x", [S, D], F32, kind=EI).ap()
    ident_d = nc.dram_tensor("ident", [128, 128], BF16, kind=EI).ap()
    cd = {
        "ident32": (nc.dram_tensor("c_ident32", [128, 128], F32, kind=EI).ap(), [128, 128], F32),
        "triu": (nc.dram_tensor("c_triu", [128, 128], F32, kind=EI).ap(), [128, 128], F32),
        "ones32": (nc.dram_tensor("c_ones32", [128, 128], F32, kind=EI).ap(), [128, 128], F32),
        "sel127": (nc.dram_tensor("c_sel127", [128, 128], F32, kind=EI).ap(), [128, 128], F32),
        "tri": (nc.dram_tensor("c_tri", [128, 128], BF16, kind=EI).ap(), [128, 128], BF16),
        "onehot": (nc.dram_tensor("c_onehot", [16, 16, 128], BF16, kind=EI).ap(), [16, 16, 128], BF16),
    }
    cos_d = nc.dram_tensor("c_cosT", [128, S], F32, kind=EI).ap()
    sin_d = nc.dram_tensor("c_sinT", [128, S], F32, kind=EI).ap()
    mo_d = nc.dram_tensor("c_mo", [128, 16, 16], F32, kind=EI).ap()
    m2_d = nc.dram_tensor("c_m2", [128, 16, 16], F32, kind=EI).ap()
    eblk_d = nc.dram_tensor("c_eblk", [16, S], BF16, kind=EI).ap()
    mot_d = nc.dram_tensor("c_mot", [128, 512], F32, kind=EI).ap()
    m2t_d = nc.dram_tensor("c_m2t", [128, 512], F32, kind=EI).ap()
    o1t_d = nc.dram_tensor("c_o1t", [128, 512], F32, kind=EI).ap()
    nsd = nc.dram_tensor("nsd", [8, 16, S], BF16, kind=SCR).ap()
    norm_mix_even = nc.dram_tensor("norm_mix_even", [1, D], F32, kind=EI).ap()
    w_in_even = nc.dram_tensor("w_in_even", [1, D, 4624], F32, kind=EI).ap()
    conv_w = nc.dram_tensor("conv_w", [1, 4, 2048], F32, kind=EI).ap()
    conv_b = nc.dram_tensor("conv_b", [1, 2048], F32, kind=EI).ap()
    dt_bias = nc.dram_tensor("dt_bias", [1, 16], F32, kind=EI).ap()
    a_log = nc.dram_tensor("a_log", [1, 16], F32, kind=EI).ap()
    d_skip_rep = nc.dram_tensor("d_skip_rep", [1, 1024], F32, kind=EI).ap()
    ssd_gate_norm = nc.dram_tensor("ssd_gate_norm", [1, 1024], F32, kind=EI).ap()
    w_out_even = nc.dram_tensor("w_out_even", [1, 1536, D], F32, kind=EI).ap()
    xbcT = nc.dram_tensor("xbcT", [2048, S], BF16, kind=SCR).ap()
    qkT0 = nc.dram_tensor("qkT0", [1024, S], BF16, kind=SCR).ap()
    zs = nc.dram_tensor("zs", [S, 1024], BF16, kind=SCR).ap()
    vp0 = nc.dram_tensor("vp0", [S, 8 * 65], BF16, kind=SCR).ap()
    RL0 = nc.dram_tensor("RL0", [16, 12, S], BF16, kind=SCR).ap()
    xs0 = nc.dram_tensor("xs0", [S, 1024], BF16, kind=SCR).ap()
    xc0 = nc.dram_tensor("xc0", [S, 1024], BF16, kind=SCR).ap()
    ysc = nc.dram_tensor("ysc", [S, 1024], BF16, kind=SCR).ap()
    y0 = nc.dram_tensor("y0", [S, 1536], BF16, kind=SCR).ap()
    btok = nc.dram_tensor("btok", [S, 512], BF16, kind=SCR).ap()
    awd = nc.dram_tensor("awd", [S, 16], F32, kind=SCR).ap()
    wub = nc.dram_tensor("wub", [2, D, DFF], BF16, kind="Internal").ap()
    wdb = nc.dram_tensor("wdb", [2, DFF, D], BF16, kind="Internal").ap()
    PRECONV = os.environ.get("PRECONV", "1") == "1"
    done_conv = set()
    norm_mlp = nc.dram_tensor("norm_mlp", [2, D], F32, kind=EI).ap()
    w_up = nc.dram_tensor("w_up", [2, D, DFF], F32, kind=EI).ap()
    w_down = nc.dram_tensor("w_down", [2, DFF, D], F32, kind=EI).ap()
    final_norm = nc.dram_tensor("final_norm", [1, D], F32, kind=EI).ap()
    norm_mix_odd = nc.dram_tensor("norm_mix_odd", [1, D], F32, kind=EI).ap()
    w_in_odd = nc.dram_tensor("w_in_odd", [1, D, 3088], F32, kind=EI).ap()
    fgate_bias = nc.dram_tensor("fgate_bias", [1, 16], F32, kind=EI).ap()
    w_out_odd = nc.dram_tensor("w_out_odd", [1, D, D], F32, kind=EI).ap()
    out = nc.dram_tensor("out", [S, D], F32, kind="ExternalOutput").ap()
    xs = nc.dram_tensor("xs", [S, D], F32, kind=SCR).ap()
    qkT = nc.dram_tensor("qkT", [2048, S], BF16, kind=SCR).ap()
    vp = nc.dram_tensor("vp", [S, 16 * 65], BF16, kind=SCR).ap()
    RL = nc.dram_tensor("RL", [16, 12, S], BF16, kind=SCR).ap()
    y1 = nc.dram_tensor("y1", [S, D], BF16, kind=SCR).ap()
    kb = KB(nc, S)
    kb.setup_consts(ident_d, cd)
    c = kb.c
    cur = x_in
    for ph in phases:
        if ph == "mlp0":
            kb.phase_mlp(cur, norm_mlp[0:1, :], w_up[0], w_down[0], xout_dram=xs, wbf=(wub[0], wdb[0]) if 0 in done_conv else None)
            cur = xs
        elif ph == "mlp1":
            kb.phase_mlp(cur, norm_mlp[1:2, :], w_up[1], w_down[1], xout_dram=xs, wbf=(wub[1], wdb[1]) if 1 in done_conv else None)
            cur = xs
        elif ph == "l0":
            LS = int(os.environ.get("L0_STOP", "9"))
            kb.phase_l0_proj(cur, norm_mix_even[0:1, :], w_in_even[0], conv_w[0], conv_b, dt_bias[0:1, :], a_log[0:1, :], cos_d, sin_d,
                             xbcT, qkT0, zs, vp0, RL0, c, aw_d=awd)
            if LS >= 2:
                kb.phase_ssd_prep(xbcT, xs0, xc0, btok_d=btok)
            if LS >= 3:
                if os.environ.get("SSD_QUAD", "0") == "1":
                    kb.phase_ssd_attn(xbcT, xc0, RL0, ysc, c)
                else:
                    kb.phase_ssd_chunk(xbcT, btok, xc0, RL0, awd, ysc, c)
            if LS >= 4:
                kb.phase_ssd_post(ysc, xs0, zs, d_skip_rep[0:1, :], ssd_gate_norm[0:1, :], y0)
            if LS >= 5:
                kb.phase_moba_gate(qkT0, mot_d, m2t_d, o1t_d, nsd, c)
                wj = [(w_up[0], wub[0], 2048), (w_down[0], wdb[0], 1024)] if (PRECONV and "mlp0" in phases) else None
                kb.phase_moba(qkT0, vp0, nsd, eblk_d, y0, c, wjobs=wj)
                if wj:
                    done_conv.add(0)
            if LS >= 6:
                kb.phase_out_proj(cur, y0, w_out_even[0], 12, xout_dram=xs)
                cur = xs
        elif ph == "fox":
            kb.phase_fox_proj(cur, norm_mix_odd[0:1, :], w_in_odd[0], fgate_bias[0:1, :], qkT, vp, RL, c)
            FS = int(os.environ.get("FOX_STOP", "3"))
            if FS >= 2:
                wj = [(w_up[1], wub[1], 2048), (w_down[1], wdb[1], 1024)] if (PRECONV and "mlp1" in phases) else None
                kb.phase_fox_attn(qkT, vp, RL, y1, c, wjobs=wj)
                if wj:
                    done_conv.add(1)
            if FS >= 3:
                kb.phase_out_proj(cur, y1, w_out_odd[0], 8, xout_dram=xs)
                cur = xs
        elif ph == "final":
            kb.phase_final_norm(cur, final_norm[0:1, :], out)
    print("instructions:", kb.sc.nins, "waits:", kb.sc.nwait, "sems left:", len(kb.sc.pool))
    return nc


PHASES = ["l0", "mlp0", "fox", "mlp1", "final"]
_NC_CACHE = {}


def kernel(**inputs):
    S = S_FULL
    x = np.ascontiguousarray(np.asarray(inputs["x"], dtype=np.float32))
    B = x.shape[0]
    if "nc" not in _NC_CACHE:
        _NC_CACHE["nc"] = build(S, PHASES)
    nc = _NC_CACHE["nc"]
    shared = dict(host_consts(S))
    f = lambda k: np.ascontiguousarray(np.asarray(inputs[k], dtype=np.float32))
    for k in ("norm_mix_even", "w_in_even", "conv_w", "conv_b", "dt_bias", "a_log", "ssd_gate_norm", "w_out_even",
              "norm_mix_odd", "w_in_odd", "fgate_bias", "w_out_odd", "norm_mlp", "w_up", "w_down"):
        shared[k] = f(k)
    shared["final_norm"] = f("final_norm").reshape(1, D)
    shared["d_skip_rep"] = np.ascontiguousarray(np.repeat(f("d_skip"), 64, axis=1))
    in_maps = []
    for b in range(B):
        m = dict(shared)
        m["x"] = x[b]
        in_maps.append(m)
    res = run_bass_kernel_spmd(nc, in_maps, core_ids=list(range(B)))
    return np.stack([np.asarray(r["out"], dtype=np.float32) for r in res.results], axis=0)
```
